# Optimizing a Trainium2 kernel written in Bass

```python
import math
import jax, jax.numpy as jnp
from jax import lax
import numpy as np

D_MODEL = 2048
BATCH = 1
SEQ = 16384
DEPTH = 2

N_MIXERS = 2
N_GDN_LAYERS = (DEPTH + 1) // 2
N_DSA_LAYERS = DEPTH // 2

GDN_QK_HEADS = 16
GDN_V_HEADS = 32
GDN_DK = 128
GDN_DV = 128
GDN_CONV = 4
GDN_CHUNK = 64
GDN_QK_W = GDN_QK_HEADS * GDN_DK
GDN_V_W = GDN_V_HEADS * GDN_DV
GDN_IN_W = 2 * GDN_QK_W + 2 * GDN_V_W + 2 * GDN_V_HEADS

DSA_HEADS = 16
DSA_KV_HEADS = 4
DSA_GROUP = DSA_HEADS // DSA_KV_HEADS
DSA_DH = 128
IDX_HEADS = 16
IDX_DIM = 128
TOPK_MAX = 256
Q_BLOCK = 128
DSA_IN_W = DSA_HEADS * DSA_DH + 2 * DSA_KV_HEADS * DSA_DH + IDX_HEADS * IDX_DIM + IDX_DIM + IDX_HEADS

N_BUCKETS = 32
MAX_DISTANCE = 128

D_FF = 5120
FFN_CONV = 3

PLE_DIM = 256

DN_ALPHA = (2.0 * DEPTH) ** 0.25
DN_BETA = (8.0 * DEPTH) ** -0.25
LN_EPS = 1e-5
RMS_EPS = 1e-6

kernel_name = "hybrid_gdn_dsa_convglu_deepnorm"


def layer_norm(x, g, b):
    xf = x.astype(jnp.float32)
    mu = jnp.mean(xf, -1, keepdims=True)
    var = jnp.mean(jnp.square(xf - mu), -1, keepdims=True)
    y = (xf - mu) * lax.rsqrt(var + LN_EPS) * g.astype(jnp.float32) + b.astype(jnp.float32)
    return y.astype(x.dtype)


def causal_dwconv(x, w):
    K = w.shape[0]
    L = x.shape[1]
    xp = jnp.pad(x, ((0, 0), (K - 1, 0), (0, 0)))
    return sum(xp[:, j:j + L] * w[j] for j in range(K))


def l2norm(x):
    return x * lax.rsqrt(jnp.sum(jnp.square(x), -1, keepdims=True) + RMS_EPS)


def gated_delta_rule(q, k, v, g, beta):
    B, L, H, dk = q.shape
    dv = v.shape[-1]
    C = GDN_CHUNK
    N = L // C

    def chunk(a):
        a = a.reshape(B, N, C, H, *a.shape[3:])
        return jnp.moveaxis(a, 3, 1)

    q, k, v, g, beta = chunk(q), chunk(k), chunk(v), chunk(g), chunk(beta)
    gc = jnp.cumsum(g, axis=-1)
    kb = k * beta[..., None]
    vb = v * beta[..., None]
    tri = jnp.tril(jnp.ones((C, C), dtype=bool))
    strict = jnp.tril(jnp.ones((C, C), dtype=bool), -1)
    diff = gc[..., :, None] - gc[..., None, :]
    decay = jnp.where(tri, jnp.exp(jnp.where(tri, diff, 0.0)), 0.0)
    lmat = jnp.where(strict, jnp.einsum('bhncd,bhnsd->bhncs', kb, k) * decay, 0.0)
    amat = lmat + jnp.eye(C, dtype=jnp.float32)
    rhs = jnp.concatenate([vb, kb * jnp.exp(gc)[..., None]], axis=-1)
    sol = lax.linalg.triangular_solve(amat, rhs, left_side=True, lower=True, unit_diagonal=True)
    u, w = sol[..., :dv], sol[..., dv:]
    qk = jnp.where(tri, jnp.einsum('bhncd,bhnsd->bhncs', q, k) * decay, 0.0)
    q_dec = q * jnp.exp(gc)[..., None]
    k_dec = k * jnp.exp(gc[..., -1:] - gc)[..., None]
    g_last = jnp.exp(gc[..., -1])

    def step(S, inp):
        u_n, w_n, qd_n, kd_n, qk_n, gl_n = inp
        v_new = u_n - jnp.einsum('bhck,bhkv->bhcv', w_n, S)
        o_n = jnp.einsum('bhck,bhkv->bhcv', qd_n, S) + jnp.einsum('bhcs,bhsv->bhcv', qk_n, v_new)
        S = S * gl_n[..., None, None] + jnp.einsum('bhck,bhcv->bhkv', kd_n, v_new)
        return S, o_n

    xs = tuple(jnp.moveaxis(a, 2, 0) for a in (u, w, q_dec, k_dec, qk, g_last))
    S0 = jnp.zeros((B, H, dk, dv), jnp.float32)
    _, o = lax.scan(step, S0, xs)
    o = jnp.transpose(o, (1, 0, 3, 2, 4))
    return o.reshape(B, L, H, dv)


def gdn_mixer(x, w_in, conv_w, a_log, dt_bias, norm_g, w_o):
    B, L, _ = x.shape
    proj = x @ w_in
    qkv, z, a, b = jnp.split(proj, [2 * GDN_QK_W + GDN_V_W, 2 * GDN_QK_W + 2 * GDN_V_W,
                                    2 * GDN_QK_W + 2 * GDN_V_W + GDN_V_HEADS], axis=-1)
    qkv = jax.nn.silu(causal_dwconv(qkv, conv_w)).astype(jnp.float32)
    q, k, v = jnp.split(qkv, [GDN_QK_W, 2 * GDN_QK_W], axis=-1)
    rep = GDN_V_HEADS // GDN_QK_HEADS
    q = l2norm(q.reshape(B, L, GDN_QK_HEADS, GDN_DK)) * (GDN_DK ** -0.5)
    k = l2norm(k.reshape(B, L, GDN_QK_HEADS, GDN_DK))
    q = jnp.repeat(q, rep, axis=2)
    k = jnp.repeat(k, rep, axis=2)
    v = v.reshape(B, L, GDN_V_HEADS, GDN_DV)
    beta = jax.nn.sigmoid(b.astype(jnp.float32))
    g = -jnp.exp(a_log.astype(jnp.float32)) * jax.nn.softplus(a.astype(jnp.float32) + dt_bias.astype(jnp.float32))
    o = gated_delta_rule(q, k, v, g, beta)
    zf = z.astype(jnp.float32).reshape(B, L, GDN_V_HEADS, GDN_DV)
    o = o * lax.rsqrt(jnp.mean(jnp.square(o), -1, keepdims=True) + RMS_EPS) * norm_g.astype(jnp.float32) * jax.nn.silu(zf)
    return o.reshape(B, L, GDN_V_W).astype(x.dtype) @ w_o


def rel_bucket(dist):
    max_exact = N_BUCKETS // 2
    d = jnp.maximum(dist, 0)
    df = jnp.maximum(d, 1).astype(jnp.float32)
    large = max_exact + (jnp.log(df / max_exact) / math.log(MAX_DISTANCE / max_exact)
                         * (N_BUCKETS - max_exact)).astype(jnp.int32)
    large = jnp.minimum(large, N_BUCKETS - 1)
    return jnp.where(d < max_exact, d, large)


def dsa_mixer(x, w_in, kidx_ln_g, kidx_ln_b, rel_bias, w_o):
    B, L, _ = x.shape
    k_top = min(TOPK_MAX, L // 4)
    proj = x @ w_in
    sq = DSA_HEADS * DSA_DH
    skv = DSA_KV_HEADS * DSA_DH
    si = IDX_HEADS * IDX_DIM
    q, k, v, qi, ki, wi = jnp.split(proj, [sq, sq + skv, sq + 2 * skv, sq + 2 * skv + si,
                                           sq + 2 * skv + si + IDX_DIM], axis=-1)
    q = q.reshape(B, L, DSA_HEADS, DSA_DH)
    k = k.reshape(B, L, DSA_KV_HEADS, DSA_DH)
    v = v.reshape(B, L, DSA_KV_HEADS, DSA_DH)
    qi = qi.reshape(B, L, IDX_HEADS, IDX_DIM)
    ki = layer_norm(ki, kidx_ln_g, kidx_ln_b).astype(jnp.float32)
    wi = wi * ((IDX_HEADS ** -0.5) * (IDX_DIM ** -0.5))
    nb = L // Q_BLOCK
    scale = DSA_DH ** -0.5
    key_pos = jnp.arange(L)
    bias_tab = rel_bias.astype(jnp.float32)

    def blocks(a):
        return jnp.moveaxis(a.reshape(B, nb, Q_BLOCK, *a.shape[2:]), 1, 0)

    def attend(args):
        q_b, qi_b, wi_b, s0 = args
        t = s0 + jnp.arange(Q_BLOCK)
        sc = jnp.einsum('bqhd,bsd->bqhs', qi_b.astype(jnp.float32), ki)
        score = jnp.einsum('bqhs,bqh->bqs', jax.nn.relu(sc), wi_b.astype(jnp.float32))
        causal = key_pos[None, :] <= t[:, None]
        score = jnp.where(causal[None], score, -jnp.inf)
        _, idx = lax.top_k(score, k_top)
        k_sel = jax.vmap(lambda kk, ii: kk[ii])(k, idx)
        v_sel = jax.vmap(lambda vv, ii: vv[ii])(v, idx)
        qg = q_b.reshape(B, Q_BLOCK, DSA_KV_HEADS, DSA_GROUP, DSA_DH)
        logits = jnp.einsum('bqkgd,bqskd->bqkgs', qg, k_sel).astype(jnp.float32) * scale
        dist = t[None, :, None] - idx
        bias = bias_tab[rel_bucket(dist)]
        bias = bias.reshape(B, Q_BLOCK, k_top, DSA_KV_HEADS, DSA_GROUP).transpose(0, 1, 3, 4, 2)
        valid = (dist >= 0)[:, :, None, None, :]
        logits = jnp.where(valid, logits + bias, -jnp.inf)
        probs = jax.nn.softmax(logits, axis=-1).astype(v.dtype)
        o = jnp.einsum('bqkgs,bqskd->bqkgd', probs, v_sel)
        return o.reshape(B, Q_BLOCK, DSA_HEADS * DSA_DH)

    starts = jnp.arange(nb, dtype=jnp.int32) * Q_BLOCK
    o = lax.map(attend, (blocks(q), blocks(qi), blocks(wi), starts))
    o = jnp.moveaxis(o, 0, 1).reshape(B, L, DSA_HEADS * DSA_DH)
    return o @ w_o


def conv_glu(x, w_gate, w_up, conv_w, w_down):
    gate = causal_dwconv(x @ w_gate, conv_w)
    return (jax.nn.silu(gate) * (x @ w_up)) @ w_down


def setup_inputs(seed: int = 0) -> dict:
    key = jax.random.key(seed)
    ks = jax.random.split(key, 32)

    def nrm(k, shape, scale):
        return jax.random.normal(k, shape, jnp.float32) * scale

    NA, NB, L = N_GDN_LAYERS, N_DSA_LAYERS, DEPTH
    x = nrm(ks[0], (BATCH, SEQ, D_MODEL), 1.0)
    p = nrm(ks[1], (DEPTH, BATCH, SEQ, PLE_DIM), 1.0)
    gdn_w_in = nrm(ks[2], (NA, D_MODEL, GDN_IN_W), D_MODEL ** -0.5)
    gdn_conv_w = nrm(ks[3], (NA, GDN_CONV, 2 * GDN_QK_W + GDN_V_W), GDN_CONV ** -0.5)
    gdn_a_log = jnp.log(jax.random.uniform(ks[4], (NA, GDN_V_HEADS), jnp.float32, 1.0, 16.0))
    dt = jnp.exp(jax.random.uniform(ks[5], (NA, GDN_V_HEADS), jnp.float32, math.log(1e-3), math.log(1e-1)))
    gdn_dt_bias = dt + jnp.log(-jnp.expm1(-dt))
    gdn_norm_g = 1.0 + nrm(ks[6], (NA, GDN_DV), 0.02)
    gdn_w_o = nrm(ks[7], (NA, GDN_V_W, D_MODEL), DN_BETA * GDN_V_W ** -0.5)
    dsa_w_in = nrm(ks[8], (NB, D_MODEL, DSA_IN_W), D_MODEL ** -0.5)
    dsa_kidx_ln_g = 1.0 + nrm(ks[9], (NB, IDX_DIM), 0.02)
    dsa_kidx_ln_b = nrm(ks[10], (NB, IDX_DIM), 0.02)
    dsa_w_o = nrm(ks[11], (NB, DSA_HEADS * DSA_DH, D_MODEL), DN_BETA * (DSA_HEADS * DSA_DH) ** -0.5)
    rel_bias = nrm(ks[12], (N_BUCKETS, DSA_HEADS), 0.5)
    ln1_g = 1.0 + nrm(ks[13], (L, D_MODEL), 0.02)
    ln1_b = nrm(ks[14], (L, D_MODEL), 0.02)
    ffn_w_gate = nrm(ks[15], (L, D_MODEL, D_FF), D_MODEL ** -0.5)
    ffn_w_up = nrm(ks[16], (L, D_MODEL, D_FF), D_MODEL ** -0.5)
    ffn_conv_w = nrm(ks[17], (L, FFN_CONV, D_FF), FFN_CONV ** -0.5)
    ffn_w_down = nrm(ks[18], (L, D_FF, D_MODEL), DN_BETA * D_FF ** -0.5)
    ln2_g = 1.0 + nrm(ks[19], (L, D_MODEL), 0.02)
    ln2_b = nrm(ks[20], (L, D_MODEL), 0.02)
    ple_w_proj = nrm(ks[21], (L, PLE_DIM, D_MODEL), 0.5 * PLE_DIM ** -0.5)
    ple_w_gate = nrm(ks[22], (L, D_MODEL, D_MODEL), D_MODEL ** -0.5)
    return {"x": x, "p": p,
            "gdn_w_in": gdn_w_in, "gdn_conv_w": gdn_conv_w, "gdn_a_log": gdn_a_log,
            "gdn_dt_bias": gdn_dt_bias, "gdn_norm_g": gdn_norm_g, "gdn_w_o": gdn_w_o,
            "dsa_w_in": dsa_w_in, "dsa_kidx_ln_g": dsa_kidx_ln_g, "dsa_kidx_ln_b": dsa_kidx_ln_b,
            "dsa_w_o": dsa_w_o, "rel_bias": rel_bias,
            "ln1_g": ln1_g, "ln1_b": ln1_b,
            "ffn_w_gate": ffn_w_gate, "ffn_w_up": ffn_w_up, "ffn_conv_w": ffn_conv_w, "ffn_w_down": ffn_w_down,
            "ln2_g": ln2_g, "ln2_b": ln2_b,
            "ple_w_proj": ple_w_proj, "ple_w_gate": ple_w_gate}


def reference(x, p, gdn_w_in, gdn_conv_w, gdn_a_log, gdn_dt_bias, gdn_norm_g, gdn_w_o,
              dsa_w_in, dsa_kidx_ln_g, dsa_kidx_ln_b, dsa_w_o, rel_bias,
              ln1_g, ln1_b, ffn_w_gate, ffn_w_up, ffn_conv_w, ffn_w_down, ln2_g, ln2_b,
              ple_w_proj, ple_w_gate):
    ia = 0
    ib = 0
    for i in range(DEPTH):
        if i % N_MIXERS == 0:
            h = gdn_mixer(x, gdn_w_in[ia], gdn_conv_w[ia], gdn_a_log[ia], gdn_dt_bias[ia],
                          gdn_norm_g[ia], gdn_w_o[ia])
            ia += 1
        else:
            h = dsa_mixer(x, dsa_w_in[ib], dsa_kidx_ln_g[ib], dsa_kidx_ln_b[ib], rel_bias, dsa_w_o[ib])
            ib += 1
        x = layer_norm(DN_ALPHA * x + h, ln1_g[i], ln1_b[i])
        f = conv_glu(x, ffn_w_gate[i], ffn_w_up[i], ffn_conv_w[i], ffn_w_down[i])
        x = layer_norm(DN_ALPHA * x + f, ln2_g[i], ln2_b[i])
        x = x + jax.nn.sigmoid(x @ ple_w_gate[i]) * (p[i] @ ple_w_proj[i])
    return x
```

```python
import math
import numpy as np
import ml_dtypes
import concourse.bass as bass
import concourse.mybir as mybir
from concourse.bass_utils import run_bass_kernel_spmd

F32 = mybir.dt.float32
BF16 = mybir.dt.bfloat16
AF = mybir.ActivationFunctionType
ALU = mybir.AluOpType
AX = mybir.AxisListType
NPBF = ml_dtypes.bfloat16

D = 2048
DC = D // 128
SEQ = 16384
NCORE = 8
DFF = 5120
FC = DFF // 128
PLE = 256
ALPHA = 4.0 ** 0.25
LN_EPS = 1e-5
RMS_EPS = 1e-6
DEBUG = {}


class Buf:
    __slots__ = ("name", "w", "r", "excl")

    def __init__(self, name="", excl=False):
        self.name = name
        self.w = None
        self.r = []
        self.excl = excl


class DSem:
    def __init__(self, sem, key):
        self.sem = sem
        self.key = key
        self.total = 0


class KB:
    def __init__(self, nc):
        self.nc = nc
        self.eng = {"pe": nc.tensor, "act": nc.scalar, "dve": nc.vector,
                    "pool": nc.gpsimd, "sp": nc.sync}
        self.sem = {e: nc.alloc_semaphore("cs_" + e) for e in self.eng}
        self.cnt = {e: 0 for e in self.eng}
        self.seen = {e: {} for e in self.eng}
        self.semobj = dict(self.sem)
        self.nds = 0
        self.dry = False
        self.psum_tiles = None
        self.psum_i = 0

    def dsem(self):
        key = "ds%d" % self.nds
        s = self.nc.alloc_semaphore(key)
        self.nds += 1
        d = DSem(s, key)
        self.semobj[key] = s
        return d

    def _waits(self, e, reads, writes, skip_self=False):
        deps = {}

        def add(d):
            if d is None:
                return
            kk, v = d
            if deps.get(kk, 0) < v:
                deps[kk] = v
        for b in reads:
            add(b.w)
        for b in writes:
            add(b.w)
            for d in b.r:
                add(d)
        seen = self.seen[e]
        for kk, v in deps.items():
            if skip_self and kk == e:
                continue
            if seen.get(kk, 0) >= v:
                continue
            self.eng[e].wait_ge(self.semobj[kk], v)
            seen[kk] = v

    def op(self, e, fn, reads=(), writes=(), skip_self=False):
        if self.dry:
            return None
        ex = [b for b in reads if b.excl]
        if ex:
            writes = list(writes) + ex
        self._waits(e, reads, writes, skip_self=skip_self)
        ins = fn(self.eng[e])
        self.cnt[e] += 1
        ins.then_inc(self.sem[e], 1)
        tag = (e, self.cnt[e])
        for b in reads:
            b.r.append(tag)
            if len(b.r) > 64:
                b.r = _prune(b.r)
        for b in writes:
            b.w = tag
            b.r = []
        return ins

    def dma(self, q, out, in_, ds, reads=(), writes=()):
        if self.dry:
            return None
        self._waits(q, reads, writes)
        ins = self.eng[q].dma_start(out=out, in_=in_)
        ds.total += 16
        ins.then_inc(ds.sem, 16)
        tag = (ds.key, ds.total)
        for b in reads:
            b.r.append(tag)
        for b in writes:
            b.w = tag
            b.r = []
        return ins

    def wait_all(self, e, bufs):
        if self.dry:
            return
        self._waits(e, [], bufs)

    def init_psum(self, n=8):
        self.npsum = n
        self.psum_tiles = []
        for i in range(n):
            t = self.nc.alloc_psum_tensor("psb%d" % i, [128, 512], F32)
            self.psum_tiles.append((t, Buf("ps%d" % i, excl=True)))

    def psum(self):
        t = self.psum_tiles[self.psum_i % self.npsum]
        self.psum_i += 1
        return t


def _prune(r):
    best = {}
    for kk, v in r:
        if best.get(kk, 0) < v:
            best[kk] = v
    return list(best.items())


class WStream:
    def __init__(self, k, nslots, slot_elems, name="w"):
        self.k = k
        self.n = nslots
        self.slots = []
        for i in range(nslots):
            t = k.nc.alloc_sbuf_tensor("%s_slot%d" % (name, i), [128, slot_elems], BF16)
            self.slots.append((t, Buf("%s%d" % (name, i)), k.dsem()))
        self.reqs = []
        self.issued = 0
        self.consumed = 0
        self.depth = nslots - 1

    def reset(self):
        self.issued = 0
        self.consumed = 0

    def _issue(self, i):
        ap = self.reqs[i]
        t, b, ds = self.slots[i % self.n]
        n = ap.shape[1]
        self.k.dma("pool", t[:, 0:n], ap, ds, writes=[b])

    def next(self, ap, hold=0):
        i = self.consumed
        self.consumed += 1
        t, b, ds = self.slots[i % self.n]
        if self.k.dry:
            self.reqs.append(ap)
            return t, b
        while self.issued < len(self.reqs) and self.issued <= i + self.depth - hold:
            self._issue(self.issued)
            self.issued += 1
        return t, b


def build_token_stage(kmc, ntiles, with_dsa_proj):
    nc = bass.Bass("TRN2", target_bir_lowering=False)
    k = KB(nc)
    k.init_psum(7)
    tps = nc.alloc_psum_tensor("tps", [128, 1024], BF16)
    tps_b = Buf("tps", excl=True)
    NT = 2 + 512 * ntiles

    def din(name, shape, dt=F32):
        return nc.dram_tensor(name, shape, dt, kind="ExternalInput").ap()

    def dout(name, shape, dt=F32):
        return nc.dram_tensor(name, shape, dt, kind="ExternalOutput").ap()

    xT = din("xT", [D, NT])
    mixT = din("mixT", [kmc * 128, NT], BF16)
    pT = din("pT", [PLE, NT])
    w_o = din("w_o", [DC, 128, kmc * 128])
    w_gate = din("w_gate", [FC, 128, DC * 128])
    w_up = din("w_up", [FC, 128, DC * 128])
    w_down = din("w_down", [DC, 128, FC * 128])
    w_pg = din("w_pg", [DC, 128, DC * 128])
    w_pp = din("w_pp", [DC, 128, 2 * 128])
    vecs = din("vecs", [128, 4 * DC])
    convw = din("convw", [128, FC * 3])
    hscale = din("hscale", [128, 1])
    if with_dsa_proj:
        w_fm = din("w_fm", [36, 128, DC * 128])
        w_v = din("w_v", [2, 128, 8 * 512])
        w_kw = din("w_kw", [128, DC * 144])
        kln = din("kln", [128, 256])
        qT_o = dout("qT_o", [D, NT - 2], BF16)
        kT_o = dout("kT_o", [512, NT - 2], BF16)
        qiT_o = dout("qiT_o", [D, NT - 2], BF16)
        v_o = dout("v_o", [NT - 2, 4, 129], BF16)
        kiT_o = dout("kiT_o", [128, NT - 2], BF16)
        wi_o = dout("wi_o", [NT - 2, 16])
    outT = dout("outT", [D, NT - 2])

    sb = nc.alloc_sbuf_tensor
    xz = sb("xz", [128, DC, 512], F32)
    xz_b = [Buf("xz%d" % i) for i in range(DC)]
    xb = sb("xb", [128, DC, 512], BF16)
    xb_b = [Buf("xb%d" % i) for i in range(DC)]
    big = sb("big", [128, FC, 512], BF16)
    big_b = [Buf("big%d" % i) for i in range(FC)]
    ptile = sb("ptile", [128, 2, 512], BF16)
    pt_b = Buf("pt")
    ones = sb("ones", [128, 128], F32)
    ones_b = Buf("ones")
    vec_t = sb("vec_t", [128, 4 * DC], F32)
    cw_t = sb("cw_t", [128, FC * 3], F32)
    hs_t = sb("hs_t", [128, 1], F32)
    const_b = Buf("const")
    haloG = sb("haloG", [128, FC, 2], F32)
    halo_b = [Buf("halo%d" % i) for i in range(FC)]
    G = [sb("G%d" % i, [128, 516], F32) for i in range(2)]
    G_b = [Buf("G%d" % i) for i in range(2)]
    tmp = [sb("tmp%d" % i, [128, 512], F32) for i in range(4)]
    tmp_b = [Buf("tmp%d" % i) for i in range(4)]
    st = [sb("st%d" % i, [128, 512], F32) for i in range(4)]
    st_b = [Buf("st%d" % i) for i in range(4)]
    ws = WStream(k, 4, FC * 128)
    ld = [k.dsem() for _ in range(4)]
    st_ds = [k.dsem() for _ in range(4)]
    xz_ds = [k.dsem() for _ in range(DC)]
    stage_o = [sb("sto%d" % i, [128, 516], BF16) for i in range(4)]
    stage_b = [Buf("sto%d" % i) for i in range(4)]
    stage_i = [0]
    if with_dsa_proj:
        kln_t = sb("kln_t", [128, 256], F32)
        identb = sb("identb", [128, 128], BF16)
        ident_b = Buf("ident")
        kif = sb("kif", [128, 160], F32)
        kif_b = Buf("kif")
        kib = sb("kib", [128, 128], BF16)
        kib_b = Buf("kib")
        sm = sb("sm", [128, 16], F32)
        sm_b = Buf("sm")
        wist = [sb("wist%d" % i, [128, 16], F32) for i in range(2)]
        wist_b = [Buf("wist%d" % i) for i in range(2)]
        wist_ds = [k.dsem() for _ in range(2)]

    tmp_i = [0]

    def gettmp():
        i = tmp_i[0] % 4
        tmp_i[0] += 1
        return tmp[i], tmp_b[i]

    def consts():
        dc = k.dsem()
        k.dma("sp", vec_t[:], vecs[:, :], dc, writes=[const_b])
        k.dma("sp", cw_t[:], convw[:, :], dc, writes=[const_b])
        k.dma("sp", hs_t[:], hscale[:, :], dc, writes=[const_b])
        if with_dsa_proj:
            k.dma("sp", kln_t[:], kln[:, :], dc, writes=[const_b])
        k.op("dve", lambda e: e.memset(ones[:], 1.0), writes=[ones_b])
        if with_dsa_proj:
            k.op("pool", lambda e: e.memset(tmp[0][:, 0:128], 0.0), writes=[tmp_b[0]])
            k.op("pool", lambda e: e.affine_select(
                out=tmp[0][:, 0:128], in_=ones[:], pattern=[[-1, 128]], compare_op=ALU.is_equal,
                fill=0.0, base=0, channel_multiplier=1), reads=[ones_b], writes=[tmp_b[0]])
            k.op("dve", lambda e: e.tensor_copy(out=identb[:], in_=tmp[0][:, 0:128]),
                 reads=[tmp_b[0]], writes=[ident_b])

    def layer_norm(W, gi):
        ps1, pb1 = k.psum()
        ps2, pb2 = k.psum()
        for m in range(DC):
            sq, sqb = gettmp()
            k.op("act", lambda e: e.activation(out=sq[:, 0:W], in_=xz[:, m, 0:W], func=AF.Square),
                 reads=[xz_b[m]], writes=[sqb])
            k.op("pe", lambda e: e.matmul(ps1[:, 0:W], ones[:], xz[:, m, 0:W],
                                          start=(m == 0), stop=(m == DC - 1)),
                 reads=[ones_b, xz_b[m]], writes=[pb1], skip_self=True)
            k.op("pe", lambda e: e.matmul(ps2[:, 0:W], ones[:], sq[:, 0:W],
                                          start=(m == 0), stop=(m == DC - 1)),
                 reads=[ones_b, sqb], writes=[pb2], skip_self=True)
        mean, rstd, nmr, scr = st
        k.op("dve", lambda e: e.tensor_scalar(out=mean[:, 0:W], in0=ps1[:, 0:W], scalar1=1.0 / D,
                                              scalar2=None, op0=ALU.mult),
             reads=[pb1], writes=[st_b[0]])
        k.op("dve", lambda e: e.tensor_tensor(out=scr[:, 0:W], in0=mean[:, 0:W], in1=mean[:, 0:W],
                                              op=ALU.mult), reads=[st_b[0]], writes=[st_b[3]])
        k.op("dve", lambda e: e.scalar_tensor_tensor(out=scr[:, 0:W], in0=ps2[:, 0:W], scalar=1.0 / D,
                                                     in1=scr[:, 0:W], op0=ALU.mult, op1=ALU.subtract),
             reads=[pb2, st_b[3]], writes=[st_b[3]])
        k.op("dve", lambda e: e.tensor_scalar(out=scr[:, 0:W], in0=scr[:, 0:W], scalar1=0.0,
                                              scalar2=LN_EPS, op0=ALU.max, op1=ALU.add),
             reads=[st_b[3]], writes=[st_b[3]])
        k.op("act", lambda e: e.activation(out=scr[:, 0:W], in_=scr[:, 0:W], func=AF.Sqrt),
             reads=[st_b[3]], writes=[st_b[3]])
        k.op("dve", lambda e: e.reciprocal(out=rstd[:, 0:W], in_=scr[:, 0:W]),
             reads=[st_b[3]], writes=[st_b[1]])
        k.op("dve", lambda e: e.scalar_tensor_tensor(out=nmr[:, 0:W], in0=mean[:, 0:W], scalar=-1.0,
                                                     in1=rstd[:, 0:W], op0=ALU.mult, op1=ALU.mult),
             reads=[st_b[0], st_b[1]], writes=[st_b[2]])
        for m in range(DC):
            t, tb = gettmp()
            k.op("dve", lambda e: e.tensor_tensor(out=t[:, 0:W], in0=xz[:, m, 0:W], in1=rstd[:, 0:W],
                                                  op=ALU.mult), reads=[xz_b[m], st_b[1]], writes=[tb])
            k.op("dve", lambda e: e.tensor_tensor(out=t[:, 0:W], in0=t[:, 0:W], in1=nmr[:, 0:W],
                                                  op=ALU.add), reads=[tb, st_b[2]], writes=[tb])
            k.op("act", lambda e: e.activation(out=xz[:, m, 0:W], in_=t[:, 0:W], func=AF.Identity,
                                               bias=vec_t[:, (gi + 1) * DC + m:(gi + 1) * DC + m + 1],
                                               scale=vec_t[:, gi * DC + m:gi * DC + m + 1]),
                 reads=[tb, const_b], writes=[xz_b[m]])
            k.op("pool", lambda e: e.tensor_copy(out=xb[:, m, 0:W], in_=xz[:, m, 0:W]),
                 reads=[xz_b[m]], writes=[xb_b[m]])

    def mm_acc(ps, pb, W, wt, wb, nk, rhs_of, rhs_bufs):
        for kc in range(nk):
            k.op("pe", lambda e: e.matmul(ps[:, 0:W], wt[:, kc * 128:(kc + 1) * 128], rhs_of(kc),
                                          start=(kc == 0), stop=(kc == nk - 1)),
                 reads=[wb, rhs_bufs[kc]], writes=[pb], skip_self=True)

    def tile_pass(c0, W, halo_only):
        k.dma("sp", xz[:, :, 0:W], xT[:, c0:c0 + W].rearrange("(m p) w -> p m w", p=128), ld[0],
              writes=xz_b)
        k.dma("sp", big[:, 0:kmc, 0:W], mixT[:, c0:c0 + W].rearrange("(m p) w -> p m w", p=128), ld[1],
              writes=big_b[0:kmc])
        for m in range(DC):
            wt, wb = ws.next(w_o[m])
            ps, pb = k.psum()
            mm_acc(ps, pb, W, wt, wb, kmc, lambda kc: big[:, kc, 0:W], big_b)
            k.op("dve", lambda e: e.scalar_tensor_tensor(out=xz[:, m, 0:W], in0=xz[:, m, 0:W],
                                                         scalar=ALPHA, in1=ps[:, 0:W],
                                                         op0=ALU.mult, op1=ALU.add),
                 reads=[xz_b[m], pb], writes=[xz_b[m]])
        layer_norm(W, 0)
        for j in range(FC):
            wg, wgb = ws.next(w_gate[j])
            psg, pbg = k.psum()
            mm_acc(psg, pbg, W, wg, wgb, DC, lambda kc: xb[:, kc, 0:W], xb_b)
            if halo_only:
                k.op("dve", lambda e: e.tensor_scalar(out=haloG[:, j, :], in0=psg[:, 0:2],
                                                      scalar1=hs_t[:, 0:1], scalar2=None, op0=ALU.mult),
                     reads=[pbg, const_b], writes=[halo_b[j]])
                continue
            wu, wub = ws.next(w_up[j])
            psu, pbu = k.psum()
            mm_acc(psu, pbu, W, wu, wub, DC, lambda kc: xb[:, kc, 0:W], xb_b)
            g, gb = G[j % 2], G_b[j % 2]
            k.op("pool", lambda e: e.tensor_copy(out=g[:, 0:2], in_=haloG[:, j, :]),
                 reads=[halo_b[j]], writes=[gb])
            k.op("act", lambda e: e.copy(out=g[:, 2:2 + W], in_=psg[:, 0:W]), reads=[pbg], writes=[gb])
            k.op("pool", lambda e: e.tensor_copy(out=haloG[:, j, :], in_=g[:, W:W + 2]),
                 reads=[gb], writes=[halo_b[j]])
            c, cb = gettmp()
            k.op("dve", lambda e: e.tensor_scalar(out=c[:, 0:W], in0=g[:, 0:W],
                                                  scalar1=cw_t[:, 3 * j:3 * j + 1], scalar2=None,
                                                  op0=ALU.mult), reads=[gb, const_b], writes=[cb])
            k.op("dve", lambda e: e.scalar_tensor_tensor(out=c[:, 0:W], in0=g[:, 1:1 + W],
                                                         scalar=cw_t[:, 3 * j + 1:3 * j + 2],
                                                         in1=c[:, 0:W], op0=ALU.mult, op1=ALU.add),
                 reads=[gb, cb, const_b], writes=[cb])
            k.op("dve", lambda e: e.scalar_tensor_tensor(out=c[:, 0:W], in0=g[:, 2:2 + W],
                                                         scalar=cw_t[:, 3 * j + 2:3 * j + 3],
                                                         in1=c[:, 0:W], op0=ALU.mult, op1=ALU.add),
                 reads=[gb, cb, const_b], writes=[cb])
            k.op("act", lambda e: e.activation(out=c[:, 0:W], in_=c[:, 0:W], func=AF.Silu),
                 reads=[cb], writes=[cb])
            k.op("dve", lambda e: e.tensor_tensor(out=big[:, j, 0:W], in0=c[:, 0:W], in1=psu[:, 0:W],
                                                  op=ALU.mult), reads=[cb, pbu], writes=[big_b[j]])
        if halo_only:
            return
        for m in range(DC):
            wt, wb = ws.next(w_down[m])
            ps, pb = k.psum()
            mm_acc(ps, pb, W, wt, wb, FC, lambda kc: big[:, kc, 0:W], big_b)
            k.op("dve", lambda e: e.scalar_tensor_tensor(out=xz[:, m, 0:W], in0=xz[:, m, 0:W],
                                                         scalar=ALPHA, in1=ps[:, 0:W],
                                                         op0=ALU.mult, op1=ALU.add),
                 reads=[xz_b[m], pb], writes=[xz_b[m]])
        layer_norm(W, 2)
        k.dma("pool", ptile[:, :, 0:W], pT[:, c0:c0 + W].rearrange("(m p) w -> p m w", p=128), ld[2],
              writes=[pt_b])
        for m in range(DC):
            wt, wb = ws.next(w_pg[m])
            psa, pba = k.psum()
            mm_acc(psa, pba, W, wt, wb, DC, lambda kc: xb[:, kc, 0:W], xb_b)
            wt2, wb2 = ws.next(w_pp[m], hold=1)
            psb, pbb = k.psum()
            mm_acc(psb, pbb, W, wt2, wb2, 2, lambda kc: ptile[:, kc, 0:W], [pt_b, pt_b])
            t, tb = gettmp()
            k.op("act", lambda e: e.activation(out=t[:, 0:W], in_=psa[:, 0:W], func=AF.Sigmoid),
                 reads=[pba], writes=[tb])
            k.op("dve", lambda e: e.tensor_tensor(out=t[:, 0:W], in0=t[:, 0:W], in1=psb[:, 0:W],
                                                  op=ALU.mult), reads=[tb, pbb], writes=[tb])
            k.op("dve", lambda e: e.tensor_tensor(out=xz[:, m, 0:W], in0=xz[:, m, 0:W], in1=t[:, 0:W],
                                                  op=ALU.add), reads=[xz_b[m], tb], writes=[xz_b[m]])
            k.dma("sp", outT[m * 128:(m + 1) * 128, c0 - 2:c0 - 2 + W], xz[:, m, 0:W], xz_ds[m],
                  reads=[xz_b[m]])
            if with_dsa_proj:
                k.op("pool", lambda e: e.tensor_copy(out=big[:, m, 0:W], in_=xz[:, m, 0:W]),
                     reads=[xz_b[m]], writes=[big_b[m]])
        if with_dsa_proj:
            dsa_proj(c0 - 2, W)

    def getstage():
        i = stage_i[0] % 4
        stage_i[0] += 1
        return stage_o[i], stage_b[i], st_ds[i]

    def dsa_proj(t0, W):
        outs = [(qT_o, 16), (kT_o, 4), (qiT_o, 16)]
        ci = 0
        for dst, nch in (outs if DEBUG.get("fm", True) else []):
            for c in range(nch):
                wt, wb = ws.next(w_fm[ci])
                ci += 1
                ps, pb = k.psum()
                mm_acc(ps, pb, W, wt, wb, DC, lambda kc: big[:, kc, 0:W], big_b)
                so, sob, sds = getstage()
                k.op("act", lambda e: e.copy(out=so[:, 0:W], in_=ps[:, 0:W]), reads=[pb], writes=[sob])
                k.dma("sp", dst[c * 128:(c + 1) * 128, t0:t0 + W], so[:, 0:W], sds, reads=[sob])
        wv0, wvb0 = ws.next(w_v[0])
        wv1, wvb1 = ws.next(w_v[1], hold=1)
        wkw, wkwb = ws.next(w_kw, hold=2)
        for s in range(W // 128 if DEBUG.get('tm', True) else 0):
            sl = slice(s * 128, (s + 1) * 128)
            ps, pb = k.psum()
            for kc in range(DC):
                wt, wb = (wv0, wvb0) if kc < 8 else (wv1, wvb1)
                k.op("pe", lambda e: e.matmul(ps[:, 0:512], big[:, kc, sl],
                                              wt[:, (kc % 8) * 512:(kc % 8 + 1) * 512],
                                              start=(kc == 0), stop=(kc == DC - 1)),
                     reads=[wb, big_b[kc]], writes=[pb], skip_self=True)
            so, sob, sds = getstage()
            k.op("pool", lambda e: e.memset(so[:, 0:516], 1.0), writes=[sob])
            sov = so[:, 0:516].rearrange("p (g c) -> p g c", c=129)
            k.op("act", lambda e: e.copy(out=sov[:, :, 0:128],
                                         in_=ps[:, 0:512].rearrange("p (g c) -> p g c", c=128)),
                 reads=[pb], writes=[sob])
            k.dma("sp", v_o[t0 + s * 128:t0 + (s + 1) * 128, :, :], sov, sds, reads=[sob])
            if not DEBUG.get("kw", True):
                continue
            ps2, pb2 = k.psum()
            for kc in range(DC):
                k.op("pe", lambda e: e.matmul(ps2[:, 0:144], big[:, kc, sl], wkw[:, kc * 144:(kc + 1) * 144],
                                              start=(kc == 0), stop=(kc == DC - 1)),
                     reads=[wkwb, big_b[kc]], writes=[pb2], skip_self=True)
            wi_t, wi_b, wi_ds = wist[s % 2], wist_b[s % 2], wist_ds[s % 2]
            k.op("dve", lambda e: e.tensor_scalar(out=wi_t[:], in0=ps2[:, 128:144],
                                                  scalar1=(16 ** -0.5) * (128 ** -0.5), scalar2=None,
                                                  op0=ALU.mult), reads=[pb2], writes=[wi_b])
            k.dma("sp", wi_o[t0 + s * 128:t0 + (s + 1) * 128, :], wi_t[:], wi_ds, reads=[wi_b])
            if not DEBUG.get("kiln", True):
                continue
            _n = [0]
            _lim = DEBUG.get("kiln_n", 99)
            def kop(*a_, **kw_):
                _n[0] += 1
                if _n[0] <= _lim:
                    k.op(*a_, **kw_)
            kop("dve", lambda e: e.memset(sm[:], 0.0), writes=[sm_b])
            kop("act", lambda e: e.copy(out=kif[:, 0:128], in_=ps2[:, 0:128]), reads=[pb2], writes=[kif_b])
            kop("dve", lambda e: e.reduce_sum(out=sm[:, 0:1], in_=kif[:, 0:128], axis=AX.X),
                 reads=[kif_b, sm_b], writes=[sm_b])
            t, tb = gettmp()
            kop("act", lambda e: e.activation(out=t[:, 0:128], in_=ps2[:, 0:128], func=AF.Square,
                                               accum_out=sm[:, 1:2]), reads=[pb2, sm_b], writes=[tb, sm_b])
            kop("dve", lambda e: e.tensor_scalar(out=sm[:, 2:4], in0=sm[:, 0:2], scalar1=1.0 / 128,
                                                  scalar2=None, op0=ALU.mult), reads=[sm_b], writes=[sm_b])
            kop("dve", lambda e: e.tensor_tensor(out=sm[:, 4:5], in0=sm[:, 2:3], in1=sm[:, 2:3],
                                                  op=ALU.mult), reads=[sm_b], writes=[sm_b])
            kop("dve", lambda e: e.tensor_tensor(out=sm[:, 5:6], in0=sm[:, 3:4], in1=sm[:, 4:5],
                                                  op=ALU.subtract), reads=[sm_b], writes=[sm_b])
            kop("dve", lambda e: e.tensor_scalar(out=sm[:, 5:6], in0=sm[:, 5:6], scalar1=0.0,
                                                  scalar2=LN_EPS, op0=ALU.max, op1=ALU.add),
                 reads=[sm_b], writes=[sm_b])
            kop("act", lambda e: e.activation(out=sm[:, 6:7], in_=sm[:, 5:6], func=AF.Sqrt),
                 reads=[sm_b], writes=[sm_b])
            kop("dve", lambda e: e.reciprocal(out=sm[:, 7:8], in_=sm[:, 6:7]), reads=[sm_b], writes=[sm_b])
            kop("dve", lambda e: e.tensor_scalar(out=kif[:, 0:128], in0=kif[:, 0:128],
                                                  scalar1=sm[:, 2:3], scalar2=sm[:, 7:8],
                                                  op0=ALU.subtract, op1=ALU.mult),
                 reads=[kif_b, sm_b], writes=[kif_b])
            kop("dve", lambda e: e.tensor_tensor(out=kif[:, 0:128], in0=kif[:, 0:128], in1=kln_t[:, 0:128],
                                                  op=ALU.mult), reads=[kif_b, const_b], writes=[kif_b])
            kop("dve", lambda e: e.tensor_tensor(out=kib[:], in0=kif[:, 0:128], in1=kln_t[:, 128:256],
                                                  op=ALU.add), reads=[kif_b, const_b], writes=[kib_b])
            if not DEBUG.get("kitr", True):
                continue
            ps3b, pb3 = tps, tps_b
            k.op("pe", lambda e: e.transpose(ps3b[:, 0:128], kib[:], identb[:]),
                 reads=[kib_b, ident_b], writes=[pb3])
            so, sob, sds = getstage()
            k.op("act", lambda e: e.copy(out=so[:, 0:128], in_=ps3b[:, 0:128]), reads=[pb3], writes=[sob])
            k.dma("sp", kiT_o[:, t0 + s * 128:t0 + (s + 1) * 128], so[:, 0:128], sds, reads=[sob])

    def program():
        consts()
        if DEBUG.get("halo", True):
            tile_pass(0, 2, True)
        for i in range(ntiles if DEBUG.get("main", True) else 0):
            tile_pass(2 + 512 * i, 512, False)

    k.dry = True
    program()
    k.dry = False
    ws.reset()
    tmp_i[0] = 0
    stage_i[0] = 0
    k.psum_i = 0
    program()
    for d in st_ds + xz_ds + (wist_ds if with_dsa_proj else []):
        if d.total:
            nc.sync.wait_ge(d.sem, d.total)
    return nc


def _lhsT_layout(w):
    K, M = w.shape
    return np.ascontiguousarray(w.reshape(K // 128, 128, M // 128, 128).transpose(2, 1, 0, 3)
                                .reshape(M // 128, 128, (K // 128) * 128))


def _rhs_layout(w):
    K, N = w.shape
    return np.ascontiguousarray(w.reshape(K // 128, 128, N).transpose(1, 0, 2).reshape(128, (K // 128) * N))


def _pvec(v):
    return np.ascontiguousarray(v.reshape(-1, 128).T)


class Rot:
    def __init__(self, k, name, shape, dt, n):
        self.t = [(k.nc.alloc_sbuf_tensor("%s%d" % (name, i), shape, dt), Buf("%s%d" % (name, i)))
                  for i in range(n)]
        self.i = 0

    def get(self):
        r = self.t[self.i % len(self.t)]
        self.i += 1
        return r


def build_gdn_stage(L):
    nc = bass.Bass("TRN2", target_bir_lowering=False)
    k = KB(nc)
    k.init_psum(7)
    tps = nc.alloc_psum_tensor("tps", [128, 1024], BF16)
    tps_b = Buf("tps", excl=True)
    NTILE = L // 512

    def din(name, shape, dt=F32):
        return nc.dram_tensor(name, shape, dt, kind="ExternalInput").ap()

    xT = din("xT", [D, L])
    w_fm = din("w_fm", [12, 128, DC * 128])
    w_ab = din("w_ab", [128, DC * 8])
    convw = din("convw", [128, 8 * 4])
    hvec = din("hvec", [128, 8])
    normg = din("normg", [128, 1])
    cmask = din("cmask", [128, 3 * 128])
    ogT = nc.dram_tensor("ogT", [512, L], BF16, kind="ExternalOutput").ap()

    sb = nc.alloc_sbuf_tensor
    wfm = sb("wfm", [128, 12, DC * 128], BF16)
    wfm_b = Buf("wfm")
    wab = sb("wab", [128, DC * 8], BF16)
    cw = sb("cw", [128, 32], F32)
    hv = sb("hv", [128, 8], F32)
    negA = sb("negA", [128, 4], F32)
    ng = sb("ng", [128, 1], F32)
    cm = sb("cm", [128, 384], F32)
    ones = sb("ones", [128, 128], F32)
    ident = sb("ident", [128, 128], F32)
    identb = sb("identb", [128, 128], BF16)
    const_b = Buf("const")
    U = cm[:, 0:128]
    PMtri = cm[:, 128:256]
    strict01 = cm[:, 256:384]

    xb = [sb("xb%d" % i, [128, DC, 512], BF16) for i in range(2)]
    xb_b = [Buf("xb%d" % i) for i in range(2)]
    xb_ds = [k.dsem() for _ in range(2)]
    halo = sb("halo", [128, 8, 3], F32)
    halo_b = [Buf("halo%d" % i) for i in range(8)]
    Gc = Rot(k, "Gc", [128, 516], F32, 2)
    t512 = Rot(k, "t512", [128, 512], F32, 4)
    qT = sb("qT", [128, 2, 512], BF16)
    qT_b = [Buf("qT%d" % i) for i in range(2)]
    kT = sb("kT", [128, 2, 512], BF16)
    kT_b = [Buf("kT%d" % i) for i in range(2)]
    vT = sb("vT", [128, 4, 512], BF16)
    vT_b = [Buf("vT%d" % i) for i in range(4)]
    zs = sb("zs", [128, 4, 512], F32)
    zs_b = [Buf("zs%d" % i) for i in range(4)]
    tok = sb("tok", [128, 4, 32], F32)
    tok_b = [Buf("tok%d" % i) for i in range(4)]
    f128 = Rot(k, "f128", [128, 128], F32, 16)
    b128 = Rot(k, "b128", [128, 128], BF16, 8)
    P_ktok = Rot(k, "pktok", [128, 128], BF16, 4)
    P_KK = Rot(k, "pkk", [128, 128], F32, 4)
    P_QK = Rot(k, "pqk", [128, 128], F32, 4)
    P_egc = Rot(k, "pegc", [128, 128], F32, 8)
    P_u = Rot(k, "pu", [128, 128], F32, 8)
    P_qkT = Rot(k, "pqkT", [128, 128], BF16, 8)
    P_wT = Rot(k, "pwT", [128, 128], BF16, 8)
    P_qd = Rot(k, "pqd", [128, 128], BF16, 8)
    P_kdec = Rot(k, "pkdec", [128, 128], BF16, 8)
    P_vn = Rot(k, "pvn", [128, 128], BF16, 8)
    S = sb("S", [128, 4, 128], F32)
    Sb = sb("Sb", [128, 4, 128], BF16)
    S_b = [Buf("S%d" % i) for i in range(4)]
    Sb_b = [Buf("Sb%d" % i) for i in range(4)]
    ostage = [sb("ost%d" % i, [128, 4, 512], BF16) for i in range(2)]
    ost_b = [Buf("ost%d" % i) for i in range(2)]
    ost_ds = [k.dsem() for _ in range(2)]

    def consts():
        dc = k.dsem()
        for i in range(12):
            k.dma("pool", wfm[:, i, :], w_fm[i], dc, writes=[wfm_b])
        k.dma("pool", wab[:], w_ab[:, :], dc, writes=[wfm_b])
        dc2 = k.dsem()
        k.dma("sp", cw[:], convw[:, :], dc2, writes=[const_b])
        k.dma("sp", hv[:], hvec[:, :], dc2, writes=[const_b])
        k.dma("sp", ng[:], normg[:, :], dc2, writes=[const_b])
        k.dma("sp", cm[:], cmask[:, :], dc2, writes=[const_b])
        k.op("dve", lambda e: e.memset(ones[:], 1.0), writes=[const_b])
        k.op("pool", lambda e: e.memset(ident[:], 0.0), writes=[const_b])
        k.op("pool", lambda e: e.affine_select(
            out=ident[:], in_=ones[:], pattern=[[-1, 128]], compare_op=ALU.is_equal,
            fill=0.0, base=0, channel_multiplier=1), reads=[const_b], writes=[const_b])
        k.op("dve", lambda e: e.tensor_copy(out=identb[:], in_=ident[:]), reads=[const_b], writes=[const_b])
        k.op("act", lambda e: e.activation(out=negA[:], in_=hv[:, 0:4], func=AF.Exp),
             reads=[const_b], writes=[const_b])
        k.op("dve", lambda e: e.tensor_scalar(out=negA[:], in0=negA[:], scalar1=-1.0, scalar2=None,
                                              op0=ALU.mult), reads=[const_b], writes=[const_b])
        k.op("dve", lambda e: e.memset(halo[:], 0.0), writes=halo_b)
        k.op("dve", lambda e: e.memset(S[:], 0.0), writes=S_b)
        k.op("dve", lambda e: e.memset(Sb[:], 0.0), writes=Sb_b)

    def load_x(i):
        k.dma("pool", xb[i % 2][:], xT[:, i * 512:(i + 1) * 512].rearrange("(m p) w -> p m w", p=128),
              xb_ds[i % 2], writes=[xb_b[i % 2]])

    def proj_tile(i):
        x, xbuf = xb[i % 2], xb_b[i % 2]
        for c in range(12):
            ps, pb = k.psum()
            for kc in range(DC):
                k.op("pe", lambda e: e.matmul(ps[:, 0:512], wfm[:, c, kc * 128:(kc + 1) * 128], x[:, kc, :],
                                              start=(kc == 0), stop=(kc == DC - 1)),
                     reads=[wfm_b, xbuf], writes=[pb], skip_self=True)
            if c >= 8:
                k.op("act", lambda e: e.activation(out=zs[:, c - 8, :], in_=ps[:, 0:512], func=AF.Silu),
                     reads=[pb], writes=[zs_b[c - 8]])
                continue
            g, gb = Gc.get()
            k.op("pool", lambda e: e.tensor_copy(out=g[:, 0:3], in_=halo[:, c, :]), reads=[halo_b[c]], writes=[gb])
            k.op("act", lambda e: e.copy(out=g[:, 3:515], in_=ps[:, 0:512]), reads=[pb], writes=[gb])
            k.op("pool", lambda e: e.tensor_copy(out=halo[:, c, :], in_=g[:, 512:515]), reads=[gb], writes=[halo_b[c]])
            t, tb = t512.get()
            k.op("dve", lambda e: e.tensor_scalar(out=t[:], in0=g[:, 0:512], scalar1=cw[:, 4 * c:4 * c + 1],
                                                  scalar2=None, op0=ALU.mult), reads=[gb, const_b], writes=[tb])
            for j in range(1, 4):
                k.op("dve", lambda e: e.scalar_tensor_tensor(out=t[:], in0=g[:, j:j + 512],
                                                             scalar=cw[:, 4 * c + j:4 * c + j + 1], in1=t[:],
                                                             op0=ALU.mult, op1=ALU.add),
                     reads=[gb, tb, const_b], writes=[tb])
            if c >= 4:
                k.op("act", lambda e: e.activation(out=vT[:, c - 4, :], in_=t[:], func=AF.Silu),
                     reads=[tb], writes=[vT_b[c - 4]])
                continue
            k.op("act", lambda e: e.activation(out=t[:], in_=t[:], func=AF.Silu), reads=[tb], writes=[tb])
            sq, sqb = t512.get()
            k.op("act", lambda e: e.activation(out=sq[:], in_=t[:], func=AF.Square), reads=[tb], writes=[sqb])
            ps2, pb2 = k.psum()
            k.op("pe", lambda e: e.matmul(ps2[:, 0:512], ones[:], sq[:], start=True, stop=True),
                 reads=[const_b, sqb], writes=[pb2])
            k.op("dve", lambda e: e.tensor_scalar(out=sq[:], in0=ps2[:, 0:512], scalar1=RMS_EPS, scalar2=None,
                                                  op0=ALU.add), reads=[pb2], writes=[sqb])
            k.op("act", lambda e: e.activation(out=sq[:], in_=sq[:], func=AF.Sqrt), reads=[sqb], writes=[sqb])
            k.op("dve", lambda e: e.reciprocal(out=sq[:], in_=sq[:]), reads=[sqb], writes=[sqb])
            if c < 2:
                k.op("dve", lambda e: e.scalar_tensor_tensor(out=qT[:, c, :], in0=t[:], scalar=128 ** -0.5,
                                                             in1=sq[:], op0=ALU.mult, op1=ALU.mult),
                     reads=[tb, sqb], writes=[qT_b[c]])
            else:
                k.op("dve", lambda e: e.tensor_tensor(out=kT[:, c - 2, :], in0=t[:], in1=sq[:], op=ALU.mult),
                     reads=[tb, sqb], writes=[kT_b[c - 2]])
        for s in range(4):
            ps, pb = k.psum()
            for kc in range(DC):
                k.op("pe", lambda e: e.matmul(ps[:, 0:8], x[:, kc, s * 128:(s + 1) * 128], wab[:, kc * 8:(kc + 1) * 8],
                                              start=(kc == 0), stop=(kc == DC - 1)),
                     reads=[wfm_b, xbuf], writes=[pb], skip_self=True)
            tk, tkb = tok[:, s, :], tok_b[s]
            k.op("act", lambda e: e.activation(out=tk[:, 4:8], in_=ps[:, 4:8], func=AF.Sigmoid),
                 reads=[pb], writes=[tkb])
            k.op("dve", lambda e: e.tensor_tensor(out=tk[:, 0:4], in0=ps[:, 0:4], in1=hv[:, 4:8], op=ALU.add),
                 reads=[pb, const_b, tkb], writes=[tkb])
            k.op("act", lambda e: e.activation(out=tk[:, 0:4], in_=tk[:, 0:4], func=AF.Exp), reads=[tkb], writes=[tkb])
            k.op("act", lambda e: e.activation(out=tk[:, 0:4], in_=tk[:, 0:4], func=AF.Ln, bias=1.0),
                 reads=[tkb], writes=[tkb])
            k.op("dve", lambda e: e.tensor_tensor(out=tk[:, 0:4], in0=tk[:, 0:4], in1=negA[:], op=ALU.mult),
                 reads=[tkb, const_b], writes=[tkb])
            k.op("dve", lambda e: e.tensor_scalar(out=tk[:, 8:12], in0=tk[:, 4:8], scalar1=-1.0, scalar2=None,
                                                  op0=ALU.mult), reads=[tkb], writes=[tkb])

    def chunk(i, s, ost, ostb):
        cs = slice(s * 128, (s + 1) * 128)
        tk, tkb = tok[:, s, :], tok_b[s]
        ps, pb = k.psum()
        k.op("pe", lambda e: e.matmul(ps[:, 0:4], U, tk[:, 0:4], start=True, stop=True),
             reads=[const_b, tkb], writes=[pb])
        k.op("dve", lambda e: e.tensor_copy(out=tk[:, 12:16], in_=ps[:, 0:4]), reads=[pb, tkb], writes=[tkb])
        ktok = []
        for qh in range(2):
            k.op("pe", lambda e: e.transpose(tps[:, qh * 128:(qh + 1) * 128], kT[:, qh, cs], identb[:]),
                 reads=[kT_b[qh], const_b], writes=[tps_b])
            kt, ktb = P_ktok.get()
            k.op("act", lambda e: e.copy(out=kt[:], in_=tps[:, qh * 128:(qh + 1) * 128]), reads=[tps_b], writes=[ktb])
            ktok.append((kt, ktb))
        KK = []
        QK = []
        for qh in range(2):
            ps1, pb1 = k.psum()
            k.op("pe", lambda e: e.matmul(ps1[:, 0:128], kT[:, qh, cs], kT[:, qh, cs], start=True, stop=True),
                 reads=[kT_b[qh]], writes=[pb1])
            ps2, pb2 = k.psum()
            k.op("pe", lambda e: e.matmul(ps2[:, 0:128], qT[:, qh, cs], kT[:, qh, cs], start=True, stop=True),
                 reads=[qT_b[qh], kT_b[qh]], writes=[pb2])
            kk, kkb = P_KK.get()
            k.op("act", lambda e: e.copy(out=kk[:], in_=ps1[:, 0:128]), reads=[pb1], writes=[kkb])
            qk, qkb = P_QK.get()
            k.op("act", lambda e: e.copy(out=qk[:], in_=ps2[:, 0:128]), reads=[pb2], writes=[qkb])
            KK.append((kk, kkb))
            QK.append((qk, qkb))
        pre = []
        for j in range(4):
            qh = j // 2
            gu, gub = f128.get()
            k.op("dve", lambda e: e.tensor_scalar(out=gu[:], in0=U, scalar1=tk[:, j:j + 1], scalar2=None,
                                                  op0=ALU.mult), reads=[const_b, tkb], writes=[gub])
            psr, pbr = k.psum()
            k.op("pe", lambda e: e.matmul(psr[:, 0:128], ones[:], gu[:], start=True, stop=True),
                 reads=[const_b, gub], writes=[pbr])
            gcr, gcrb = f128.get()
            k.op("act", lambda e: e.copy(out=gcr[:], in_=psr[:, 0:128]), reads=[pbr], writes=[gcrb])
            egc, egcb = P_egc.get()
            k.op("act", lambda e: e.activation(out=egc[:], in_=gcr[:], func=AF.Exp), reads=[gcrb], writes=[egcb])
            k.op("act", lambda e: e.activation(out=tk[:, 16 + j:17 + j], in_=tk[:, 12 + j:13 + j], func=AF.Exp),
                 reads=[tkb], writes=[tkb])
            k.op("dve", lambda e: e.tensor_tensor(out=tk[:, 16 + j:17 + j], in0=tk[:, 16 + j:17 + j],
                                                  in1=tk[:, 4 + j:5 + j], op=ALU.mult), reads=[tkb], writes=[tkb])
            k.op("act", lambda e: e.activation(out=tk[:, 20 + j:21 + j], in_=tk[:, 12 + j:13 + j], func=AF.Exp,
                                               bias=gcr[:, 127:128], scale=-1.0), reads=[tkb, gcrb], writes=[tkb])
            dec, decb = f128.get()
            k.op("dve", lambda e: e.scalar_tensor_tensor(out=dec[:], in0=gcr[:], scalar=tk[:, 12 + j:13 + j],
                                                         in1=PMtri, op0=ALU.subtract, op1=ALU.add),
                 reads=[gcrb, tkb, const_b], writes=[decb])
            k.op("act", lambda e: e.activation(out=dec[:], in_=dec[:], func=AF.Exp, scale=-1.0),
                 reads=[decb], writes=[decb])
            qkm, qkmb = b128.get()
            k.op("dve", lambda e: e.tensor_tensor(out=qkm[:], in0=QK[qh][0][:], in1=dec[:], op=ALU.mult),
                 reads=[QK[qh][1], decb], writes=[qkmb])
            k.op("pe", lambda e: e.transpose(tps[:, 256:384], qkm[:], identb[:]),
                 reads=[qkmb, const_b], writes=[tps_b])
            qkT, qkTb = P_qkT.get()
            k.op("act", lambda e: e.copy(out=qkT[:], in_=tps[:, 256:384]), reads=[tps_b], writes=[qkTb])
            M, Mb = f128.get()
            k.op("dve", lambda e: e.scalar_tensor_tensor(out=M[:], in0=KK[qh][0][:], scalar=tk[:, 8 + j:9 + j],
                                                         in1=dec[:], op0=ALU.mult, op1=ALU.mult),
                 reads=[KK[qh][1], tkb, decb], writes=[Mb])
            k.op("pool", lambda e: e.tensor_tensor(out=M[:], in0=M[:], in1=strict01, op=ALU.mult),
                 reads=[Mb, const_b], writes=[Mb])
            psx, pbx = k.psum()
            k.op("pe", lambda e: e.transpose(psx[:, 0:128], M[:], ident[:]), reads=[Mb, const_b], writes=[pbx])
            X, Xb = f128.get()
            k.op("act", lambda e: e.copy(out=X[:], in_=psx[:, 0:128]), reads=[pbx], writes=[Xb])
            TT, TTb = f128.get()
            k.op("pool", lambda e: e.tensor_tensor(out=TT[:], in0=X[:], in1=ident[:], op=ALU.add),
                 reads=[Xb, const_b], writes=[TTb])
            for it in range(1, 7):
                psm, pbm = k.psum()
                k.op("pe", lambda e: e.matmul(psm[:, 0:128], X[:], M[:], start=True, stop=True),
                     reads=[Xb, Mb], writes=[pbm])
                if it < 6:
                    psx2, pbx2 = k.psum()
                    k.op("pe", lambda e: e.matmul(psx2[:, 0:128], M[:], X[:], start=True, stop=True),
                         reads=[Xb, Mb], writes=[pbx2])
                M2, M2b = f128.get()
                k.op("act", lambda e: e.copy(out=M2[:], in_=psm[:, 0:128]), reads=[pbm], writes=[M2b])
                if it < 6:
                    X2, X2b = f128.get()
                    k.op("dve", lambda e: e.tensor_copy(out=X2[:], in_=psx2[:, 0:128]), reads=[pbx2], writes=[X2b])
                pst, pbt = k.psum()
                k.op("pe", lambda e: e.matmul(pst[:, 0:128], M2[:], TT[:], start=True, stop=True),
                     reads=[M2b, TTb], writes=[pbt])
                TT2, TT2b = f128.get()
                k.op("dve", lambda e: e.tensor_tensor(out=TT2[:], in0=TT[:], in1=pst[:, 0:128], op=ALU.add),
                     reads=[TTb, pbt], writes=[TT2b])
                M, Mb = M2, M2b
                if it < 6:
                    X, Xb = X2, X2b
                TT, TTb = TT2, TT2b
            Tb, Tbb = b128.get()
            k.op("pool", lambda e: e.tensor_copy(out=Tb[:], in_=TT[:]), reads=[TTb], writes=[Tbb])
            kbg, kbgb = b128.get()
            k.op("pool", lambda e: e.tensor_scalar(out=kbg[:], in0=ktok[qh][0][:], scalar1=tk[:, 16 + j:17 + j],
                                                   scalar2=None, op0=ALU.mult), reads=[ktok[qh][1], tkb], writes=[kbgb])
            kdec, kdecb = P_kdec.get()
            k.op("pool", lambda e: e.tensor_scalar(out=kdec[:], in0=ktok[qh][0][:], scalar1=tk[:, 20 + j:21 + j],
                                                   scalar2=None, op0=ALU.mult), reads=[ktok[qh][1], tkb], writes=[kdecb])
            k.op("pe", lambda e: e.transpose(tps[:, 384:512], vT[:, j, cs], identb[:]),
                 reads=[vT_b[j], const_b], writes=[tps_b])
            vb, vbb = b128.get()
            k.op("act", lambda e: e.activation(out=vb[:], in_=tps[:, 384:512], func=AF.Identity,
                                               scale=tk[:, 4 + j:5 + j]), reads=[tps_b, tkb], writes=[vbb])
            psu, pbu = k.psum()
            k.op("pe", lambda e: e.matmul(psu[:, 0:128], Tb[:], vb[:], start=True, stop=True),
                 reads=[Tbb, vbb], writes=[pbu])
            u, ub = P_u.get()
            k.op("act", lambda e: e.copy(out=u[:], in_=psu[:, 0:128]), reads=[pbu], writes=[ub])
            psw, pbw = k.psum()
            k.op("pe", lambda e: e.matmul(psw[:, 0:128], kbg[:], Tb[:], start=True, stop=True),
                 reads=[kbgb, Tbb], writes=[pbw])
            wT, wTb = P_wT.get()
            k.op("act", lambda e: e.copy(out=wT[:], in_=psw[:, 0:128]), reads=[pbw], writes=[wTb])
            qd, qdb = P_qd.get()
            k.op("pool", lambda e: e.tensor_tensor(out=qd[:], in0=qT[:, qh, cs], in1=egc[:], op=ALU.mult),
                 reads=[qT_b[qh], egcb], writes=[qdb])
            pre.append(dict(u=(u, ub), wT=(wT, wTb), qd=(qd, qdb), qkT=(qkT, qkTb), kdec=(kdec, kdecb),
                            egc=(egc, egcb)))
        vn = []
        for j in range(4):
            p = pre[j]
            ps1, pb1 = k.psum()
            k.op("pe", lambda e: e.matmul(ps1[:, 0:128], p["wT"][0][:], Sb[:, j, :], start=True, stop=True),
                 reads=[p["wT"][1], Sb_b[j]], writes=[pb1])
            v, vb_ = P_vn.get()
            k.op("dve", lambda e: e.tensor_tensor(out=v[:], in0=p["u"][0][:], in1=ps1[:, 0:128], op=ALU.subtract),
                 reads=[p["u"][1], pb1], writes=[vb_])
            vn.append((v, vb_))
        for j in range(4):
            p = pre[j]
            v, vb_ = vn[j]
            pso, pbo = k.psum()
            k.op("pe", lambda e: e.matmul(pso[:, 0:128], Sb[:, j, :], p["qd"][0][:], start=True, stop=False),
                 reads=[Sb_b[j], p["qd"][1]], writes=[pbo])
            k.op("pe", lambda e: e.matmul(pso[:, 0:128], v[:], p["qkT"][0][:], start=False, stop=True),
                 reads=[vb_, p["qkT"][1]], writes=[pbo], skip_self=True)
            psk, pbk = k.psum()
            k.op("pe", lambda e: e.matmul(psk[:, 0:128], p["kdec"][0][:], v[:], start=True, stop=True),
                 reads=[p["kdec"][1], vb_], writes=[pbk])
            k.op("dve", lambda e: e.scalar_tensor_tensor(out=S[:, j, :], in0=S[:, j, :],
                                                         scalar=p["egc"][0][:, 127:128], in1=psk[:, 0:128],
                                                         op0=ALU.mult, op1=ALU.add),
                 reads=[S_b[j], p["egc"][1], pbk], writes=[S_b[j]])
            k.op("act", lambda e: e.copy(out=Sb[:, j, :], in_=S[:, j, :]), reads=[S_b[j]], writes=[Sb_b[j]])
            o, ob = f128.get()
            k.op("act", lambda e: e.copy(out=o[:], in_=pso[:, 0:128]), reads=[pbo], writes=[ob])
            sq, sqb = f128.get()
            k.op("act", lambda e: e.activation(out=sq[:], in_=o[:], func=AF.Square), reads=[ob], writes=[sqb])
            pss, pbs = k.psum()
            k.op("pe", lambda e: e.matmul(pss[:, 0:128], ones[:], sq[:], start=True, stop=True),
                 reads=[const_b, sqb], writes=[pbs])
            k.op("dve", lambda e: e.tensor_scalar(out=sq[:], in0=pss[:, 0:128], scalar1=1.0 / 128,
                                                  scalar2=RMS_EPS, op0=ALU.mult, op1=ALU.add),
                 reads=[pbs], writes=[sqb])
            k.op("act", lambda e: e.activation(out=sq[:], in_=sq[:], func=AF.Sqrt), reads=[sqb], writes=[sqb])
            k.op("dve", lambda e: e.reciprocal(out=sq[:], in_=sq[:]), reads=[sqb], writes=[sqb])
            k.op("dve", lambda e: e.tensor_tensor(out=o[:], in0=o[:], in1=sq[:], op=ALU.mult),
                 reads=[ob, sqb], writes=[ob])
            k.op("dve", lambda e: e.scalar_tensor_tensor(out=ost[:, j, cs], in0=o[:], scalar=ng[:, 0:1],
                                                         in1=zs[:, j, cs], op0=ALU.mult, op1=ALU.mult),
                 reads=[ob, const_b, zs_b[j]], writes=[ostb])

    def program():
        consts()
        load_x(0)
        for i in range(NTILE):
            if i + 1 < NTILE:
                load_x(i + 1)
            proj_tile(i)
            ost, ostb, ods = ostage[i % 2], ost_b[i % 2], ost_ds[i % 2]
            for s in range(4):
                chunk(i, s, ost, ostb)
            k.dma("sp", ogT[:, i * 512:(i + 1) * 512].rearrange("(j p) w -> p j w", p=128), ost[:], ods,
                  reads=[ostb])

    program()
    for d in ost_ds:
        if d.total:
            nc.sync.wait_ge(d.sem, d.total)
    return nc


def gdn_consts():
    r = np.arange(128)
    U = (r[:, None] <= r[None, :]).astype(np.float32)
    PM = np.where(r[None, :] <= r[:, None], 0.0, 1e5).astype(np.float32)
    strict = (r[None, :] < r[:, None]).astype(np.float32)
    return np.ascontiguousarray(np.concatenate([U, PM, strict], axis=1))


def gdn_inputs(c, xT, w_in, conv_w, a_log, dt_bias, norm_g):
    qc = [w_in[:, (2 * c + i) * 128:(2 * c + i + 1) * 128] for i in range(2)]
    kc = [w_in[:, 2048 + (2 * c + i) * 128:2048 + (2 * c + i + 1) * 128] for i in range(2)]
    vc = [w_in[:, 4096 + (4 * c + i) * 128:4096 + (4 * c + i + 1) * 128] for i in range(4)]
    zc = [w_in[:, 8192 + (4 * c + i) * 128:8192 + (4 * c + i + 1) * 128] for i in range(4)]
    wa = w_in[:, 12288 + 4 * c:12288 + 4 * c + 4]
    wb = w_in[:, 12320 + 4 * c:12320 + 4 * c + 4]
    chans = ([(2 * c + i) * 128 for i in range(2)] + [2048 + (2 * c + i) * 128 for i in range(2)]
             + [4096 + (4 * c + i) * 128 for i in range(4)])
    cw = np.stack([conv_w[:, ch:ch + 128].T for ch in chans], axis=1)
    hv = np.concatenate([a_log[4 * c:4 * c + 4], dt_bias[4 * c:4 * c + 4]])
    return {
        "xT": xT,
        "w_fm": _lhsT_layout(np.concatenate(qc + kc + vc + zc, axis=1)),
        "w_ab": _rhs_layout(np.concatenate([wa, wb], axis=1)),
        "convw": np.ascontiguousarray(cw.reshape(128, 32)),
        "hvec": np.ascontiguousarray(np.broadcast_to(hv[None, :], (128, 8))),
        "normg": np.ascontiguousarray(norm_g.reshape(128, 1)),
        "cmask": gdn_consts(),
    }


NEG = -1.0e30
NBIS = 24
TOPK = 256


def build_dsa_stage(NB):
    nc = bass.Bass("TRN2", target_bir_lowering=False)
    k = KB(nc)
    LK = 1024 * NB
    NQ = 128 * NB
    scale = 128 ** -0.5

    def din(name, shape, dt=F32):
        return nc.dram_tensor(name, shape, dt, kind="ExternalInput").ap()

    qT = din("qT", [D, NQ], BF16)
    qiT = din("qiT", [D, NQ], BF16)
    wi = din("wi", [128, NB * 16])
    kT = din("kT", [512, LK], BF16)
    vaug = din("vaug", [LK, 4 * 129], BF16)
    kiT = din("kiT", [128, LK], BF16)
    pmask = din("pmask", [128, 896])
    trimask = din("trimask", [128, 128])
    Dn = din("Dn", [128, 2 * 16 * 128])
    cfar = din("cfar", [128, 16])
    oT = nc.dram_tensor("oT", [D, NQ], BF16, kind="ExternalOutput").ap()

    acc_ps = [(nc.alloc_psum_tensor("acc%d" % i, [128, 512], F32), Buf("acc%d" % i, excl=True)) for i in range(4)]
    k.init_psum(3)
    tps = nc.alloc_psum_tensor("tps", [128, 1024], BF16)
    tps_b = Buf("tps", excl=True)

    sb = nc.alloc_sbuf_tensor
    sc = sb("sc", [128, LK], F32)
    sc_b = Buf("sc")
    selT = sb("selT", [128, LK], BF16)
    selT_b = Buf("selT")
    qb = [sb("qb%d" % i, [128, 16, 128], BF16) for i in range(2)]
    qb_b = [Buf("qb%d" % i) for i in range(2)]
    qib = [sb("qib%d" % i, [128, 16, 128], BF16) for i in range(2)]
    qib_b = [Buf("qib%d" % i) for i in range(2)]
    q_ds = [k.dsem() for _ in range(2)]
    wi_t = sb("wi_t", [128, NB * 16], F32)
    pm_t = sb("pm_t", [128, 896], F32)
    tri_t = sb("tri_t", [128, 128], F32)
    Dn_t = sb("Dn_t", [128, 2, 16, 128], F32)
    cf_t = sb("cf_t", [128, 16], F32)
    identb = sb("identb", [128, 128], BF16)
    ones = sb("ones", [128, 128], F32)
    const_b = Buf("const")
    kis = [(sb("kis%d" % i, [128, 512], BF16), Buf("kis%d" % i), k.dsem()) for i in range(3)]
    rl = Rot(k, "rl", [128, 512], F32, 3)
    eb = Rot(k, "eb", [128, 512], BF16, 3)
    pb_ = Rot(k, "pb", [128, 512], BF16, 3)
    nt = Rot(k, "nt", [128, 256], F32, 2)
    KCH = 1024
    ks = [(sb("ks%d" % i, [128, KCH], BF16), Buf("ks%d" % i), k.dsem()) for i in range(3)]
    vs = [(sb("vs%d" % i, [128, KCH // 128, 129], BF16), Buf("vs%d" % i), k.dsem()) for i in range(3)]
    bs = sb("bs", [128, 16], F32)
    bs_b = Buf("bs")
    selc = Rot(k, "selc", [128, 512], BF16, 2)
    osb = Rot(k, "osb", [128, 128], BF16, 2)
    ost = [(sb("ost%d" % i, [128, 128], BF16), Buf("ost%d" % i), k.dsem()) for i in range(4)]
    cnt = {"ki": 0, "kv": 0, "ost": 0}

    def consts():
        dc = k.dsem()
        k.dma("sp", wi_t[:], wi[:, :], dc, writes=[const_b])
        k.dma("sp", pm_t[:], pmask[:, :], dc, writes=[const_b])
        k.dma("sp", tri_t[:], trimask[:, :], dc, writes=[const_b])
        k.dma("sp", Dn_t[:].rearrange("p a b c -> p (a b c)"), Dn[:, :], dc, writes=[const_b])
        k.dma("sp", cf_t[:], cfar[:, :], dc, writes=[const_b])
        k.op("dve", lambda e: e.memset(ones[:], 1.0), writes=[const_b])
        t, tb = rl.get()
        k.op("pool", lambda e: e.memset(t[:, 0:128], 0.0), writes=[tb])
        k.op("pool", lambda e: e.affine_select(
            out=t[:, 0:128], in_=ones[:], pattern=[[-1, 128]], compare_op=ALU.is_equal,
            fill=0.0, base=0, channel_multiplier=1), reads=[const_b], writes=[tb])
        k.op("dve", lambda e: e.tensor_copy(out=identb[:], in_=t[:, 0:128]), reads=[tb], writes=[const_b])

    def load_q(j):
        i = j % 2
        k.dma("sp", qb[i][:], qT[:, j * 128:(j + 1) * 128].rearrange("(h p) t -> p h t", p=128), q_ds[i],
              writes=[qb_b[i]])
        k.dma("sp", qib[i][:], qiT[:, j * 128:(j + 1) * 128].rearrange("(h p) t -> p h t", p=128), q_ds[i],
              writes=[qib_b[i]])

    def block(j):
        L = 1024 * (j + 1)
        q, q_b, qi, qi_b = qb[j % 2], qb_b[j % 2], qib[j % 2], qib_b[j % 2]
        for ch in range(L // 512):
            kt, ktb, kds = kis[cnt["ki"] % 3]
            cnt["ki"] += 1
            k.dma("sp", kt[:], kiT[:, ch * 512:(ch + 1) * 512], kds, writes=[ktb])
            scc = sc[:, ch * 512:(ch + 1) * 512]
            for h in range(16):
                ps, pb = k.psum()
                k.op("pe", lambda e: e.matmul(ps[:, 0:512], qi[:, h, :], kt[:], start=True, stop=True),
                     reads=[qi_b, ktb], writes=[pb])
                r, rb = rl.get()
                k.op("act", lambda e: e.activation(out=r[:], in_=ps[:, 0:512], func=AF.Relu), reads=[pb], writes=[rb])
                wcol = wi_t[:, j * 16 + h:j * 16 + h + 1]
                if h == 0:
                    k.op("dve", lambda e: e.tensor_scalar(out=scc, in0=r[:], scalar1=wcol, scalar2=None,
                                                          op0=ALU.mult), reads=[rb, const_b], writes=[sc_b])
                else:
                    k.op("dve", lambda e: e.scalar_tensor_tensor(out=scc, in0=r[:], scalar=wcol, in1=scc,
                                                                 op0=ALU.mult, op1=ALU.add),
                         reads=[rb, const_b, sc_b], writes=[sc_b])
        k.op("dve", lambda e: e.tensor_reduce(out=bs[:, 0:1], in_=sc[:, 0:L], axis=AX.X, op=ALU.min),
             reads=[sc_b], writes=[bs_b])
        k.op("dve", lambda e: e.tensor_scalar(out=bs[:, 0:1], in0=bs[:, 0:1], scalar1=-1.0, scalar2=None,
                                              op0=ALU.add), reads=[bs_b], writes=[bs_b])
        k.op("dve", lambda e: e.tensor_tensor(out=sc[:, 0:896], in0=sc[:, 0:896], in1=pm_t[:], op=ALU.add),
             reads=[sc_b, const_b], writes=[sc_b])
        k.op("dve", lambda e: e.tensor_tensor(out=sc[:, L - 128:L], in0=sc[:, L - 128:L], in1=tri_t[:], op=ALU.add),
             reads=[sc_b, const_b], writes=[sc_b])
        k.op("dve", lambda e: e.reduce_max(out=bs[:, 1:2], in_=sc[:, 0:L], axis=AX.X), reads=[sc_b, bs_b], writes=[bs_b])
        lo, hi, mid, cn, ge, df = (bs[:, i:i + 1] for i in (0, 1, 2, 3, 4, 5))
        for it in range(NBIS):
            k.op("dve", lambda e: e.tensor_tensor(out=mid, in0=lo, in1=hi, op=ALU.add), reads=[bs_b], writes=[bs_b])
            k.op("dve", lambda e: e.tensor_scalar(out=mid, in0=mid, scalar1=0.5, scalar2=None, op0=ALU.mult),
                 reads=[bs_b], writes=[bs_b])
            k.op("dve", lambda e: e.tensor_scalar(out=selT[:, 0:L], in0=sc[:, 0:L], scalar1=mid, scalar2=0.0,
                                                  op0=ALU.is_gt, op1=ALU.add, accum_out=cn),
                 reads=[sc_b, bs_b, selT_b], writes=[selT_b, bs_b])
            k.op("dve", lambda e: e.tensor_scalar(out=ge, in0=cn, scalar1=float(TOPK) - 0.5, scalar2=None,
                                                  op0=ALU.is_gt), reads=[bs_b], writes=[bs_b])
            k.op("dve", lambda e: e.tensor_tensor(out=df, in0=mid, in1=lo, op=ALU.subtract), reads=[bs_b], writes=[bs_b])
            k.op("dve", lambda e: e.scalar_tensor_tensor(out=lo, in0=df, scalar=ge, in1=lo, op0=ALU.mult, op1=ALU.add),
                 reads=[bs_b], writes=[bs_b])
            k.op("dve", lambda e: e.tensor_tensor(out=df, in0=hi, in1=mid, op=ALU.subtract), reads=[bs_b], writes=[bs_b])
            k.op("dve", lambda e: e.scalar_tensor_tensor(out=hi, in0=df, scalar=ge, in1=mid, op0=ALU.mult, op1=ALU.add),
                 reads=[bs_b], writes=[bs_b])
        for ch in range(L // 512):
            s1, s1b = selc.get()
            k.op("dve", lambda e: e.tensor_scalar(out=s1[:], in0=sc[:, ch * 512:(ch + 1) * 512], scalar1=lo,
                                                  scalar2=None, op0=ALU.is_gt), reads=[sc_b, bs_b], writes=[s1b])
            for i in range(4):
                k.op("pe", lambda e: e.transpose(tps[:, i * 128:(i + 1) * 128], s1[:, i * 128:(i + 1) * 128], identb[:]),
                     reads=[s1b, const_b], writes=[tps_b], skip_self=True)
            k.op("act", lambda e: e.copy(out=selT[:, ch * 512:(ch + 1) * 512], in_=tps[:, 0:512]),
                 reads=[tps_b], writes=[selT_b])
        for g in range(4):
            for kc in range(L // KCH):
                kw = KCH
                kt, ktb, kds = ks[cnt["kv"] % 3]
                vt, vtb, vds = vs[cnt["kv"] % 3]
                cnt["kv"] += 1
                k.dma("sp", kt[:, 0:kw], kT[g * 128:(g + 1) * 128, kc * KCH:kc * KCH + kw], kds, writes=[ktb])
                k.dma("sp", vt[:, 0:kw // 128, :],
                      vaug[kc * KCH:kc * KCH + kw, g * 129:(g + 1) * 129].rearrange("(s p) c -> p s c", p=128),
                      vds, writes=[vtb])
                for hh in range(4):
                    h = 4 * g + hh
                    ops_, opb = acc_ps[hh]
                    for grp in range(kw // 512):
                        S0 = (kc * KCH + grp * 512) // 128
                        last = (S0 + 4 == L // 128)
                        ps, pb = k.psum()
                        for i in range(4):
                            k.op("pe", lambda e: e.matmul(ps[:, i * 128:(i + 1) * 128],
                                                          kt[:, grp * 512 + i * 128:grp * 512 + (i + 1) * 128],
                                                          q[:, h, :], start=True, stop=True),
                                 reads=[ktb, q_b], writes=[pb], skip_self=True)
                        e_, e_b = eb.get()
                        if not last:
                            k.op("act", lambda e: e.activation(out=e_[:], in_=ps[:, 0:512], func=AF.Exp,
                                                               bias=cf_t[:, h:h + 1], scale=scale),
                                 reads=[pb, const_b], writes=[e_b])
                        else:
                            k.op("act", lambda e: e.activation(out=e_[:, 0:256], in_=ps[:, 0:256], func=AF.Exp,
                                                               bias=cf_t[:, h:h + 1], scale=scale),
                                 reads=[pb, const_b], writes=[e_b])
                            n_, n_b = nt.get()
                            for dlt in (1, 0):
                                cs = slice((1 - dlt) * 128, (2 - dlt) * 128)
                                k.op("dve", lambda e: e.scalar_tensor_tensor(
                                    out=n_[:, cs], in0=ps[:, 256 + (1 - dlt) * 128:256 + (2 - dlt) * 128], scalar=scale,
                                    in1=Dn_t[:, dlt, h, :], op0=ALU.mult, op1=ALU.add),
                                    reads=[pb, const_b, n_b], writes=[n_b])
                            k.op("act", lambda e: e.activation(out=e_[:, 256:512], in_=n_[:], func=AF.Exp),
                                 reads=[n_b, e_b], writes=[e_b])
                        p_, p_b = pb_.get()
                        k.op("pool", lambda e: e.tensor_tensor(out=p_[:], in0=e_[:],
                                                               in1=selT[:, S0 * 128:S0 * 128 + 512], op=ALU.mult),
                             reads=[e_b, selT_b], writes=[p_b])
                        for i in range(4):
                            first = (S0 + i == 0)
                            final = (S0 + i == L // 128 - 1)
                            k.op("pe", lambda e: e.matmul(ops_[:, 0:129], p_[:, i * 128:(i + 1) * 128],
                                                          vt[:, grp * 4 + i, :], start=first, stop=final),
                                 reads=[p_b, vtb], writes=[opb], skip_self=True)
            for hh in range(4):
                h = 4 * g + hh
                ops_, opb = acc_ps[hh]
                k.op("dve", lambda e: e.reciprocal(out=bs[:, 8 + hh:9 + hh], in_=ops_[:, 128:129]),
                     reads=[opb, bs_b], writes=[bs_b])
                o_, o_b = osb.get()
                k.op("act", lambda e: e.activation(out=o_[:], in_=ops_[:, 0:128], func=AF.Identity,
                                                   scale=bs[:, 8 + hh:9 + hh]), reads=[opb, bs_b], writes=[o_b])
                k.op("pe", lambda e: e.transpose(tps[:, 512 + hh * 128:512 + (hh + 1) * 128], o_[:], identb[:]),
                     reads=[o_b, const_b], writes=[tps_b])
                st_, st_b, st_ds = ost[cnt["ost"] % 4]
                cnt["ost"] += 1
                k.op("act", lambda e: e.copy(out=st_[:], in_=tps[:, 512 + hh * 128:512 + (hh + 1) * 128]),
                     reads=[tps_b], writes=[st_b])
                k.dma("sp", oT[h * 128:(h + 1) * 128, j * 128:(j + 1) * 128], st_[:], st_ds, reads=[st_b])

    consts()
    load_q(0)
    for j in range(NB):
        if j + 1 < NB:
            load_q(j + 1)
        block(j)
    for _, _, d in ost:
        if d.total:
            nc.sync.wait_ge(d.sem, d.total)
    return nc


def rel_bucket_np(d):
    d = np.maximum(d, 0)
    df = np.maximum(d, 1).astype(np.float32)
    large = 16 + (np.log(df / np.float32(16)) / np.float32(math.log(128 / 16)) * np.float32(16)).astype(np.int32)
    large = np.minimum(large, 31)
    return np.where(d < 16, d, large)


def dsa_inputs(c, NB, qT, qiT, wi, kT, vaug, kiT, rel_bias):
    SEQL = 1024 * NB
    blocks = [8 * j + c for j in range(NB)]
    cols = np.concatenate([np.arange(b * 128, (b + 1) * 128) for b in blocks])
    sh = 128 * (7 - c)

    def shift_cols(a):
        out = np.zeros_like(a)
        out[:, sh:] = a[:, :SEQL - sh]
        return out

    def shift_rows(a):
        out = np.zeros_like(a)
        out[sh:] = a[:SEQL - sh]
        return out
    pm = np.zeros((128, 896), np.float32)
    pm[:, :sh] = NEG
    r = np.arange(128)
    tri = np.where(r[None, :] <= r[:, None], 0.0, NEG).astype(np.float32)
    dn = np.zeros((128, 2, 16, 128), np.float32)
    for dl in range(2):
        dist = 128 * dl + r[None, :] - r[:, None]
        dn[:, dl, :, :] = rel_bias[rel_bucket_np(dist)].transpose(0, 2, 1)
    wic = wi[cols].reshape(NB, 128, 16).transpose(1, 0, 2).reshape(128, NB * 16)
    return {
        "qT": np.ascontiguousarray(qT[:, cols]), "qiT": np.ascontiguousarray(qiT[:, cols]),
        "wi": np.ascontiguousarray(wic),
        "kT": shift_cols(kT), "vaug": shift_rows(vaug), "kiT": shift_cols(kiT),
        "pmask": pm, "trimask": tri, "Dn": np.ascontiguousarray(dn.reshape(128, -1)),
        "cfar": np.ascontiguousarray(np.broadcast_to(rel_bias[31][None, :], (128, 16))),
    }


_NC_CACHE = {}


def _get_nc(key, fn):
    if key not in _NC_CACHE:
        _NC_CACHE[key] = fn()
    return _NC_CACHE[key]


def _run(nc, in_maps):
    res = run_bass_kernel_spmd(nc, in_maps, core_ids=list(range(NCORE)))
    return res.results


def _token_inputs(li, w_o, ln1_g, ln1_b, ffn_w_gate, ffn_w_up, ffn_conv_w, ffn_w_down, ln2_g, ln2_b,
                  ple_w_proj, ple_w_gate):
    return {
        "w_o": _lhsT_layout(w_o), "w_gate": _lhsT_layout(ffn_w_gate[li]), "w_up": _lhsT_layout(ffn_w_up[li]),
        "w_down": _lhsT_layout(ffn_w_down[li]), "w_pg": _lhsT_layout(ple_w_gate[li]),
        "w_pp": _lhsT_layout(ple_w_proj[li]),
        "vecs": np.ascontiguousarray(np.concatenate([_pvec(ln1_g[li]), _pvec(ln1_b[li]), _pvec(ln2_g[li]),
                                                     _pvec(ln2_b[li])], axis=1)),
        "convw": np.ascontiguousarray(ffn_conv_w[li].reshape(3, FC, 128).transpose(2, 1, 0).reshape(128, FC * 3)),
    }


def _seg(aT, c, nt=2048):
    out = np.zeros((aT.shape[0], nt + 2), aT.dtype)
    if c == 0:
        out[:, 2:] = aT[:, 0:nt]
    else:
        out[:] = aT[:, nt * c - 2:nt * (c + 1)]
    return out


def kernel(x, p, gdn_w_in, gdn_conv_w, gdn_a_log, gdn_dt_bias, gdn_norm_g, gdn_w_o,
           dsa_w_in, dsa_kidx_ln_g, dsa_kidx_ln_b, dsa_w_o, rel_bias,
           ln1_g, ln1_b, ffn_w_gate, ffn_w_up, ffn_conv_w, ffn_w_down, ln2_g, ln2_b,
           ple_w_proj, ple_w_gate):
    import sys
    import time
    t00 = time.time()

    def log(msg):
        print("[kernel %.1fs] %s" % (time.time() - t00, msg), file=sys.stderr, flush=True)
    f32 = lambda a: np.ascontiguousarray(np.asarray(a, dtype=np.float32))
    x = f32(x)
    p = f32(p)
    xT = np.ascontiguousarray(x[0].T)
    pT = [np.ascontiguousarray(p[i, 0].T) for i in range(2)]
    NT = SEQ // NCORE

    nc_a = _get_nc("gdn", lambda: build_gdn_stage(SEQ))
    log("gdn built")
    w_in = f32(gdn_w_in[0])
    ins = [gdn_inputs(c, xT, w_in, f32(gdn_conv_w[0]), f32(gdn_a_log[0]), f32(gdn_dt_bias[0]), f32(gdn_norm_g[0]))
           for c in range(NCORE)]
    ra = _run(nc_a, ins)
    log("gdn done")
    ogT = np.concatenate([np.asarray(r["ogT"]) for r in ra], axis=0)
    del ins, ra

    nc_b = _get_nc("tok0", lambda: build_token_stage(32, NT // 512, True))
    log("tok0 built")
    wd = f32(dsa_w_in[0])
    shared = _token_inputs(0, f32(gdn_w_o[0]), f32(ln1_g), f32(ln1_b), f32(ffn_w_gate), f32(ffn_w_up),
                           f32(ffn_conv_w), f32(ffn_w_down), f32(ln2_g), f32(ln2_b), f32(ple_w_proj), f32(ple_w_gate))
    shared["w_fm"] = _lhsT_layout(np.concatenate([wd[:, 0:2048], wd[:, 2048:2560], wd[:, 3072:5120]], axis=1))
    shared["w_v"] = np.ascontiguousarray(_rhs_layout(wd[:, 2560:3072]).reshape(128, 2, 8 * 512).transpose(1, 0, 2))
    shared["w_kw"] = _rhs_layout(wd[:, 5120:5264])
    shared["kln"] = np.ascontiguousarray(np.broadcast_to(
        np.concatenate([f32(dsa_kidx_ln_g[0]), f32(dsa_kidx_ln_b[0])])[None, :], (128, 256)))
    ins = []
    for c in range(NCORE):
        d = dict(shared)
        d["xT"] = _seg(xT, c)
        d["mixT"] = _seg(ogT, c)
        d["pT"] = _seg(pT[0], c)
        d["hscale"] = np.full((128, 1), 0.0 if c == 0 else 1.0, np.float32)
        ins.append(d)
    rb = _run(nc_b, ins)
    log("tok0 done")
    cat = lambda name, ax: np.concatenate([np.asarray(r[name]) for r in rb], axis=ax)
    x1T = cat("outT", 1)
    qT = cat("qT_o", 1)
    kT = cat("kT_o", 1)
    qiT = cat("qiT_o", 1)
    vaug = cat("v_o", 0).reshape(SEQ, 516)
    kiT = cat("kiT_o", 1)
    wi = cat("wi_o", 0)
    del ins, rb, ogT, shared

    NB = SEQ // 1024
    nc_c = _get_nc("dsa", lambda: build_dsa_stage(NB))
    log("dsa built")
    rb_ = f32(rel_bias)
    ins = [dsa_inputs(c, NB, qT, qiT, wi, kT, vaug, kiT, rb_) for c in range(NCORE)]
    rc = _run(nc_c, ins)
    log("dsa done")
    oT = np.zeros((D, SEQ), NPBF)
    for c in range(NCORE):
        oc = np.asarray(rc[c]["oT"])
        for j in range(NB):
            b = 8 * j + c
            oT[:, b * 128:(b + 1) * 128] = oc[:, j * 128:(j + 1) * 128]
    del ins, rc, qT, qiT, kT, vaug, kiT, wi

    nc_d = _get_nc("tok1", lambda: build_token_stage(16, NT // 512, False))
    log("tok1 built")
    shared = _token_inputs(1, f32(dsa_w_o[0]), f32(ln1_g), f32(ln1_b), f32(ffn_w_gate), f32(ffn_w_up),
                           f32(ffn_conv_w), f32(ffn_w_down), f32(ln2_g), f32(ln2_b), f32(ple_w_proj), f32(ple_w_gate))
    ins = []
    for c in range(NCORE):
        d = dict(shared)
        d["xT"] = _seg(x1T, c)
        d["mixT"] = _seg(oT, c)
        d["pT"] = _seg(pT[1], c)
        d["hscale"] = np.full((128, 1), 0.0 if c == 0 else 1.0, np.float32)
        ins.append(d)
    rd = _run(nc_d, ins)
    log("tok1 done")
    outT = np.concatenate([np.asarray(r["outT"]) for r in rd], axis=1)
    return np.ascontiguousarray(outT.T).reshape(1, SEQ, D).astype(np.float32)
```

```python
import math
import numpy as np
import ml_dtypes
import concourse.bass as bass
import concourse.mybir as mybir
from concourse.bass_utils import run_bass_kernel_spmd

F32 = mybir.dt.float32
BF16 = mybir.dt.bfloat16
AF = mybir.ActivationFunctionType
ALU = mybir.AluOpType
AX = mybir.AxisListType
NPBF = ml_dtypes.bfloat16

D = 2048
DC = D // 128
SEQ = 16384
NCORE = 8
DFF = 5120
FC = DFF // 128
PLE = 256
ALPHA = 4.0 ** 0.25
LN_EPS = 1e-5
RMS_EPS = 1e-6
DEBUG = {}


class Buf:
    __slots__ = ("name", "w", "r", "excl")

    def __init__(self, name="", excl=False):
        self.name = name
        self.w = None
        self.r = []
        self.excl = excl


class DSem:
    def __init__(self, sem, key):
        self.sem = sem
        self.key = key
        self.total = 0


class KB:
    def __init__(self, nc):
        self.nc = nc
        self.eng = {"pe": nc.tensor, "act": nc.scalar, "dve": nc.vector,
                    "pool": nc.gpsimd, "sp": nc.sync}
        self.sem = {e: nc.alloc_semaphore("cs_" + e) for e in self.eng}
        self.cnt = {e: 0 for e in self.eng}
        self.seen = {e: {} for e in self.eng}
        self.semobj = dict(self.sem)
        self.nds = 0
        self.dry = False
        self.psum_tiles = None
        self.psum_i = 0

    def dsem(self):
        key = "ds%d" % self.nds
        s = self.nc.alloc_semaphore(key)
        self.nds += 1
        d = DSem(s, key)
        self.semobj[key] = s
        return d

    def _waits(self, e, reads, writes, skip_self=False):
        deps = {}

        def add(d):
            if d is None:
                return
            kk, v = d
            if deps.get(kk, 0) < v:
                deps[kk] = v
        for b in reads:
            add(b.w)
        for b in writes:
            add(b.w)
            for d in b.r:
                add(d)
        seen = self.seen[e]
        for kk, v in deps.items():
            if skip_self and kk == e:
                continue
            if seen.get(kk, 0) >= v:
                continue
            self.eng[e].wait_ge(self.semobj[kk], v)
            seen[kk] = v

    def op(self, e, fn, reads=(), writes=(), skip_self=False):
        if self.dry:
            return None
        ex = [b for b in reads if b.excl]
        if ex:
            writes = list(writes) + ex
        self._waits(e, reads, writes, skip_self=skip_self)
        ins = fn(self.eng[e])
        self.cnt[e] += 1
        ins.then_inc(self.sem[e], 1)
        tag = (e, self.cnt[e])
        for b in reads:
            b.r.append(tag)
            if len(b.r) > 64:
                b.r = _prune(b.r)
        for b in writes:
            b.w = tag
            b.r = []
        return ins

    def dma(self, q, out, in_, ds, reads=(), writes=()):
        if self.dry:
            return None
        self._waits(q, reads, writes)
        ins = self.eng[q].dma_start(out=out, in_=in_)
        ds.total += 16
        ins.then_inc(ds.sem, 16)
        tag = (ds.key, ds.total)
        for b in reads:
            b.r.append(tag)
        for b in writes:
            b.w = tag
            b.r = []
        return ins

    def wait_all(self, e, bufs):
        if self.dry:
            return
        self._waits(e, [], bufs)

    def init_psum(self, n=8):
        self.npsum = n
        self.psum_tiles = []
        for i in range(n):
            t = self.nc.alloc_psum_tensor("psb%d" % i, [128, 512], F32)
            self.psum_tiles.append((t, Buf("ps%d" % i, excl=True)))

    def psum(self):
        t = self.psum_tiles[self.psum_i % self.npsum]
        self.psum_i += 1
        return t


def _prune(r):
    best = {}
    for kk, v in r:
        if best.get(kk, 0) < v:
            best[kk] = v
    return list(best.items())


class WStream:
    def __init__(self, k, nslots, slot_elems, name="w"):
        self.k = k
        self.n = nslots
        self.slots = []
        for i in range(nslots):
            t = k.nc.alloc_sbuf_tensor("%s_slot%d" % (name, i), [128, slot_elems], BF16)
            self.slots.append((t, Buf("%s%d" % (name, i)), k.dsem()))
        self.reqs = []
        self.issued = 0
        self.consumed = 0
        self.depth = nslots - 1

    def reset(self):
        self.issued = 0
        self.consumed = 0

    def _issue(self, i):
        ap = self.reqs[i]
        t, b, ds = self.slots[i % self.n]
        n = ap.shape[1]
        self.k.dma("pool", t[:, 0:n], ap, ds, writes=[b])

    def next(self, ap, hold=0):
        i = self.consumed
        self.consumed += 1
        t, b, ds = self.slots[i % self.n]
        if self.k.dry:
            self.reqs.append(ap)
            return t, b
        while self.issued < len(self.reqs) and self.issued <= i + self.depth - hold:
            self._issue(self.issued)
            self.issued += 1
        return t, b


def build_token_stage(kmc, ntiles, with_dsa_proj):
    nc = bass.Bass("TRN2", target_bir_lowering=False)
    k = KB(nc)
    k.init_psum(7)
    tps = nc.alloc_psum_tensor("tps", [128, 1024], BF16)
    tps_b = Buf("tps", excl=True)
    NT = 2 + 512 * ntiles

    def din(name, shape, dt=F32):
        return nc.dram_tensor(name, shape, dt, kind="ExternalInput").ap()

    def dout(name, shape, dt=F32):
        return nc.dram_tensor(name, shape, dt, kind="ExternalOutput").ap()

    xT = din("xT", [D, NT])
    mixT = din("mixT", [kmc * 128, NT], BF16)
    pT = din("pT", [PLE, NT])
    w_o = din("w_o", [DC, 128, kmc * 128])
    w_gate = din("w_gate", [FC, 128, DC * 128])
    w_up = din("w_up", [FC, 128, DC * 128])
    w_down = din("w_down", [DC, 128, FC * 128])
    w_pg = din("w_pg", [DC, 128, DC * 128])
    w_pp = din("w_pp", [DC, 128, 2 * 128])
    vecs = din("vecs", [128, 4 * DC])
    convw = din("convw", [128, FC * 3])
    hscale = din("hscale", [128, 1])
    if with_dsa_proj:
        w_fm = din("w_fm", [36, 128, DC * 128])
        w_v = din("w_v", [2, 128, 8 * 512])
        w_kw = din("w_kw", [128, DC * 144])
        kln = din("kln", [128, 256])
        qT_o = dout("qT_o", [D, NT - 2], BF16)
        kT_o = dout("kT_o", [512, NT - 2], BF16)
        qiT_o = dout("qiT_o", [D, NT - 2], BF16)
        v_o = dout("v_o", [NT - 2, 4, 129], BF16)
        kiT_o = dout("kiT_o", [128, NT - 2], BF16)
        wi_o = dout("wi_o", [NT - 2, 16])
    outT = dout("outT", [D, NT - 2])

    sb = nc.alloc_sbuf_tensor
    xz = sb("xz", [128, DC, 512], F32)
    xz_b = [Buf("xz%d" % i) for i in range(DC)]
    xb = sb("xb", [128, DC, 512], BF16)
    xb_b = [Buf("xb%d" % i) for i in range(DC)]
    big = sb("big", [128, FC, 512], BF16)
    big_b = [Buf("big%d" % i) for i in range(FC)]
    ptile = sb("ptile", [128, 2, 512], BF16)
    pt_b = Buf("pt")
    ones = sb("ones", [128, 128], F32)
    ones_b = Buf("ones")
    vec_t = sb("vec_t", [128, 4 * DC], F32)
    cw_t = sb("cw_t", [128, FC * 3], F32)
    hs_t = sb("hs_t", [128, 1], F32)
    const_b = Buf("const")
    haloG = sb("haloG", [128, FC, 2], F32)
    halo_b = [Buf("halo%d" % i) for i in range(FC)]
    G = [sb("G%d" % i, [128, 516], F32) for i in range(2)]
    G_b = [Buf("G%d" % i) for i in range(2)]
    tmp = [sb("tmp%d" % i, [128, 512], F32) for i in range(4)]
    tmp_b = [Buf("tmp%d" % i) for i in range(4)]
    st = [sb("st%d" % i, [128, 512], F32) for i in range(4)]
    st_b = [Buf("st%d" % i) for i in range(4)]
    ws = WStream(k, 4, FC * 128)
    ld = [k.dsem() for _ in range(4)]
    st_ds = [k.dsem() for _ in range(4)]
    xz_ds = [k.dsem() for _ in range(DC)]
    stage_o = [sb("sto%d" % i, [128, 516], BF16) for i in range(4)]
    stage_b = [Buf("sto%d" % i) for i in range(4)]
    stage_i = [0]
    if with_dsa_proj:
        kln_t = sb("kln_t", [128, 256], F32)
        identb = sb("identb", [128, 128], BF16)
        ident_b = Buf("ident")
        kif = sb("kif", [128, 160], F32)
        kif_b = Buf("kif")
        kib = sb("kib", [128, 128], BF16)
        kib_b = Buf("kib")
        sm = sb("sm", [128, 16], F32)
        sm_b = Buf("sm")
        wist = [sb("wist%d" % i, [128, 16], F32) for i in range(2)]
        wist_b = [Buf("wist%d" % i) for i in range(2)]
        wist_ds = [k.dsem() for _ in range(2)]

    tmp_i = [0]

    def gettmp():
        i = tmp_i[0] % 4
        tmp_i[0] += 1
        return tmp[i], tmp_b[i]

    def consts():
        dc = k.dsem()
        k.dma("sp", vec_t[:], vecs[:, :], dc, writes=[const_b])
        k.dma("sp", cw_t[:], convw[:, :], dc, writes=[const_b])
        k.dma("sp", hs_t[:], hscale[:, :], dc, writes=[const_b])
        if with_dsa_proj:
            k.dma("sp", kln_t[:], kln[:, :], dc, writes=[const_b])
        k.op("dve", lambda e: e.memset(ones[:], 1.0), writes=[ones_b])
        if with_dsa_proj:
            k.op("pool", lambda e: e.memset(tmp[0][:, 0:128], 0.0), writes=[tmp_b[0]])
            k.op("pool", lambda e: e.affine_select(
                out=tmp[0][:, 0:128], in_=ones[:], pattern=[[-1, 128]], compare_op=ALU.is_equal,
                fill=0.0, base=0, channel_multiplier=1), reads=[ones_b], writes=[tmp_b[0]])
            k.op("dve", lambda e: e.tensor_copy(out=identb[:], in_=tmp[0][:, 0:128]),
                 reads=[tmp_b[0]], writes=[ident_b])

    def layer_norm(W, gi):
        ps1, pb1 = k.psum()
        ps2, pb2 = k.psum()
        for m in range(DC):
            sq, sqb = gettmp()
            k.op("act", lambda e: e.activation(out=sq[:, 0:W], in_=xz[:, m, 0:W], func=AF.Square),
                 reads=[xz_b[m]], writes=[sqb])
            k.op("pe", lambda e: e.matmul(ps1[:, 0:W], ones[:], xz[:, m, 0:W],
                                          start=(m == 0), stop=(m == DC - 1)),
                 reads=[ones_b, xz_b[m]], writes=[pb1], skip_self=True)
            k.op("pe", lambda e: e.matmul(ps2[:, 0:W], ones[:], sq[:, 0:W],
                                          start=(m == 0), stop=(m == DC - 1)),
                 reads=[ones_b, sqb], writes=[pb2], skip_self=True)
        mean, rstd, nmr, scr = st
        k.op("dve", lambda e: e.tensor_scalar(out=mean[:, 0:W], in0=ps1[:, 0:W], scalar1=1.0 / D,
                                              scalar2=None, op0=ALU.mult),
             reads=[pb1], writes=[st_b[0]])
        k.op("dve", lambda e: e.tensor_tensor(out=scr[:, 0:W], in0=mean[:, 0:W], in1=mean[:, 0:W],
                                              op=ALU.mult), reads=[st_b[0]], writes=[st_b[3]])
        k.op("dve", lambda e: e.scalar_tensor_tensor(out=scr[:, 0:W], in0=ps2[:, 0:W], scalar=1.0 / D,
                                                     in1=scr[:, 0:W], op0=ALU.mult, op1=ALU.subtract),
             reads=[pb2, st_b[3]], writes=[st_b[3]])
        k.op("dve", lambda e: e.tensor_scalar(out=scr[:, 0:W], in0=scr[:, 0:W], scalar1=0.0,
                                              scalar2=LN_EPS, op0=ALU.max, op1=ALU.add),
             reads=[st_b[3]], writes=[st_b[3]])
        k.op("act", lambda e: e.activation(out=scr[:, 0:W], in_=scr[:, 0:W], func=AF.Sqrt),
             reads=[st_b[3]], writes=[st_b[3]])
        k.op("dve", lambda e: e.reciprocal(out=rstd[:, 0:W], in_=scr[:, 0:W]),
             reads=[st_b[3]], writes=[st_b[1]])
        k.op("dve", lambda e: e.scalar_tensor_tensor(out=nmr[:, 0:W], in0=mean[:, 0:W], scalar=-1.0,
                                                     in1=rstd[:, 0:W], op0=ALU.mult, op1=ALU.mult),
             reads=[st_b[0], st_b[1]], writes=[st_b[2]])
        for m in range(DC):
            t, tb = gettmp()
            k.op("dve", lambda e: e.tensor_tensor(out=t[:, 0:W], in0=xz[:, m, 0:W], in1=rstd[:, 0:W],
                                                  op=ALU.mult), reads=[xz_b[m], st_b[1]], writes=[tb])
            k.op("dve", lambda e: e.tensor_tensor(out=t[:, 0:W], in0=t[:, 0:W], in1=nmr[:, 0:W],
                                                  op=ALU.add), reads=[tb, st_b[2]], writes=[tb])
            k.op("act", lambda e: e.activation(out=xz[:, m, 0:W], in_=t[:, 0:W], func=AF.Identity,
                                               bias=vec_t[:, (gi + 1) * DC + m:(gi + 1) * DC + m + 1],
                                               scale=vec_t[:, gi * DC + m:gi * DC + m + 1]),
                 reads=[tb, const_b], writes=[xz_b[m]])
            k.op("pool", lambda e: e.tensor_copy(out=xb[:, m, 0:W], in_=xz[:, m, 0:W]),
                 reads=[xz_b[m]], writes=[xb_b[m]])

    def mm_acc(ps, pb, W, wt, wb, nk, rhs_of, rhs_bufs):
        for kc in range(nk):
            k.op("pe", lambda e: e.matmul(ps[:, 0:W], wt[:, kc * 128:(kc + 1) * 128], rhs_of(kc),
                                          start=(kc == 0), stop=(kc == nk - 1)),
                 reads=[wb, rhs_bufs[kc]], writes=[pb], skip_self=True)

    def tile_pass(c0, W, halo_only):
        k.dma("sp", xz[:, :, 0:W], xT[:, c0:c0 + W].rearrange("(m p) w -> p m w", p=128), ld[0],
              writes=xz_b)
        k.dma("sp", big[:, 0:kmc, 0:W], mixT[:, c0:c0 + W].rearrange("(m p) w -> p m w", p=128), ld[1],
              writes=big_b[0:kmc])
        for m in range(DC):
            wt, wb = ws.next(w_o[m])
            ps, pb = k.psum()
            mm_acc(ps, pb, W, wt, wb, kmc, lambda kc: big[:, kc, 0:W], big_b)
            k.op("dve", lambda e: e.scalar_tensor_tensor(out=xz[:, m, 0:W], in0=xz[:, m, 0:W],
                                                         scalar=ALPHA, in1=ps[:, 0:W],
                                                         op0=ALU.mult, op1=ALU.add),
                 reads=[xz_b[m], pb], writes=[xz_b[m]])
        layer_norm(W, 0)
        for j in range(FC):
            wg, wgb = ws.next(w_gate[j])
            psg, pbg = k.psum()
            mm_acc(psg, pbg, W, wg, wgb, DC, lambda kc: xb[:, kc, 0:W], xb_b)
            if halo_only:
                k.op("dve", lambda e: e.tensor_scalar(out=haloG[:, j, :], in0=psg[:, 0:2],
                                                      scalar1=hs_t[:, 0:1], scalar2=None, op0=ALU.mult),
                     reads=[pbg, const_b], writes=[halo_b[j]])
                continue
            wu, wub = ws.next(w_up[j])
            psu, pbu = k.psum()
            mm_acc(psu, pbu, W, wu, wub, DC, lambda kc: xb[:, kc, 0:W], xb_b)
            g, gb = G[j % 2], G_b[j % 2]
            k.op("pool", lambda e: e.tensor_copy(out=g[:, 0:2], in_=haloG[:, j, :]),
                 reads=[halo_b[j]], writes=[gb])
            k.op("act", lambda e: e.copy(out=g[:, 2:2 + W], in_=psg[:, 0:W]), reads=[pbg], writes=[gb])
            k.op("pool", lambda e: e.tensor_copy(out=haloG[:, j, :], in_=g[:, W:W + 2]),
                 reads=[gb], writes=[halo_b[j]])
            c, cb = gettmp()
            k.op("dve", lambda e: e.tensor_scalar(out=c[:, 0:W], in0=g[:, 0:W],
                                                  scalar1=cw_t[:, 3 * j:3 * j + 1], scalar2=None,
                                                  op0=ALU.mult), reads=[gb, const_b], writes=[cb])
            k.op("dve", lambda e: e.scalar_tensor_tensor(out=c[:, 0:W], in0=g[:, 1:1 + W],
                                                         scalar=cw_t[:, 3 * j + 1:3 * j + 2],
                                                         in1=c[:, 0:W], op0=ALU.mult, op1=ALU.add),
                 reads=[gb, cb, const_b], writes=[cb])
            k.op("dve", lambda e: e.scalar_tensor_tensor(out=c[:, 0:W], in0=g[:, 2:2 + W],
                                                         scalar=cw_t[:, 3 * j + 2:3 * j + 3],
                                                         in1=c[:, 0:W], op0=ALU.mult, op1=ALU.add),
                 reads=[gb, cb, const_b], writes=[cb])
            k.op("act", lambda e: e.activation(out=c[:, 0:W], in_=c[:, 0:W], func=AF.Silu),
                 reads=[cb], writes=[cb])
            k.op("dve", lambda e: e.tensor_tensor(out=big[:, j, 0:W], in0=c[:, 0:W], in1=psu[:, 0:W],
                                                  op=ALU.mult), reads=[cb, pbu], writes=[big_b[j]])
        if halo_only:
            return
        for m in range(DC):
            wt, wb = ws.next(w_down[m])
            ps, pb = k.psum()
            mm_acc(ps, pb, W, wt, wb, FC, lambda kc: big[:, kc, 0:W], big_b)
            k.op("dve", lambda e: e.scalar_tensor_tensor(out=xz[:, m, 0:W], in0=xz[:, m, 0:W],
                                                         scalar=ALPHA, in1=ps[:, 0:W],
                                                         op0=ALU.mult, op1=ALU.add),
                 reads=[xz_b[m], pb], writes=[xz_b[m]])
        layer_norm(W, 2)
        k.dma("pool", ptile[:, :, 0:W], pT[:, c0:c0 + W].rearrange("(m p) w -> p m w", p=128), ld[2],
              writes=[pt_b])
        for m in range(DC):
            wt, wb = ws.next(w_pg[m])
            psa, pba = k.psum()
            mm_acc(psa, pba, W, wt, wb, DC, lambda kc: xb[:, kc, 0:W], xb_b)
            wt2, wb2 = ws.next(w_pp[m], hold=1)
            psb, pbb = k.psum()
            mm_acc(psb, pbb, W, wt2, wb2, 2, lambda kc: ptile[:, kc, 0:W], [pt_b, pt_b])
            t, tb = gettmp()
            k.op("act", lambda e: e.activation(out=t[:, 0:W], in_=psa[:, 0:W], func=AF.Sigmoid),
                 reads=[pba], writes=[tb])
            k.op("dve", lambda e: e.tensor_tensor(out=t[:, 0:W], in0=t[:, 0:W], in1=psb[:, 0:W],
                                                  op=ALU.mult), reads=[tb, pbb], writes=[tb])
            k.op("dve", lambda e: e.tensor_tensor(out=xz[:, m, 0:W], in0=xz[:, m, 0:W], in1=t[:, 0:W],
                                                  op=ALU.add), reads=[xz_b[m], tb], writes=[xz_b[m]])
            k.dma("sp", outT[m * 128:(m + 1) * 128, c0 - 2:c0 - 2 + W], xz[:, m, 0:W], xz_ds[m],
                  reads=[xz_b[m]])
            if with_dsa_proj:
                k.op("pool", lambda e: e.tensor_copy(out=big[:, m, 0:W], in_=xz[:, m, 0:W]),
                     reads=[xz_b[m]], writes=[big_b[m]])
        if with_dsa_proj:
            dsa_proj(c0 - 2, W)

    def getstage():
        i = stage_i[0] % 4
        stage_i[0] += 1
        return stage_o[i], stage_b[i], st_ds[i]

    def dsa_proj(t0, W):
        outs = [(qT_o, 16), (kT_o, 4), (qiT_o, 16)]
        ci = 0
        for dst, nch in (outs if DEBUG.get("fm", True) else []):
            for c in range(nch):
                wt, wb = ws.next(w_fm[ci])
                ci += 1
                ps, pb = k.psum()
                mm_acc(ps, pb, W, wt, wb, DC, lambda kc: big[:, kc, 0:W], big_b)
                so, sob, sds = getstage()
                k.op("act", lambda e: e.copy(out=so[:, 0:W], in_=ps[:, 0:W]), reads=[pb], writes=[sob])
                k.dma("sp", dst[c * 128:(c + 1) * 128, t0:t0 + W], so[:, 0:W], sds, reads=[sob])
        wv0, wvb0 = ws.next(w_v[0])
        wv1, wvb1 = ws.next(w_v[1], hold=1)
        wkw, wkwb = ws.next(w_kw, hold=2)
        for s in range(W // 128 if DEBUG.get('tm', True) else 0):
            sl = slice(s * 128, (s + 1) * 128)
            ps, pb = k.psum()
            for kc in range(DC):
                wt, wb = (wv0, wvb0) if kc < 8 else (wv1, wvb1)
                k.op("pe", lambda e: e.matmul(ps[:, 0:512], big[:, kc, sl],
                                              wt[:, (kc % 8) * 512:(kc % 8 + 1) * 512],
                                              start=(kc == 0), stop=(kc == DC - 1)),
                     reads=[wb, big_b[kc]], writes=[pb], skip_self=True)
            so, sob, sds = getstage()
            k.op("pool", lambda e: e.memset(so[:, 0:516], 1.0), writes=[sob])
            sov = so[:, 0:516].rearrange("p (g c) -> p g c", c=129)
            k.op("act", lambda e: e.copy(out=sov[:, :, 0:128],
                                         in_=ps[:, 0:512].rearrange("p (g c) -> p g c", c=128)),
                 reads=[pb], writes=[sob])
            k.dma("sp", v_o[t0 + s * 128:t0 + (s + 1) * 128, :, :], sov, sds, reads=[sob])
            if not DEBUG.get("kw", True):
                continue
            ps2, pb2 = k.psum()
            for kc in range(DC):
                k.op("pe", lambda e: e.matmul(ps2[:, 0:144], big[:, kc, sl], wkw[:, kc * 144:(kc + 1) * 144],
                                              start=(kc == 0), stop=(kc == DC - 1)),
                     reads=[wkwb, big_b[kc]], writes=[pb2], skip_self=True)
            wi_t, wi_b, wi_ds = wist[s % 2], wist_b[s % 2], wist_ds[s % 2]
            k.op("dve", lambda e: e.tensor_scalar(out=wi_t[:], in0=ps2[:, 128:144],
                                                  scalar1=(16 ** -0.5) * (128 ** -0.5), scalar2=None,
                                                  op0=ALU.mult), reads=[pb2], writes=[wi_b])
            k.dma("sp", wi_o[t0 + s * 128:t0 + (s + 1) * 128, :], wi_t[:], wi_ds, reads=[wi_b])
            if not DEBUG.get("kiln", True):
                continue
            _n = [0]
            _lim = DEBUG.get("kiln_n", 99)
            def kop(*a_, **kw_):
                _n[0] += 1
                if _n[0] <= _lim:
                    k.op(*a_, **kw_)
            kop("dve", lambda e: e.memset(sm[:], 0.0), writes=[sm_b])
            kop("act", lambda e: e.copy(out=kif[:, 0:128], in_=ps2[:, 0:128]), reads=[pb2], writes=[kif_b])
            kop("dve", lambda e: e.reduce_sum(out=sm[:, 0:1], in_=kif[:, 0:128], axis=AX.X),
                 reads=[kif_b, sm_b], writes=[sm_b])
            t, tb = gettmp()
            kop("act", lambda e: e.activation(out=t[:, 0:128], in_=ps2[:, 0:128], func=AF.Square,
                                               accum_out=sm[:, 1:2]), reads=[pb2, sm_b], writes=[tb, sm_b])
            kop("dve", lambda e: e.tensor_scalar(out=sm[:, 2:4], in0=sm[:, 0:2], scalar1=1.0 / 128,
                                                  scalar2=None, op0=ALU.mult), reads=[sm_b], writes=[sm_b])
            kop("dve", lambda e: e.tensor_tensor(out=sm[:, 4:5], in0=sm[:, 2:3], in1=sm[:, 2:3],
                                                  op=ALU.mult), reads=[sm_b], writes=[sm_b])
            kop("dve", lambda e: e.tensor_tensor(out=sm[:, 5:6], in0=sm[:, 3:4], in1=sm[:, 4:5],
                                                  op=ALU.subtract), reads=[sm_b], writes=[sm_b])
            kop("dve", lambda e: e.tensor_scalar(out=sm[:, 5:6], in0=sm[:, 5:6], scalar1=0.0,
                                                  scalar2=LN_EPS, op0=ALU.max, op1=ALU.add),
                 reads=[sm_b], writes=[sm_b])
            kop("act", lambda e: e.activation(out=sm[:, 6:7], in_=sm[:, 5:6], func=AF.Sqrt),
                 reads=[sm_b], writes=[sm_b])
            kop("dve", lambda e: e.reciprocal(out=sm[:, 7:8], in_=sm[:, 6:7]), reads=[sm_b], writes=[sm_b])
            kop("dve", lambda e: e.tensor_scalar(out=kif[:, 0:128], in0=kif[:, 0:128],
                                                  scalar1=sm[:, 2:3], scalar2=sm[:, 7:8],
                                                  op0=ALU.subtract, op1=ALU.mult),
                 reads=[kif_b, sm_b], writes=[kif_b])
            kop("dve", lambda e: e.tensor_tensor(out=kif[:, 0:128], in0=kif[:, 0:128], in1=kln_t[:, 0:128],
                                                  op=ALU.mult), reads=[kif_b, const_b], writes=[kif_b])
            kop("dve", lambda e: e.tensor_tensor(out=kib[:], in0=kif[:, 0:128], in1=kln_t[:, 128:256],
                                                  op=ALU.add), reads=[kif_b, const_b], writes=[kib_b])
            if not DEBUG.get("kitr", True):
                continue
            ps3b, pb3 = tps, tps_b
            k.op("pe", lambda e: e.transpose(ps3b[:, 0:128], kib[:], identb[:]),
                 reads=[kib_b, ident_b], writes=[pb3])
            so, sob, sds = getstage()
            k.op("act", lambda e: e.copy(out=so[:, 0:128], in_=ps3b[:, 0:128]), reads=[pb3], writes=[sob])
            k.dma("sp", kiT_o[:, t0 + s * 128:t0 + (s + 1) * 128], so[:, 0:128], sds, reads=[sob])

    def program():
        consts()
        if DEBUG.get("halo", True):
            tile_pass(0, 2, True)
        for i in range(ntiles if DEBUG.get("main", True) else 0):
            tile_pass(2 + 512 * i, 512, False)

    k.dry = True
    program()
    k.dry = False
    ws.reset()
    tmp_i[0] = 0
    stage_i[0] = 0
    k.psum_i = 0
    program()
    for d in st_ds + xz_ds + (wist_ds if with_dsa_proj else []):
        if d.total:
            nc.sync.wait_ge(d.sem, d.total)
    return nc


def _lhsT_layout(w):
    K, M = w.shape
    return np.ascontiguousarray(w.reshape(K // 128, 128, M // 128, 128).transpose(2, 1, 0, 3)
                                .reshape(M // 128, 128, (K // 128) * 128))


def _rhs_layout(w):
    K, N = w.shape
    return np.ascontiguousarray(w.reshape(K // 128, 128, N).transpose(1, 0, 2).reshape(128, (K // 128) * N))


def _pvec(v):
    return np.ascontiguousarray(v.reshape(-1, 128).T)


class Rot:
    def __init__(self, k, name, shape, dt, n):
        self.t = [(k.nc.alloc_sbuf_tensor("%s%d" % (name, i), shape, dt), Buf("%s%d" % (name, i)))
                  for i in range(n)]
        self.i = 0

    def get(self):
        r = self.t[self.i % len(self.t)]
        self.i += 1
        return r


def build_gdn_stage(L):
    nc = bass.Bass("TRN2", target_bir_lowering=False)
    k = KB(nc)
    k.init_psum(7)
    tps = nc.alloc_psum_tensor("tps", [128, 1024], BF16)
    tps_b = Buf("tps", excl=True)
    NTILE = L // 512

    def din(name, shape, dt=F32):
        return nc.dram_tensor(name, shape, dt, kind="ExternalInput").ap()

    xT = din("xT", [D, L])
    w_fm = din("w_fm", [12, 128, DC * 128])
    w_ab = din("w_ab", [128, DC * 8])
    convw = din("convw", [128, 8 * 4])
    hvec = din("hvec", [128, 8])
    normg = din("normg", [128, 1])
    cmask = din("cmask", [128, 3 * 128])
    ogT = nc.dram_tensor("ogT", [512, L], BF16, kind="ExternalOutput").ap()

    sb = nc.alloc_sbuf_tensor
    wfm = sb("wfm", [128, 12, DC * 128], BF16)
    wfm_b = Buf("wfm")
    wab = sb("wab", [128, DC * 8], BF16)
    cw = sb("cw", [128, 32], F32)
    hv = sb("hv", [128, 8], F32)
    negA = sb("negA", [128, 4], F32)
    ng = sb("ng", [128, 1], F32)
    cm = sb("cm", [128, 384], F32)
    ones = sb("ones", [128, 128], F32)
    ident = sb("ident", [128, 128], F32)
    identb = sb("identb", [128, 128], BF16)
    const_b = Buf("const")
    U = cm[:, 0:128]
    PMtri = cm[:, 128:256]
    strict01 = cm[:, 256:384]

    xb = [sb("xb%d" % i, [128, DC, 512], BF16) for i in range(2)]
    xb_b = [Buf("xb%d" % i) for i in range(2)]
    xb_ds = [k.dsem() for _ in range(2)]
    halo = sb("halo", [128, 8, 3], F32)
    halo_b = [Buf("halo%d" % i) for i in range(8)]
    Gc = Rot(k, "Gc", [128, 516], F32, 2)
    t512 = Rot(k, "t512", [128, 512], F32, 4)
    qT = sb("qT", [128, 2, 512], BF16)
    qT_b = [Buf("qT%d" % i) for i in range(2)]
    kT = sb("kT", [128, 2, 512], BF16)
    kT_b = [Buf("kT%d" % i) for i in range(2)]
    vT = sb("vT", [128, 4, 512], BF16)
    vT_b = [Buf("vT%d" % i) for i in range(4)]
    zs = sb("zs", [128, 4, 512], F32)
    zs_b = [Buf("zs%d" % i) for i in range(4)]
    tok = sb("tok", [128, 4, 32], F32)
    tok_b = [Buf("tok%d" % i) for i in range(4)]
    f128 = Rot(k, "f128", [128, 128], F32, 16)
    b128 = Rot(k, "b128", [128, 128], BF16, 8)
    P_ktok = Rot(k, "pktok", [128, 128], BF16, 4)
    P_KK = Rot(k, "pkk", [128, 128], F32, 4)
    P_QK = Rot(k, "pqk", [128, 128], F32, 4)
    P_egc = Rot(k, "pegc", [128, 128], F32, 8)
    P_u = Rot(k, "pu", [128, 128], F32, 8)
    P_qkT = Rot(k, "pqkT", [128, 128], BF16, 8)
    P_wT = Rot(k, "pwT", [128, 128], BF16, 8)
    P_qd = Rot(k, "pqd", [128, 128], BF16, 8)
    P_kdec = Rot(k, "pkdec", [128, 128], BF16, 8)
    P_vn = Rot(k, "pvn", [128, 128], BF16, 8)
    S = sb("S", [128, 4, 128], F32)
    Sb = sb("Sb", [128, 4, 128], BF16)
    S_b = [Buf("S%d" % i) for i in range(4)]
    Sb_b = [Buf("Sb%d" % i) for i in range(4)]
    ostage = [sb("ost%d" % i, [128, 4, 512], BF16) for i in range(2)]
    ost_b = [Buf("ost%d" % i) for i in range(2)]
    ost_ds = [k.dsem() for _ in range(2)]

    def consts():
        dc = k.dsem()
        for i in range(12):
            k.dma("pool", wfm[:, i, :], w_fm[i], dc, writes=[wfm_b])
        k.dma("pool", wab[:], w_ab[:, :], dc, writes=[wfm_b])
        dc2 = k.dsem()
        k.dma("sp", cw[:], convw[:, :], dc2, writes=[const_b])
        k.dma("sp", hv[:], hvec[:, :], dc2, writes=[const_b])
        k.dma("sp", ng[:], normg[:, :], dc2, writes=[const_b])
        k.dma("sp", cm[:], cmask[:, :], dc2, writes=[const_b])
        k.op("dve", lambda e: e.memset(ones[:], 1.0), writes=[const_b])
        k.op("pool", lambda e: e.memset(ident[:], 0.0), writes=[const_b])
        k.op("pool", lambda e: e.affine_select(
            out=ident[:], in_=ones[:], pattern=[[-1, 128]], compare_op=ALU.is_equal,
            fill=0.0, base=0, channel_multiplier=1), reads=[const_b], writes=[const_b])
        k.op("dve", lambda e: e.tensor_copy(out=identb[:], in_=ident[:]), reads=[const_b], writes=[const_b])
        k.op("act", lambda e: e.activation(out=negA[:], in_=hv[:, 0:4], func=AF.Exp),
             reads=[const_b], writes=[const_b])
        k.op("dve", lambda e: e.tensor_scalar(out=negA[:], in0=negA[:], scalar1=-1.0, scalar2=None,
                                              op0=ALU.mult), reads=[const_b], writes=[const_b])
        k.op("dve", lambda e: e.memset(halo[:], 0.0), writes=halo_b)
        k.op("dve", lambda e: e.memset(S[:], 0.0), writes=S_b)
        k.op("dve", lambda e: e.memset(Sb[:], 0.0), writes=Sb_b)

    def load_x(i):
        k.dma("pool", xb[i % 2][:], xT[:, i * 512:(i + 1) * 512].rearrange("(m p) w -> p m w", p=128),
              xb_ds[i % 2], writes=[xb_b[i % 2]])

    def proj_tile(i):
        x, xbuf = xb[i % 2], xb_b[i % 2]
        for c in range(12):
            ps, pb = k.psum()
            for kc in range(DC):
                k.op("pe", lambda e: e.matmul(ps[:, 0:512], wfm[:, c, kc * 128:(kc + 1) * 128], x[:, kc, :],
                                              start=(kc == 0), stop=(kc == DC - 1)),
                     reads=[wfm_b, xbuf], writes=[pb], skip_self=True)
            if c >= 8:
                k.op("act", lambda e: e.activation(out=zs[:, c - 8, :], in_=ps[:, 0:512], func=AF.Silu),
                     reads=[pb], writes=[zs_b[c - 8]])
                continue
            g, gb = Gc.get()
            k.op("pool", lambda e: e.tensor_copy(out=g[:, 0:3], in_=halo[:, c, :]), reads=[halo_b[c]], writes=[gb])
            k.op("act", lambda e: e.copy(out=g[:, 3:515], in_=ps[:, 0:512]), reads=[pb], writes=[gb])
            k.op("pool", lambda e: e.tensor_copy(out=halo[:, c, :], in_=g[:, 512:515]), reads=[gb], writes=[halo_b[c]])
            t, tb = t512.get()
            k.op("dve", lambda e: e.tensor_scalar(out=t[:], in0=g[:, 0:512], scalar1=cw[:, 4 * c:4 * c + 1],
                                                  scalar2=None, op0=ALU.mult), reads=[gb, const_b], writes=[tb])
            for j in range(1, 4):
                k.op("dve", lambda e: e.scalar_tensor_tensor(out=t[:], in0=g[:, j:j + 512],
                                                             scalar=cw[:, 4 * c + j:4 * c + j + 1], in1=t[:],
                                                             op0=ALU.mult, op1=ALU.add),
                     reads=[gb, tb, const_b], writes=[tb])
            if c >= 4:
                k.op("act", lambda e: e.activation(out=vT[:, c - 4, :], in_=t[:], func=AF.Silu),
                     reads=[tb], writes=[vT_b[c - 4]])
                continue
            k.op("act", lambda e: e.activation(out=t[:], in_=t[:], func=AF.Silu), reads=[tb], writes=[tb])
            sq, sqb = t512.get()
            k.op("act", lambda e: e.activation(out=sq[:], in_=t[:], func=AF.Square), reads=[tb], writes=[sqb])
            ps2, pb2 = k.psum()
            k.op("pe", lambda e: e.matmul(ps2[:, 0:512], ones[:], sq[:], start=True, stop=True),
                 reads=[const_b, sqb], writes=[pb2])
            k.op("dve", lambda e: e.tensor_scalar(out=sq[:], in0=ps2[:, 0:512], scalar1=RMS_EPS, scalar2=None,
                                                  op0=ALU.add), reads=[pb2], writes=[sqb])
            k.op("act", lambda e: e.activation(out=sq[:], in_=sq[:], func=AF.Sqrt), reads=[sqb], writes=[sqb])
            k.op("dve", lambda e: e.reciprocal(out=sq[:], in_=sq[:]), reads=[sqb], writes=[sqb])
            if c < 2:
                k.op("dve", lambda e: e.scalar_tensor_tensor(out=qT[:, c, :], in0=t[:], scalar=128 ** -0.5,
                                                             in1=sq[:], op0=ALU.mult, op1=ALU.mult),
                     reads=[tb, sqb], writes=[qT_b[c]])
            else:
                k.op("dve", lambda e: e.tensor_tensor(out=kT[:, c - 2, :], in0=t[:], in1=sq[:], op=ALU.mult),
                     reads=[tb, sqb], writes=[kT_b[c - 2]])
        for s in range(4):
            ps, pb = k.psum()
            for kc in range(DC):
                k.op("pe", lambda e: e.matmul(ps[:, 0:8], x[:, kc, s * 128:(s + 1) * 128], wab[:, kc * 8:(kc + 1) * 8],
                                              start=(kc == 0), stop=(kc == DC - 1)),
                     reads=[wfm_b, xbuf], writes=[pb], skip_self=True)
            tk, tkb = tok[:, s, :], tok_b[s]
            k.op("act", lambda e: e.activation(out=tk[:, 4:8], in_=ps[:, 4:8], func=AF.Sigmoid),
                 reads=[pb], writes=[tkb])
            k.op("dve", lambda e: e.tensor_tensor(out=tk[:, 0:4], in0=ps[:, 0:4], in1=hv[:, 4:8], op=ALU.add),
                 reads=[pb, const_b, tkb], writes=[tkb])
            k.op("act", lambda e: e.activation(out=tk[:, 0:4], in_=tk[:, 0:4], func=AF.Exp), reads=[tkb], writes=[tkb])
            k.op("act", lambda e: e.activation(out=tk[:, 0:4], in_=tk[:, 0:4], func=AF.Ln, bias=1.0),
                 reads=[tkb], writes=[tkb])
            k.op("dve", lambda e: e.tensor_tensor(out=tk[:, 0:4], in0=tk[:, 0:4], in1=negA[:], op=ALU.mult),
                 reads=[tkb, const_b], writes=[tkb])
            k.op("dve", lambda e: e.tensor_scalar(out=tk[:, 8:12], in0=tk[:, 4:8], scalar1=-1.0, scalar2=None,
                                                  op0=ALU.mult), reads=[tkb], writes=[tkb])

    def chunk(i, s, ost, ostb):
        cs = slice(s * 128, (s + 1) * 128)
        tk, tkb = tok[:, s, :], tok_b[s]
        ps, pb = k.psum()
        k.op("pe", lambda e: e.matmul(ps[:, 0:4], U, tk[:, 0:4], start=True, stop=True),
             reads=[const_b, tkb], writes=[pb])
        k.op("dve", lambda e: e.tensor_copy(out=tk[:, 12:16], in_=ps[:, 0:4]), reads=[pb, tkb], writes=[tkb])
        ktok = []
        for qh in range(2):
            k.op("pe", lambda e: e.transpose(tps[:, qh * 128:(qh + 1) * 128], kT[:, qh, cs], identb[:]),
                 reads=[kT_b[qh], const_b], writes=[tps_b])
            kt, ktb = P_ktok.get()
            k.op("act", lambda e: e.copy(out=kt[:], in_=tps[:, qh * 128:(qh + 1) * 128]), reads=[tps_b], writes=[ktb])
            ktok.append((kt, ktb))
        KK = []
        QK = []
        for qh in range(2):
            ps1, pb1 = k.psum()
            k.op("pe", lambda e: e.matmul(ps1[:, 0:128], kT[:, qh, cs], kT[:, qh, cs], start=True, stop=True),
                 reads=[kT_b[qh]], writes=[pb1])
            ps2, pb2 = k.psum()
            k.op("pe", lambda e: e.matmul(ps2[:, 0:128], qT[:, qh, cs], kT[:, qh, cs], start=True, stop=True),
                 reads=[qT_b[qh], kT_b[qh]], writes=[pb2])
            kk, kkb = P_KK.get()
            k.op("act", lambda e: e.copy(out=kk[:], in_=ps1[:, 0:128]), reads=[pb1], writes=[kkb])
            qk, qkb = P_QK.get()
            k.op("act", lambda e: e.copy(out=qk[:], in_=ps2[:, 0:128]), reads=[pb2], writes=[qkb])
            KK.append((kk, kkb))
            QK.append((qk, qkb))
        pre = []
        for j in range(4):
            qh = j // 2
            gu, gub = f128.get()
            k.op("dve", lambda e: e.tensor_scalar(out=gu[:], in0=U, scalar1=tk[:, j:j + 1], scalar2=None,
                                                  op0=ALU.mult), reads=[const_b, tkb], writes=[gub])
            psr, pbr = k.psum()
            k.op("pe", lambda e: e.matmul(psr[:, 0:128], ones[:], gu[:], start=True, stop=True),
                 reads=[const_b, gub], writes=[pbr])
            gcr, gcrb = f128.get()
            k.op("act", lambda e: e.copy(out=gcr[:], in_=psr[:, 0:128]), reads=[pbr], writes=[gcrb])
            egc, egcb = P_egc.get()
            k.op("act", lambda e: e.activation(out=egc[:], in_=gcr[:], func=AF.Exp), reads=[gcrb], writes=[egcb])
            k.op("act", lambda e: e.activation(out=tk[:, 16 + j:17 + j], in_=tk[:, 12 + j:13 + j], func=AF.Exp),
                 reads=[tkb], writes=[tkb])
            k.op("dve", lambda e: e.tensor_tensor(out=tk[:, 16 + j:17 + j], in0=tk[:, 16 + j:17 + j],
                                                  in1=tk[:, 4 + j:5 + j], op=ALU.mult), reads=[tkb], writes=[tkb])
            k.op("act", lambda e: e.activation(out=tk[:, 20 + j:21 + j], in_=tk[:, 12 + j:13 + j], func=AF.Exp,
                                               bias=gcr[:, 127:128], scale=-1.0), reads=[tkb, gcrb], writes=[tkb])
            dec, decb = f128.get()
            k.op("dve", lambda e: e.scalar_tensor_tensor(out=dec[:], in0=gcr[:], scalar=tk[:, 12 + j:13 + j],
                                                         in1=PMtri, op0=ALU.subtract, op1=ALU.add),
                 reads=[gcrb, tkb, const_b], writes=[decb])
            k.op("act", lambda e: e.activation(out=dec[:], in_=dec[:], func=AF.Exp, scale=-1.0),
                 reads=[decb], writes=[decb])
            qkm, qkmb = b128.get()
            k.op("dve", lambda e: e.tensor_tensor(out=qkm[:], in0=QK[qh][0][:], in1=dec[:], op=ALU.mult),
                 reads=[QK[qh][1], decb], writes=[qkmb])
            k.op("pe", lambda e: e.transpose(tps[:, 256:384], qkm[:], identb[:]),
                 reads=[qkmb, const_b], writes=[tps_b])
            qkT, qkTb = P_qkT.get()
            k.op("act", lambda e: e.copy(out=qkT[:], in_=tps[:, 256:384]), reads=[tps_b], writes=[qkTb])
            M, Mb = f128.get()
            k.op("dve", lambda e: e.scalar_tensor_tensor(out=M[:], in0=KK[qh][0][:], scalar=tk[:, 8 + j:9 + j],
                                                         in1=dec[:], op0=ALU.mult, op1=ALU.mult),
                 reads=[KK[qh][1], tkb, decb], writes=[Mb])
            k.op("pool", lambda e: e.tensor_tensor(out=M[:], in0=M[:], in1=strict01, op=ALU.mult),
                 reads=[Mb, const_b], writes=[Mb])
            psx, pbx = k.psum()
            k.op("pe", lambda e: e.transpose(psx[:, 0:128], M[:], ident[:]), reads=[Mb, const_b], writes=[pbx])
            X, Xb = f128.get()
            k.op("act", lambda e: e.copy(out=X[:], in_=psx[:, 0:128]), reads=[pbx], writes=[Xb])
            TT, TTb = f128.get()
            k.op("pool", lambda e: e.tensor_tensor(out=TT[:], in0=X[:], in1=ident[:], op=ALU.add),
                 reads=[Xb, const_b], writes=[TTb])
            for it in range(1, 7):
                psm, pbm = k.psum()
                k.op("pe", lambda e: e.matmul(psm[:, 0:128], X[:], M[:], start=True, stop=True),
                     reads=[Xb, Mb], writes=[pbm])
                if it < 6:
                    psx2, pbx2 = k.psum()
                    k.op("pe", lambda e: e.matmul(psx2[:, 0:128], M[:], X[:], start=True, stop=True),
                         reads=[Xb, Mb], writes=[pbx2])
                M2, M2b = f128.get()
                k.op("act", lambda e: e.copy(out=M2[:], in_=psm[:, 0:128]), reads=[pbm], writes=[M2b])
                if it < 6:
                    X2, X2b = f128.get()
                    k.op("dve", lambda e: e.tensor_copy(out=X2[:], in_=psx2[:, 0:128]), reads=[pbx2], writes=[X2b])
                pst, pbt = k.psum()
                k.op("pe", lambda e: e.matmul(pst[:, 0:128], M2[:], TT[:], start=True, stop=True),
                     reads=[M2b, TTb], writes=[pbt])
                TT2, TT2b = f128.get()
                k.op("dve", lambda e: e.tensor_tensor(out=TT2[:], in0=TT[:], in1=pst[:, 0:128], op=ALU.add),
                     reads=[TTb, pbt], writes=[TT2b])
                M, Mb = M2, M2b
                if it < 6:
                    X, Xb = X2, X2b
                TT, TTb = TT2, TT2b
            Tb, Tbb = b128.get()
            k.op("pool", lambda e: e.tensor_copy(out=Tb[:], in_=TT[:]), reads=[TTb], writes=[Tbb])
            kbg, kbgb = b128.get()
            k.op("pool", lambda e: e.tensor_scalar(out=kbg[:], in0=ktok[qh][0][:], scalar1=tk[:, 16 + j:17 + j],
                                                   scalar2=None, op0=ALU.mult), reads=[ktok[qh][1], tkb], writes=[kbgb])
            kdec, kdecb = P_kdec.get()
            k.op("pool", lambda e: e.tensor_scalar(out=kdec[:], in0=ktok[qh][0][:], scalar1=tk[:, 20 + j:21 + j],
                                                   scalar2=None, op0=ALU.mult), reads=[ktok[qh][1], tkb], writes=[kdecb])
            k.op("pe", lambda e: e.transpose(tps[:, 384:512], vT[:, j, cs], identb[:]),
                 reads=[vT_b[j], const_b], writes=[tps_b])
            vb, vbb = b128.get()
            k.op("act", lambda e: e.activation(out=vb[:], in_=tps[:, 384:512], func=AF.Identity,
                                               scale=tk[:, 4 + j:5 + j]), reads=[tps_b, tkb], writes=[vbb])
            psu, pbu = k.psum()
            k.op("pe", lambda e: e.matmul(psu[:, 0:128], Tb[:], vb[:], start=True, stop=True),
                 reads=[Tbb, vbb], writes=[pbu])
            u, ub = P_u.get()
            k.op("act", lambda e: e.copy(out=u[:], in_=psu[:, 0:128]), reads=[pbu], writes=[ub])
            psw, pbw = k.psum()
            k.op("pe", lambda e: e.matmul(psw[:, 0:128], kbg[:], Tb[:], start=True, stop=True),
                 reads=[kbgb, Tbb], writes=[pbw])
            wT, wTb = P_wT.get()
            k.op("act", lambda e: e.copy(out=wT[:], in_=psw[:, 0:128]), reads=[pbw], writes=[wTb])
            qd, qdb = P_qd.get()
            k.op("pool", lambda e: e.tensor_tensor(out=qd[:], in0=qT[:, qh, cs], in1=egc[:], op=ALU.mult),
                 reads=[qT_b[qh], egcb], writes=[qdb])
            pre.append(dict(u=(u, ub), wT=(wT, wTb), qd=(qd, qdb), qkT=(qkT, qkTb), kdec=(kdec, kdecb),
                            egc=(egc, egcb)))
        vn = []
        for j in range(4):
            p = pre[j]
            ps1, pb1 = k.psum()
            k.op("pe", lambda e: e.matmul(ps1[:, 0:128], p["wT"][0][:], Sb[:, j, :], start=True, stop=True),
                 reads=[p["wT"][1], Sb_b[j]], writes=[pb1])
            v, vb_ = P_vn.get()
            k.op("dve", lambda e: e.tensor_tensor(out=v[:], in0=p["u"][0][:], in1=ps1[:, 0:128], op=ALU.subtract),
                 reads=[p["u"][1], pb1], writes=[vb_])
            vn.append((v, vb_))
        for j in range(4):
            p = pre[j]
            v, vb_ = vn[j]
            pso, pbo = k.psum()
            k.op("pe", lambda e: e.matmul(pso[:, 0:128], Sb[:, j, :], p["qd"][0][:], start=True, stop=False),
                 reads=[Sb_b[j], p["qd"][1]], writes=[pbo])
            k.op("pe", lambda e: e.matmul(pso[:, 0:128], v[:], p["qkT"][0][:], start=False, stop=True),
                 reads=[vb_, p["qkT"][1]], writes=[pbo], skip_self=True)
            psk, pbk = k.psum()
            k.op("pe", lambda e: e.matmul(psk[:, 0:128], p["kdec"][0][:], v[:], start=True, stop=True),
                 reads=[p["kdec"][1], vb_], writes=[pbk])
            k.op("dve", lambda e: e.scalar_tensor_tensor(out=S[:, j, :], in0=S[:, j, :],
                                                         scalar=p["egc"][0][:, 127:128], in1=psk[:, 0:128],
                                                         op0=ALU.mult, op1=ALU.add),
                 reads=[S_b[j], p["egc"][1], pbk], writes=[S_b[j]])
            k.op("act", lambda e: e.copy(out=Sb[:, j, :], in_=S[:, j, :]), reads=[S_b[j]], writes=[Sb_b[j]])
            o, ob = f128.get()
            k.op("act", lambda e: e.copy(out=o[:], in_=pso[:, 0:128]), reads=[pbo], writes=[ob])
            sq, sqb = f128.get()
            k.op("act", lambda e: e.activation(out=sq[:], in_=o[:], func=AF.Square), reads=[ob], writes=[sqb])
            pss, pbs = k.psum()
            k.op("pe", lambda e: e.matmul(pss[:, 0:128], ones[:], sq[:], start=True, stop=True),
                 reads=[const_b, sqb], writes=[pbs])
            k.op("dve", lambda e: e.tensor_scalar(out=sq[:], in0=pss[:, 0:128], scalar1=1.0 / 128,
                                                  scalar2=RMS_EPS, op0=ALU.mult, op1=ALU.add),
                 reads=[pbs], writes=[sqb])
            k.op("act", lambda e: e.activation(out=sq[:], in_=sq[:], func=AF.Sqrt), reads=[sqb], writes=[sqb])
            k.op("dve", lambda e: e.reciprocal(out=sq[:], in_=sq[:]), reads=[sqb], writes=[sqb])
            k.op("dve", lambda e: e.tensor_tensor(out=o[:], in0=o[:], in1=sq[:], op=ALU.mult),
                 reads=[ob, sqb], writes=[ob])
            k.op("dve", lambda e: e.scalar_tensor_tensor(out=ost[:, j, cs], in0=o[:], scalar=ng[:, 0:1],
                                                         in1=zs[:, j, cs], op0=ALU.mult, op1=ALU.mult),
                 reads=[ob, const_b, zs_b[j]], writes=[ostb])

    def program():
        consts()
        load_x(0)
        for i in range(NTILE):
            if i + 1 < NTILE:
                load_x(i + 1)
            proj_tile(i)
            ost, ostb, ods = ostage[i % 2], ost_b[i % 2], ost_ds[i % 2]
            for s in range(4):
                chunk(i, s, ost, ostb)
            k.dma("sp", ogT[:, i * 512:(i + 1) * 512].rearrange("(j p) w -> p j w", p=128), ost[:], ods,
                  reads=[ostb])

    program()
    for d in ost_ds:
        if d.total:
            nc.sync.wait_ge(d.sem, d.total)
    return nc


def gdn_consts():
    r = np.arange(128)
    U = (r[:, None] <= r[None, :]).astype(np.float32)
    PM = np.where(r[None, :] <= r[:, None], 0.0, 1e5).astype(np.float32)
    strict = (r[None, :] < r[:, None]).astype(np.float32)
    return np.ascontiguousarray(np.concatenate([U, PM, strict], axis=1))


def gdn_inputs(c, xT, w_in, conv_w, a_log, dt_bias, norm_g):
    qc = [w_in[:, (2 * c + i) * 128:(2 * c + i + 1) * 128] for i in range(2)]
    kc = [w_in[:, 2048 + (2 * c + i) * 128:2048 + (2 * c + i + 1) * 128] for i in range(2)]
    vc = [w_in[:, 4096 + (4 * c + i) * 128:4096 + (4 * c + i + 1) * 128] for i in range(4)]
    zc = [w_in[:, 8192 + (4 * c + i) * 128:8192 + (4 * c + i + 1) * 128] for i in range(4)]
    wa = w_in[:, 12288 + 4 * c:12288 + 4 * c + 4]
    wb = w_in[:, 12320 + 4 * c:12320 + 4 * c + 4]
    chans = ([(2 * c + i) * 128 for i in range(2)] + [2048 + (2 * c + i) * 128 for i in range(2)]
             + [4096 + (4 * c + i) * 128 for i in range(4)])
    cw = np.stack([conv_w[:, ch:ch + 128].T for ch in chans], axis=1)
    hv = np.concatenate([a_log[4 * c:4 * c + 4], dt_bias[4 * c:4 * c + 4]])
    return {
        "xT": xT,
        "w_fm": _lhsT_layout(np.concatenate(qc + kc + vc + zc, axis=1)),
        "w_ab": _rhs_layout(np.concatenate([wa, wb], axis=1)),
        "convw": np.ascontiguousarray(cw.reshape(128, 32)),
        "hvec": np.ascontiguousarray(np.broadcast_to(hv[None, :], (128, 8))),
        "normg": np.ascontiguousarray(norm_g.reshape(128, 1)),
        "cmask": gdn_consts(),
    }


NEG = -1.0e30
NBIS = 24
TOPK = 256


def build_dsa_stage(NB):
    nc = bass.Bass("TRN2", target_bir_lowering=False)
    k = KB(nc)
    LK = 1024 * NB
    NQ = 128 * NB
    scale = 128 ** -0.5

    def din(name, shape, dt=F32):
        return nc.dram_tensor(name, shape, dt, kind="ExternalInput").ap()

    qT = din("qT", [D, NQ], BF16)
    qiT = din("qiT", [D, NQ], BF16)
    wi = din("wi", [128, NB * 16])
    kT = din("kT", [512, LK], BF16)
    vaug = din("vaug", [LK, 4 * 129], BF16)
    kiT = din("kiT", [128, LK], BF16)
    pmask = din("pmask", [128, 896])
    trimask = din("trimask", [128, 128])
    Dn = din("Dn", [128, 2 * 16 * 128])
    cfar = din("cfar", [128, 16])
    oT = nc.dram_tensor("oT", [D, NQ], BF16, kind="ExternalOutput").ap()

    acc_ps = [(nc.alloc_psum_tensor("acc%d" % i, [128, 512], F32), Buf("acc%d" % i, excl=True)) for i in range(4)]
    k.init_psum(3)
    tps = nc.alloc_psum_tensor("tps", [128, 1024], BF16)
    tps_b = Buf("tps", excl=True)

    sb = nc.alloc_sbuf_tensor
    sc = sb("sc", [128, LK], F32)
    sc_b = Buf("sc")
    selT = sb("selT", [128, LK], BF16)
    selT_b = Buf("selT")
    qb = [sb("qb%d" % i, [128, 16, 128], BF16) for i in range(2)]
    qb_b = [Buf("qb%d" % i) for i in range(2)]
    qib = [sb("qib%d" % i, [128, 16, 128], BF16) for i in range(2)]
    qib_b = [Buf("qib%d" % i) for i in range(2)]
    q_ds = [k.dsem() for _ in range(2)]
    wi_t = sb("wi_t", [128, NB * 16], F32)
    pm_t = sb("pm_t", [128, 896], F32)
    tri_t = sb("tri_t", [128, 128], F32)
    Dn_t = sb("Dn_t", [128, 2, 16, 128], F32)
    cf_t = sb("cf_t", [128, 16], F32)
    identb = sb("identb", [128, 128], BF16)
    ones = sb("ones", [128, 128], F32)
    const_b = Buf("const")
    kis = [(sb("kis%d" % i, [128, 512], BF16), Buf("kis%d" % i), k.dsem()) for i in range(3)]
    rl = Rot(k, "rl", [128, 512], F32, 4)
    eb = Rot(k, "eb", [128, 512], BF16, 6)
    pb_ = Rot(k, "pb", [128, 512], BF16, 6)
    grp_i = [0]
    nt = Rot(k, "nt", [128, 512], F32, 2)
    KCH = 1024
    ks = [(sb("ks%d" % i, [128, KCH], BF16), Buf("ks%d" % i), k.dsem()) for i in range(3)]
    vs = [(sb("vs%d" % i, [128, KCH // 128, 129], BF16), Buf("vs%d" % i), k.dsem()) for i in range(3)]
    side = sb("side", [128, 512], F32)
    side_b = Buf("side")
    bs = sb("bs", [128, 16], F32)
    bs_b = Buf("bs")
    bs2 = sb("bs2", [128, 4], F32)
    bs2_b = Buf("bs2")
    selT2_b = Buf("selT2")
    selc = Rot(k, "selc", [128, 512], BF16, 2)
    osb = Rot(k, "osb", [128, 128], BF16, 2)
    ost = [(sb("ost%d" % i, [128, 128], BF16), Buf("ost%d" % i), k.dsem()) for i in range(4)]
    cnt = {"ki": 0, "kv": 0, "ost": 0}

    def consts():
        dc = k.dsem()
        k.dma("sp", wi_t[:], wi[:, :], dc, writes=[const_b])
        k.dma("sp", pm_t[:], pmask[:, :], dc, writes=[const_b])
        k.dma("sp", tri_t[:], trimask[:, :], dc, writes=[const_b])
        k.dma("sp", Dn_t[:].rearrange("p a b c -> p (a b c)"), Dn[:, :], dc, writes=[const_b])
        k.dma("sp", cf_t[:], cfar[:, :], dc, writes=[const_b])
        k.op("dve", lambda e: e.memset(ones[:], 1.0), writes=[const_b])
        t, tb = rl.get()
        k.op("pool", lambda e: e.memset(t[:, 0:128], 0.0), writes=[tb])
        k.op("pool", lambda e: e.affine_select(
            out=t[:, 0:128], in_=ones[:], pattern=[[-1, 128]], compare_op=ALU.is_equal,
            fill=0.0, base=0, channel_multiplier=1), reads=[const_b], writes=[tb])
        k.op("dve", lambda e: e.tensor_copy(out=identb[:], in_=t[:, 0:128]), reads=[tb], writes=[const_b])
        for dl in range(2):
            for h in range(16):
                k.op("dve", lambda e: e.tensor_scalar(out=Dn_t[:, dl, h, :], in0=Dn_t[:, dl, h, :],
                                                      scalar1=cf_t[:, h:h + 1], scalar2=None, op0=ALU.subtract),
                     reads=[const_b], writes=[const_b])

    def load_q(j):
        i = j % 2
        k.dma("sp", qb[i][:], qT[:, j * 128:(j + 1) * 128].rearrange("(h p) t -> p h t", p=128), q_ds[i],
              writes=[qb_b[i]])
        k.dma("sp", qib[i][:], qiT[:, j * 128:(j + 1) * 128].rearrange("(h p) t -> p h t", p=128), q_ds[i],
              writes=[qib_b[i]])

    def block(j):
        L = 1024 * (j + 1)
        q, q_b, qi, qi_b = qb[j % 2], qb_b[j % 2], qib[j % 2], qib_b[j % 2]
        for ch in range(L // 512):
            kt, ktb, kds = kis[cnt["ki"] % 3]
            cnt["ki"] += 1
            k.dma("sp", kt[:], kiT[:, ch * 512:(ch + 1) * 512], kds, writes=[ktb])
            scc = sc[:, ch * 512:(ch + 1) * 512]
            for h in range(16):
                ps, pb = k.psum()
                k.op("pe", lambda e: e.matmul(ps[:, 0:512], qi[:, h, :], kt[:], start=True, stop=True),
                     reads=[qi_b, ktb], writes=[pb])
                r, rb = rl.get()
                k.op("act", lambda e: e.activation(out=r[:], in_=ps[:, 0:512], func=AF.Relu), reads=[pb], writes=[rb])
                wcol = wi_t[:, j * 16 + h:j * 16 + h + 1]
                if h == 0:
                    k.op("dve", lambda e: e.tensor_scalar(out=scc, in0=r[:], scalar1=wcol, scalar2=None,
                                                          op0=ALU.mult), reads=[rb, const_b], writes=[sc_b])
                else:
                    k.op("dve", lambda e: e.scalar_tensor_tensor(out=scc, in0=r[:], scalar=wcol, in1=scc,
                                                                 op0=ALU.mult, op1=ALU.add),
                         reads=[rb, const_b, sc_b], writes=[sc_b])
        k.op("dve", lambda e: e.tensor_reduce(out=bs[:, 0:1], in_=sc[:, 0:L], axis=AX.X, op=ALU.min),
             reads=[sc_b], writes=[bs_b])
        k.op("dve", lambda e: e.tensor_scalar(out=bs[:, 0:1], in0=bs[:, 0:1], scalar1=-1.0, scalar2=None,
                                              op0=ALU.add), reads=[bs_b], writes=[bs_b])
        k.op("dve", lambda e: e.tensor_tensor(out=sc[:, 0:896], in0=sc[:, 0:896], in1=pm_t[:], op=ALU.add),
             reads=[sc_b, const_b], writes=[sc_b])
        k.op("dve", lambda e: e.tensor_tensor(out=sc[:, L - 128:L], in0=sc[:, L - 128:L], in1=tri_t[:], op=ALU.add),
             reads=[sc_b, const_b], writes=[sc_b])
        k.op("dve", lambda e: e.reduce_max(out=bs[:, 1:2], in_=sc[:, 0:L], axis=AX.X), reads=[sc_b, bs_b], writes=[bs_b])
        lo, hi, mid, cn, ge, df, nmid, sg2 = (bs[:, i:i + 1] for i in (0, 1, 2, 3, 4, 5, 6, 7))
        sg = bs2[:, 0:1]
        for it in range(NBIS):
            k.op("dve", lambda e: e.tensor_tensor(out=mid, in0=lo, in1=hi, op=ALU.add), reads=[bs_b], writes=[bs_b])
            k.op("dve", lambda e: e.tensor_scalar(out=mid, in0=mid, scalar1=0.5, scalar2=None, op0=ALU.mult),
                 reads=[bs_b], writes=[bs_b])
            LD = (L * 7 // 16) // 128 * 128
            k.op("dve", lambda e: e.tensor_scalar(out=nmid, in0=mid, scalar1=-1.0, scalar2=None, op0=ALU.mult),
                 reads=[bs_b], writes=[bs_b])
            k.op("act", lambda e: e.activation(out=selT[:, LD:L], in_=sc[:, LD:L], func=AF.Sign, bias=nmid,
                                               scale=1.0, accum_out=sg), reads=[sc_b, bs_b, selT_b], writes=[selT2_b, bs2_b])
            k.op("dve", lambda e: e.tensor_scalar(out=selT[:, 0:LD], in0=sc[:, 0:LD], scalar1=mid, scalar2=0.0,
                                                  op0=ALU.is_gt, op1=ALU.add, accum_out=cn),
                 reads=[sc_b, bs_b, selT_b], writes=[selT_b, bs_b])
            k.op("dve", lambda e: e.tensor_scalar(out=sg2, in0=sg, scalar1=float(L - LD), scalar2=0.5,
                                                  op0=ALU.add, op1=ALU.mult), reads=[bs2_b, bs_b], writes=[bs_b])
            k.op("dve", lambda e: e.tensor_tensor(out=cn, in0=cn, in1=sg2, op=ALU.add), reads=[bs_b], writes=[bs_b])
            k.op("dve", lambda e: e.tensor_scalar(out=ge, in0=cn, scalar1=float(TOPK) - 0.5, scalar2=None,
                                                  op0=ALU.is_gt), reads=[bs_b], writes=[bs_b])
            k.op("dve", lambda e: e.tensor_tensor(out=df, in0=mid, in1=lo, op=ALU.subtract), reads=[bs_b], writes=[bs_b])
            k.op("dve", lambda e: e.scalar_tensor_tensor(out=lo, in0=df, scalar=ge, in1=lo, op0=ALU.mult, op1=ALU.add),
                 reads=[bs_b], writes=[bs_b])
            k.op("dve", lambda e: e.tensor_tensor(out=df, in0=hi, in1=mid, op=ALU.subtract), reads=[bs_b], writes=[bs_b])
            k.op("dve", lambda e: e.scalar_tensor_tensor(out=hi, in0=df, scalar=ge, in1=mid, op0=ALU.mult, op1=ALU.add),
                 reads=[bs_b], writes=[bs_b])
        for ch in range(L // 512):
            s1, s1b = selc.get()
            k.op("dve", lambda e: e.tensor_scalar(out=s1[:], in0=sc[:, ch * 512:(ch + 1) * 512], scalar1=lo,
                                                  scalar2=None, op0=ALU.is_gt), reads=[sc_b, bs_b], writes=[s1b])
            for i in range(4):
                k.op("pe", lambda e: e.transpose(tps[:, i * 128:(i + 1) * 128], s1[:, i * 128:(i + 1) * 128], identb[:]),
                     reads=[s1b, const_b], writes=[tps_b], skip_self=True)
            k.op("act", lambda e: e.copy(out=selT[:, ch * 512:(ch + 1) * 512], in_=tps[:, 0:512]),
                 reads=[tps_b], writes=[selT_b, selT2_b])
        nS = L // 128
        for g in range(4):
            q4 = q[:, 4 * g:4 * g + 4, :]
            slabs = []
            for kc in range(L // KCH):
                kt, ktb, kds = ks[cnt["kv"] % 3]
                vt, vtb, vds = vs[cnt["kv"] % 3]
                cnt["kv"] += 1
                slabs.append((kc, kt, ktb, vt, vtb, kds, vds))

            def load_slab(sl):
                kc, kt, ktb, vt, vtb, kds, vds = sl
                k.dma("sp", kt[:, 0:KCH], kT[g * 128:(g + 1) * 128, kc * KCH:(kc + 1) * KCH], kds, writes=[ktb])
                k.dma("sp", vt[:, 0:KCH // 128, :],
                      vaug[kc * KCH:(kc + 1) * KCH, g * 129:(g + 1) * 129].rearrange("(s p) c -> p s c", p=128),
                      vds, writes=[vtb])

            def logits(S):
                kc, kt, ktb, vt, vtb, _, _ = slabs[S // (KCH // 128)]
                si = S % (KCH // 128)
                ps, pb = k.psum()
                k.op("pe", lambda e: e.matmul(ps[:, 0:512], kt[:, si * 128:(si + 1) * 128], q4,
                                              start=True, stop=True), reads=[ktb, q_b], writes=[pb])
                e_, e_b = eb.get()
                dlt = nS - 1 - S
                if dlt >= 2:
                    k.op("act", lambda e: e.activation(out=e_[:], in_=ps[:, 0:512], func=AF.Exp, scale=scale),
                         reads=[pb], writes=[e_b])
                else:
                    n_, n_b = nt.get()
                    k.op("dve", lambda e: e.scalar_tensor_tensor(
                        out=n_[:].rearrange("p (h t) -> p h t", h=4),
                        in0=ps[:, 0:512].rearrange("p (h t) -> p h t", h=4), scalar=scale,
                        in1=Dn_t[:, dlt, 4 * g:4 * g + 4, :], op0=ALU.mult, op1=ALU.add),
                        reads=[pb, const_b], writes=[n_b])
                    k.op("act", lambda e: e.activation(out=e_[:], in_=n_[:], func=AF.Exp),
                         reads=[n_b], writes=[e_b])
                p_, p_b = pb_.get()
                grp_i[0] += 1
                k.op("pool" if grp_i[0] % 4 == 0 else "dve", lambda e: e.tensor_tensor(
                    out=p_[:].rearrange("p (h t) -> p h t", h=4), in0=e_[:].rearrange("p (h t) -> p h t", h=4),
                    in1=selT[:, S * 128:(S + 1) * 128].unsqueeze(1).to_broadcast([128, 4, 128]), op=ALU.mult),
                    reads=[e_b, selT_b], writes=[p_b])
                return p_, p_b

            def pv(S, p_, p_b):
                kc, kt, ktb, vt, vtb, _, _ = slabs[S // (KCH // 128)]
                si = S % (KCH // 128)
                for hh in range(4):
                    ops_, opb = acc_ps[hh]
                    k.op("pe", lambda e: e.matmul(ops_[:, 0:129], p_[:, hh * 128:(hh + 1) * 128], vt[:, si, :],
                                                  start=(S == 0), stop=(S == nS - 1)),
                         reads=[p_b, vtb], writes=[opb], skip_self=True)

            LOOK = 2
            pend = []
            nslab = 0
            for S in range(nS):
                need = S // (KCH // 128)
                while nslab <= min(need + 1, len(slabs) - 1):
                    if nslab - 2 >= 0 and nslab - 2 > (S - LOOK - 1) // (KCH // 128):
                        break
                    load_slab(slabs[nslab])
                    nslab += 1
                pend.append((S,) + logits(S))
                if len(pend) > LOOK:
                    pv(*pend.pop(0))
            while pend:
                pv(*pend.pop(0))
            for hh in range(4):
                h = 4 * g + hh
                ops_, opb = acc_ps[hh]
                k.op("dve", lambda e: e.reciprocal(out=bs[:, 8 + hh:9 + hh], in_=ops_[:, 128:129]),
                     reads=[opb, bs_b], writes=[bs_b])
                o_, o_b = osb.get()
                k.op("act", lambda e: e.activation(out=o_[:], in_=ops_[:, 0:128], func=AF.Identity,
                                                   scale=bs[:, 8 + hh:9 + hh]), reads=[opb, bs_b], writes=[o_b])
                k.op("pe", lambda e: e.transpose(tps[:, 512 + hh * 128:512 + (hh + 1) * 128], o_[:], identb[:]),
                     reads=[o_b, const_b], writes=[tps_b])
                st_, st_b, st_ds = ost[cnt["ost"] % 4]
                cnt["ost"] += 1
                k.op("act", lambda e: e.copy(out=st_[:], in_=tps[:, 512 + hh * 128:512 + (hh + 1) * 128]),
                     reads=[tps_b], writes=[st_b])
                k.dma("sp", oT[h * 128:(h + 1) * 128, j * 128:(j + 1) * 128], st_[:], st_ds, reads=[st_b])

    consts()
    load_q(0)
    for j in range(NB):
        if j + 1 < NB:
            load_q(j + 1)
        block(j)
    for _, _, d in ost:
        if d.total:
            nc.sync.wait_ge(d.sem, d.total)
    return nc


def rel_bucket_np(d):
    d = np.maximum(d, 0)
    df = np.maximum(d, 1).astype(np.float32)
    large = 16 + (np.log(df / np.float32(16)) / np.float32(math.log(128 / 16)) * np.float32(16)).astype(np.int32)
    large = np.minimum(large, 31)
    return np.where(d < 16, d, large)


def dsa_inputs(c, NB, qT, qiT, wi, kT, vaug, kiT, rel_bias):
    SEQL = 1024 * NB
    blocks = [8 * j + c for j in range(NB)]
    cols = np.concatenate([np.arange(b * 128, (b + 1) * 128) for b in blocks])
    sh = 128 * (7 - c)

    def shift_cols(a):
        out = np.zeros_like(a)
        out[:, sh:] = a[:, :SEQL - sh]
        return out

    def shift_rows(a):
        out = np.zeros_like(a)
        out[sh:] = a[:SEQL - sh]
        return out
    pm = np.zeros((128, 896), np.float32)
    pm[:, :sh] = NEG
    r = np.arange(128)
    tri = np.where(r[None, :] <= r[:, None], 0.0, NEG).astype(np.float32)
    dn = np.zeros((128, 2, 16, 128), np.float32)
    for dl in range(2):
        dist = 128 * dl + r[None, :] - r[:, None]
        dn[:, dl, :, :] = rel_bias[rel_bucket_np(dist)].transpose(0, 2, 1)
    wic = wi[cols].reshape(NB, 128, 16).transpose(1, 0, 2).reshape(128, NB * 16)
    return {
        "qT": np.ascontiguousarray(qT[:, cols]), "qiT": np.ascontiguousarray(qiT[:, cols]),
        "wi": np.ascontiguousarray(wic),
        "kT": shift_cols(kT), "vaug": shift_rows(vaug), "kiT": shift_cols(kiT),
        "pmask": pm, "trimask": tri, "Dn": np.ascontiguousarray(dn.reshape(128, -1)),
        "cfar": np.ascontiguousarray(np.broadcast_to(rel_bias[31][None, :], (128, 16))),
    }


_NC_CACHE = {}


def _get_nc(key, fn):
    if key not in _NC_CACHE:
        _NC_CACHE[key] = fn()
    return _NC_CACHE[key]


def _run(nc, in_maps):
    res = run_bass_kernel_spmd(nc, in_maps, core_ids=list(range(NCORE)))
    return res.results


def _token_inputs(li, w_o, ln1_g, ln1_b, ffn_w_gate, ffn_w_up, ffn_conv_w, ffn_w_down, ln2_g, ln2_b,
                  ple_w_proj, ple_w_gate):
    return {
        "w_o": _lhsT_layout(w_o), "w_gate": _lhsT_layout(ffn_w_gate[li]), "w_up": _lhsT_layout(ffn_w_up[li]),
        "w_down": _lhsT_layout(ffn_w_down[li]), "w_pg": _lhsT_layout(ple_w_gate[li]),
        "w_pp": _lhsT_layout(ple_w_proj[li]),
        "vecs": np.ascontiguousarray(np.concatenate([_pvec(ln1_g[li]), _pvec(ln1_b[li]), _pvec(ln2_g[li]),
                                                     _pvec(ln2_b[li])], axis=1)),
        "convw": np.ascontiguousarray(ffn_conv_w[li].reshape(3, FC, 128).transpose(2, 1, 0).reshape(128, FC * 3)),
    }


def _seg(aT, c, nt=2048):
    out = np.zeros((aT.shape[0], nt + 2), aT.dtype)
    if c == 0:
        out[:, 2:] = aT[:, 0:nt]
    else:
        out[:] = aT[:, nt * c - 2:nt * (c + 1)]
    return out


def kernel(x, p, gdn_w_in, gdn_conv_w, gdn_a_log, gdn_dt_bias, gdn_norm_g, gdn_w_o,
           dsa_w_in, dsa_kidx_ln_g, dsa_kidx_ln_b, dsa_w_o, rel_bias,
           ln1_g, ln1_b, ffn_w_gate, ffn_w_up, ffn_conv_w, ffn_w_down, ln2_g, ln2_b,
           ple_w_proj, ple_w_gate):
    import sys
    import time
    t00 = time.time()

    def log(msg):
        print("[kernel %.1fs] %s" % (time.time() - t00, msg), file=sys.stderr, flush=True)
    f32 = lambda a: np.ascontiguousarray(np.asarray(a, dtype=np.float32))
    x = f32(x)
    p = f32(p)
    xT = np.ascontiguousarray(x[0].T)
    pT = [np.ascontiguousarray(p[i, 0].T) for i in range(2)]
    NT = SEQ // NCORE

    nc_a = _get_nc("gdn", lambda: build_gdn_stage(SEQ))
    log("gdn built")
    w_in = f32(gdn_w_in[0])
    ins = [gdn_inputs(c, xT, w_in, f32(gdn_conv_w[0]), f32(gdn_a_log[0]), f32(gdn_dt_bias[0]), f32(gdn_norm_g[0]))
           for c in range(NCORE)]
    ra = _run(nc_a, ins)
    log("gdn done")
    ogT = np.concatenate([np.asarray(r["ogT"]) for r in ra], axis=0)
    del ins, ra

    nc_b = _get_nc("tok0", lambda: build_token_stage(32, NT // 512, True))
    log("tok0 built")
    wd = f32(dsa_w_in[0])
    shared = _token_inputs(0, f32(gdn_w_o[0]), f32(ln1_g), f32(ln1_b), f32(ffn_w_gate), f32(ffn_w_up),
                           f32(ffn_conv_w), f32(ffn_w_down), f32(ln2_g), f32(ln2_b), f32(ple_w_proj), f32(ple_w_gate))
    shared["w_fm"] = _lhsT_layout(np.concatenate([wd[:, 0:2048], wd[:, 2048:2560], wd[:, 3072:5120]], axis=1))
    shared["w_v"] = np.ascontiguousarray(_rhs_layout(wd[:, 2560:3072]).reshape(128, 2, 8 * 512).transpose(1, 0, 2))
    shared["w_kw"] = _rhs_layout(wd[:, 5120:5264])
    shared["kln"] = np.ascontiguousarray(np.broadcast_to(
        np.concatenate([f32(dsa_kidx_ln_g[0]), f32(dsa_kidx_ln_b[0])])[None, :], (128, 256)))
    ins = []
    for c in range(NCORE):
        d = dict(shared)
        d["xT"] = _seg(xT, c)
        d["mixT"] = _seg(ogT, c)
        d["pT"] = _seg(pT[0], c)
        d["hscale"] = np.full((128, 1), 0.0 if c == 0 else 1.0, np.float32)
        ins.append(d)
    rb = _run(nc_b, ins)
    log("tok0 done")
    cat = lambda name, ax: np.concatenate([np.asarray(r[name]) for r in rb], axis=ax)
    x1T = cat("outT", 1)
    qT = cat("qT_o", 1)
    kT = cat("kT_o", 1)
    qiT = cat("qiT_o", 1)
    vaug = cat("v_o", 0).reshape(SEQ, 516)
    kiT = cat("kiT_o", 1)
    wi = cat("wi_o", 0)
    del ins, rb, ogT, shared

    NB = SEQ // 1024
    nc_c = _get_nc("dsa", lambda: build_dsa_stage(NB))
    log("dsa built")
    rb_ = f32(rel_bias)
    ins = [dsa_inputs(c, NB, qT, qiT, wi, kT, vaug, kiT, rb_) for c in range(NCORE)]
    rc = _run(nc_c, ins)
    log("dsa done")
    oT = np.zeros((D, SEQ), NPBF)
    for c in range(NCORE):
        oc = np.asarray(rc[c]["oT"])
        for j in range(NB):
            b = 8 * j + c
            oT[:, b * 128:(b + 1) * 128] = oc[:, j * 128:(j + 1) * 128]
    del ins, rc, qT, qiT, kT, vaug, kiT, wi

    nc_d = _get_nc("tok1", lambda: build_token_stage(16, NT // 512, False))
    log("tok1 built")
    shared = _token_inputs(1, f32(dsa_w_o[0]), f32(ln1_g), f32(ln1_b), f32(ffn_w_gate), f32(ffn_w_up),
                           f32(ffn_conv_w), f32(ffn_w_down), f32(ln2_g), f32(ln2_b), f32(ple_w_proj), f32(ple_w_gate))
    ins = []
    for c in range(NCORE):
        d = dict(shared)
        d["xT"] = _seg(x1T, c)
        d["mixT"] = _seg(oT, c)
        d["pT"] = _seg(pT[1], c)
        d["hscale"] = np.full((128, 1), 0.0 if c == 0 else 1.0, np.float32)
        ins.append(d)
    rd = _run(nc_d, ins)
    log("tok1 done")
    outT = np.concatenate([np.asarray(r["outT"]) for r in rd], axis=1)
    return np.ascontiguousarray(outT.T).reshape(1, SEQ, D).astype(np.float32)
```

```python
import math
import numpy as np
import ml_dtypes
import concourse.bass as bass
import concourse.mybir as mybir
from concourse.bass_utils import run_bass_kernel_spmd

F32 = mybir.dt.float32
BF16 = mybir.dt.bfloat16
AF = mybir.ActivationFunctionType
ALU = mybir.AluOpType
AX = mybir.AxisListType
NPBF = ml_dtypes.bfloat16

D = 2048
DC = D // 128
SEQ = 16384
NCORE = 8
DFF = 5120
FC = DFF // 128
PLE = 256
ALPHA = 4.0 ** 0.25
LN_EPS = 1e-5
RMS_EPS = 1e-6
DEBUG = {}


class Buf:
    __slots__ = ("name", "w", "r", "excl")

    def __init__(self, name="", excl=False):
        self.name = name
        self.w = None
        self.r = []
        self.excl = excl


class DSem:
    def __init__(self, sem, key):
        self.sem = sem
        self.key = key
        self.total = 0


class KB:
    def __init__(self, nc):
        self.nc = nc
        self.eng = {"pe": nc.tensor, "act": nc.scalar, "dve": nc.vector,
                    "pool": nc.gpsimd, "sp": nc.sync}
        self.sem = {e: nc.alloc_semaphore("cs_" + e) for e in self.eng}
        self.cnt = {e: 0 for e in self.eng}
        self.seen = {e: {} for e in self.eng}
        self.semobj = dict(self.sem)
        self.nds = 0
        self.dry = False
        self.psum_tiles = None
        self.psum_i = 0

    def dsem(self):
        key = "ds%d" % self.nds
        s = self.nc.alloc_semaphore(key)
        self.nds += 1
        d = DSem(s, key)
        self.semobj[key] = s
        return d

    def _waits(self, e, reads, writes, skip_self=False):
        deps = {}

        def add(d):
            if d is None:
                return
            kk, v = d
            if deps.get(kk, 0) < v:
                deps[kk] = v
        for b in reads:
            add(b.w)
        for b in writes:
            add(b.w)
            for d in b.r:
                add(d)
        seen = self.seen[e]
        for kk, v in deps.items():
            if skip_self and kk == e:
                continue
            if seen.get(kk, 0) >= v:
                continue
            self.eng[e].wait_ge(self.semobj[kk], v)
            seen[kk] = v

    def op(self, e, fn, reads=(), writes=(), skip_self=False):
        if self.dry:
            return None
        ex = [b for b in reads if b.excl]
        if ex:
            writes = list(writes) + ex
        self._waits(e, reads, writes, skip_self=skip_self)
        ins = fn(self.eng[e])
        self.cnt[e] += 1
        ins.then_inc(self.sem[e], 1)
        tag = (e, self.cnt[e])
        for b in reads:
            b.r.append(tag)
            if len(b.r) > 64:
                b.r = _prune(b.r)
        for b in writes:
            b.w = tag
            b.r = []
        return ins

    def dma(self, q, out, in_, ds, reads=(), writes=()):
        if self.dry:
            return None
        self._waits(q, reads, writes)
        ins = self.eng[q].dma_start(out=out, in_=in_)
        ds.total += 16
        ins.then_inc(ds.sem, 16)
        tag = (ds.key, ds.total)
        for b in reads:
            b.r.append(tag)
        for b in writes:
            b.w = tag
            b.r = []
        return ins

    def wait_all(self, e, bufs):
        if self.dry:
            return
        self._waits(e, [], bufs)

    def init_psum(self, n=8):
        self.npsum = n
        self.psum_tiles = []
        for i in range(n):
            t = self.nc.alloc_psum_tensor("psb%d" % i, [128, 512], F32)
            self.psum_tiles.append((t, Buf("ps%d" % i, excl=True)))

    def psum(self):
        t = self.psum_tiles[self.psum_i % self.npsum]
        self.psum_i += 1
        return t


def _prune(r):
    best = {}
    for kk, v in r:
        if best.get(kk, 0) < v:
            best[kk] = v
    return list(best.items())


class WStream:
    def __init__(self, k, nslots, slot_elems, name="w"):
        self.k = k
        self.n = nslots
        self.slots = []
        for i in range(nslots):
            t = k.nc.alloc_sbuf_tensor("%s_slot%d" % (name, i), [128, slot_elems], BF16)
            self.slots.append((t, Buf("%s%d" % (name, i)), k.dsem()))
        self.reqs = []
        self.issued = 0
        self.consumed = 0
        self.depth = nslots - 1

    def reset(self):
        self.issued = 0
        self.consumed = 0

    def _issue(self, i):
        ap = self.reqs[i]
        t, b, ds = self.slots[i % self.n]
        n = ap.shape[1]
        self.k.dma("pool", t[:, 0:n], ap, ds, writes=[b])

    def next(self, ap, hold=0):
        i = self.consumed
        self.consumed += 1
        t, b, ds = self.slots[i % self.n]
        if self.k.dry:
            self.reqs.append(ap)
            return t, b
        while self.issued < len(self.reqs) and self.issued <= i + self.depth - hold:
            self._issue(self.issued)
            self.issued += 1
        return t, b


def build_token_stage(kmc, ntiles, with_dsa_proj):
    nc = bass.Bass("TRN2", target_bir_lowering=False)
    k = KB(nc)
    k.init_psum(7)
    tps = nc.alloc_psum_tensor("tps", [128, 1024], BF16)
    tps_b = Buf("tps", excl=True)
    NT = 2 + 512 * ntiles

    def din(name, shape, dt=F32):
        return nc.dram_tensor(name, shape, dt, kind="ExternalInput").ap()

    def dout(name, shape, dt=F32):
        return nc.dram_tensor(name, shape, dt, kind="ExternalOutput").ap()

    xT = din("xT", [D, NT])
    mixT = din("mixT", [kmc * 128, NT], BF16)
    pT = din("pT", [PLE, NT])
    w_o = din("w_o", [DC, 128, kmc * 128])
    w_gate = din("w_gate", [FC, 128, DC * 128])
    w_up = din("w_up", [FC, 128, DC * 128])
    w_down = din("w_down", [DC, 128, FC * 128])
    w_pg = din("w_pg", [DC, 128, DC * 128])
    w_pp = din("w_pp", [DC, 128, 2 * 128])
    vecs = din("vecs", [128, 4 * DC])
    convw = din("convw", [128, FC * 3])
    hscale = din("hscale", [128, 1])
    if with_dsa_proj:
        w_fm = din("w_fm", [36, 128, DC * 128])
        w_v = din("w_v", [2, 128, 8 * 512])
        w_kw = din("w_kw", [128, DC * 144])
        kln = din("kln", [128, 256])
        qT_o = dout("qT_o", [D, NT - 2], BF16)
        kT_o = dout("kT_o", [512, NT - 2], BF16)
        qiT_o = dout("qiT_o", [D, NT - 2], BF16)
        v_o = dout("v_o", [NT - 2, 4, 129], BF16)
        kiT_o = dout("kiT_o", [128, NT - 2], BF16)
        wi_o = dout("wi_o", [NT - 2, 16])
    outT = dout("outT", [D, NT - 2])

    sb = nc.alloc_sbuf_tensor
    xz = sb("xz", [128, DC, 512], F32)
    xz_b = [Buf("xz%d" % i) for i in range(DC)]
    xb = sb("xb", [128, DC, 512], BF16)
    xb_b = [Buf("xb%d" % i) for i in range(DC)]
    big = sb("big", [128, FC, 512], BF16)
    big_b = [Buf("big%d" % i) for i in range(FC)]
    ptile = sb("ptile", [128, 2, 512], BF16)
    pt_b = Buf("pt")
    ones = sb("ones", [128, 128], F32)
    ones_b = Buf("ones")
    vec_t = sb("vec_t", [128, 4 * DC], F32)
    cw_t = sb("cw_t", [128, FC * 3], F32)
    hs_t = sb("hs_t", [128, 1], F32)
    const_b = Buf("const")
    haloG = sb("haloG", [128, FC, 2], F32)
    halo_b = [Buf("halo%d" % i) for i in range(FC)]
    G = [sb("G%d" % i, [128, 516], F32) for i in range(2)]
    G_b = [Buf("G%d" % i) for i in range(2)]
    tmp = [sb("tmp%d" % i, [128, 512], F32) for i in range(4)]
    tmp_b = [Buf("tmp%d" % i) for i in range(4)]
    st = [sb("st%d" % i, [128, 512], F32) for i in range(4)]
    st_b = [Buf("st%d" % i) for i in range(4)]
    ws = WStream(k, 4, FC * 128)
    ld = [k.dsem() for _ in range(4)]
    st_ds = [k.dsem() for _ in range(4)]
    xz_ds = [k.dsem() for _ in range(DC)]
    stage_o = [sb("sto%d" % i, [128, 516], BF16) for i in range(4)]
    stage_b = [Buf("sto%d" % i) for i in range(4)]
    stage_i = [0]
    if with_dsa_proj:
        kln_t = sb("kln_t", [128, 256], F32)
        identb = sb("identb", [128, 128], BF16)
        ident_b = Buf("ident")
        kif = sb("kif", [128, 160], F32)
        kif_b = Buf("kif")
        kib = sb("kib", [128, 128], BF16)
        kib_b = Buf("kib")
        sm = sb("sm", [128, 16], F32)
        sm_b = Buf("sm")
        wist = [sb("wist%d" % i, [128, 16], F32) for i in range(2)]
        wist_b = [Buf("wist%d" % i) for i in range(2)]
        wist_ds = [k.dsem() for _ in range(2)]

    tmp_i = [0]

    def gettmp():
        i = tmp_i[0] % 4
        tmp_i[0] += 1
        return tmp[i], tmp_b[i]

    def consts():
        dc = k.dsem()
        k.dma("sp", vec_t[:], vecs[:, :], dc, writes=[const_b])
        k.dma("sp", cw_t[:], convw[:, :], dc, writes=[const_b])
        k.dma("sp", hs_t[:], hscale[:, :], dc, writes=[const_b])
        if with_dsa_proj:
            k.dma("sp", kln_t[:], kln[:, :], dc, writes=[const_b])
        k.op("dve", lambda e: e.memset(ones[:], 1.0), writes=[ones_b])
        if with_dsa_proj:
            k.op("pool", lambda e: e.memset(tmp[0][:, 0:128], 0.0), writes=[tmp_b[0]])
            k.op("pool", lambda e: e.affine_select(
                out=tmp[0][:, 0:128], in_=ones[:], pattern=[[-1, 128]], compare_op=ALU.is_equal,
                fill=0.0, base=0, channel_multiplier=1), reads=[ones_b], writes=[tmp_b[0]])
            k.op("dve", lambda e: e.tensor_copy(out=identb[:], in_=tmp[0][:, 0:128]),
                 reads=[tmp_b[0]], writes=[ident_b])

    def layer_norm(W, gi):
        ps1, pb1 = k.psum()
        ps2, pb2 = k.psum()
        for m in range(DC):
            sq, sqb = gettmp()
            k.op("act", lambda e: e.activation(out=sq[:, 0:W], in_=xz[:, m, 0:W], func=AF.Square),
                 reads=[xz_b[m]], writes=[sqb])
            k.op("pe", lambda e: e.matmul(ps1[:, 0:W], ones[:], xz[:, m, 0:W],
                                          start=(m == 0), stop=(m == DC - 1)),
                 reads=[ones_b, xz_b[m]], writes=[pb1], skip_self=True)
            k.op("pe", lambda e: e.matmul(ps2[:, 0:W], ones[:], sq[:, 0:W],
                                          start=(m == 0), stop=(m == DC - 1)),
                 reads=[ones_b, sqb], writes=[pb2], skip_self=True)
        mean, rstd, nmr, scr = st
        k.op("dve", lambda e: e.tensor_scalar(out=mean[:, 0:W], in0=ps1[:, 0:W], scalar1=1.0 / D,
                                              scalar2=None, op0=ALU.mult),
             reads=[pb1], writes=[st_b[0]])
        k.op("dve", lambda e: e.tensor_tensor(out=scr[:, 0:W], in0=mean[:, 0:W], in1=mean[:, 0:W],
                                              op=ALU.mult), reads=[st_b[0]], writes=[st_b[3]])
        k.op("dve", lambda e: e.scalar_tensor_tensor(out=scr[:, 0:W], in0=ps2[:, 0:W], scalar=1.0 / D,
                                                     in1=scr[:, 0:W], op0=ALU.mult, op1=ALU.subtract),
             reads=[pb2, st_b[3]], writes=[st_b[3]])
        k.op("dve", lambda e: e.tensor_scalar(out=scr[:, 0:W], in0=scr[:, 0:W], scalar1=0.0,
                                              scalar2=LN_EPS, op0=ALU.max, op1=ALU.add),
             reads=[st_b[3]], writes=[st_b[3]])
        k.op("act", lambda e: e.activation(out=scr[:, 0:W], in_=scr[:, 0:W], func=AF.Sqrt),
             reads=[st_b[3]], writes=[st_b[3]])
        k.op("dve", lambda e: e.reciprocal(out=rstd[:, 0:W], in_=scr[:, 0:W]),
             reads=[st_b[3]], writes=[st_b[1]])
        k.op("dve", lambda e: e.scalar_tensor_tensor(out=nmr[:, 0:W], in0=mean[:, 0:W], scalar=-1.0,
                                                     in1=rstd[:, 0:W], op0=ALU.mult, op1=ALU.mult),
             reads=[st_b[0], st_b[1]], writes=[st_b[2]])
        for m in range(DC):
            t, tb = gettmp()
            k.op("dve", lambda e: e.tensor_tensor(out=t[:, 0:W], in0=xz[:, m, 0:W], in1=rstd[:, 0:W],
                                                  op=ALU.mult), reads=[xz_b[m], st_b[1]], writes=[tb])
            k.op("dve", lambda e: e.tensor_tensor(out=t[:, 0:W], in0=t[:, 0:W], in1=nmr[:, 0:W],
                                                  op=ALU.add), reads=[tb, st_b[2]], writes=[tb])
            k.op("act", lambda e: e.activation(out=xz[:, m, 0:W], in_=t[:, 0:W], func=AF.Identity,
                                               bias=vec_t[:, (gi + 1) * DC + m:(gi + 1) * DC + m + 1],
                                               scale=vec_t[:, gi * DC + m:gi * DC + m + 1]),
                 reads=[tb, const_b], writes=[xz_b[m]])
            k.op("pool", lambda e: e.tensor_copy(out=xb[:, m, 0:W], in_=xz[:, m, 0:W]),
                 reads=[xz_b[m]], writes=[xb_b[m]])

    def mm_acc(ps, pb, W, wt, wb, nk, rhs_of, rhs_bufs):
        for kc in range(nk):
            k.op("pe", lambda e: e.matmul(ps[:, 0:W], wt[:, kc * 128:(kc + 1) * 128], rhs_of(kc),
                                          start=(kc == 0), stop=(kc == nk - 1)),
                 reads=[wb, rhs_bufs[kc]], writes=[pb], skip_self=True)

    def tile_pass(c0, W, halo_only):
        k.dma("sp", xz[:, :, 0:W], xT[:, c0:c0 + W].rearrange("(m p) w -> p m w", p=128), ld[0],
              writes=xz_b)
        k.dma("sp", big[:, 0:kmc, 0:W], mixT[:, c0:c0 + W].rearrange("(m p) w -> p m w", p=128), ld[1],
              writes=big_b[0:kmc])
        for m in range(DC):
            wt, wb = ws.next(w_o[m])
            ps, pb = k.psum()
            mm_acc(ps, pb, W, wt, wb, kmc, lambda kc: big[:, kc, 0:W], big_b)
            k.op("dve", lambda e: e.scalar_tensor_tensor(out=xz[:, m, 0:W], in0=xz[:, m, 0:W],
                                                         scalar=ALPHA, in1=ps[:, 0:W],
                                                         op0=ALU.mult, op1=ALU.add),
                 reads=[xz_b[m], pb], writes=[xz_b[m]])
        layer_norm(W, 0)
        for j in range(FC):
            wg, wgb = ws.next(w_gate[j])
            psg, pbg = k.psum()
            mm_acc(psg, pbg, W, wg, wgb, DC, lambda kc: xb[:, kc, 0:W], xb_b)
            if halo_only:
                k.op("dve", lambda e: e.tensor_scalar(out=haloG[:, j, :], in0=psg[:, 0:2],
                                                      scalar1=hs_t[:, 0:1], scalar2=None, op0=ALU.mult),
                     reads=[pbg, const_b], writes=[halo_b[j]])
                continue
            wu, wub = ws.next(w_up[j])
            psu, pbu = k.psum()
            mm_acc(psu, pbu, W, wu, wub, DC, lambda kc: xb[:, kc, 0:W], xb_b)
            g, gb = G[j % 2], G_b[j % 2]
            k.op("pool", lambda e: e.tensor_copy(out=g[:, 0:2], in_=haloG[:, j, :]),
                 reads=[halo_b[j]], writes=[gb])
            k.op("act", lambda e: e.copy(out=g[:, 2:2 + W], in_=psg[:, 0:W]), reads=[pbg], writes=[gb])
            k.op("pool", lambda e: e.tensor_copy(out=haloG[:, j, :], in_=g[:, W:W + 2]),
                 reads=[gb], writes=[halo_b[j]])
            c, cb = gettmp()
            k.op("dve", lambda e: e.tensor_scalar(out=c[:, 0:W], in0=g[:, 0:W],
                                                  scalar1=cw_t[:, 3 * j:3 * j + 1], scalar2=None,
                                                  op0=ALU.mult), reads=[gb, const_b], writes=[cb])
            k.op("dve", lambda e: e.scalar_tensor_tensor(out=c[:, 0:W], in0=g[:, 1:1 + W],
                                                         scalar=cw_t[:, 3 * j + 1:3 * j + 2],
                                                         in1=c[:, 0:W], op0=ALU.mult, op1=ALU.add),
                 reads=[gb, cb, const_b], writes=[cb])
            k.op("dve", lambda e: e.scalar_tensor_tensor(out=c[:, 0:W], in0=g[:, 2:2 + W],
                                                         scalar=cw_t[:, 3 * j + 2:3 * j + 3],
                                                         in1=c[:, 0:W], op0=ALU.mult, op1=ALU.add),
                 reads=[gb, cb, const_b], writes=[cb])
            k.op("act", lambda e: e.activation(out=c[:, 0:W], in_=c[:, 0:W], func=AF.Silu),
                 reads=[cb], writes=[cb])
            k.op("dve", lambda e: e.tensor_tensor(out=big[:, j, 0:W], in0=c[:, 0:W], in1=psu[:, 0:W],
                                                  op=ALU.mult), reads=[cb, pbu], writes=[big_b[j]])
        if halo_only:
            return
        for m in range(DC):
            wt, wb = ws.next(w_down[m])
            ps, pb = k.psum()
            mm_acc(ps, pb, W, wt, wb, FC, lambda kc: big[:, kc, 0:W], big_b)
            k.op("dve", lambda e: e.scalar_tensor_tensor(out=xz[:, m, 0:W], in0=xz[:, m, 0:W],
                                                         scalar=ALPHA, in1=ps[:, 0:W],
                                                         op0=ALU.mult, op1=ALU.add),
                 reads=[xz_b[m], pb], writes=[xz_b[m]])
        layer_norm(W, 2)
        k.dma("pool", ptile[:, :, 0:W], pT[:, c0:c0 + W].rearrange("(m p) w -> p m w", p=128), ld[2],
              writes=[pt_b])
        for m in range(DC):
            wt, wb = ws.next(w_pg[m])
            psa, pba = k.psum()
            mm_acc(psa, pba, W, wt, wb, DC, lambda kc: xb[:, kc, 0:W], xb_b)
            wt2, wb2 = ws.next(w_pp[m], hold=1)
            psb, pbb = k.psum()
            mm_acc(psb, pbb, W, wt2, wb2, 2, lambda kc: ptile[:, kc, 0:W], [pt_b, pt_b])
            t, tb = gettmp()
            k.op("act", lambda e: e.activation(out=t[:, 0:W], in_=psa[:, 0:W], func=AF.Sigmoid),
                 reads=[pba], writes=[tb])
            k.op("dve", lambda e: e.tensor_tensor(out=t[:, 0:W], in0=t[:, 0:W], in1=psb[:, 0:W],
                                                  op=ALU.mult), reads=[tb, pbb], writes=[tb])
            k.op("dve", lambda e: e.tensor_tensor(out=xz[:, m, 0:W], in0=xz[:, m, 0:W], in1=t[:, 0:W],
                                                  op=ALU.add), reads=[xz_b[m], tb], writes=[xz_b[m]])
            k.dma("sp", outT[m * 128:(m + 1) * 128, c0 - 2:c0 - 2 + W], xz[:, m, 0:W], xz_ds[m],
                  reads=[xz_b[m]])
            if with_dsa_proj:
                k.op("pool", lambda e: e.tensor_copy(out=big[:, m, 0:W], in_=xz[:, m, 0:W]),
                     reads=[xz_b[m]], writes=[big_b[m]])
        if with_dsa_proj:
            dsa_proj(c0 - 2, W)

    def getstage():
        i = stage_i[0] % 4
        stage_i[0] += 1
        return stage_o[i], stage_b[i], st_ds[i]

    def dsa_proj(t0, W):
        outs = [(qT_o, 16), (kT_o, 4), (qiT_o, 16)]
        ci = 0
        for dst, nch in (outs if DEBUG.get("fm", True) else []):
            for c in range(nch):
                wt, wb = ws.next(w_fm[ci])
                ci += 1
                ps, pb = k.psum()
                mm_acc(ps, pb, W, wt, wb, DC, lambda kc: big[:, kc, 0:W], big_b)
                so, sob, sds = getstage()
                k.op("act", lambda e: e.copy(out=so[:, 0:W], in_=ps[:, 0:W]), reads=[pb], writes=[sob])
                k.dma("sp", dst[c * 128:(c + 1) * 128, t0:t0 + W], so[:, 0:W], sds, reads=[sob])
        wv0, wvb0 = ws.next(w_v[0])
        wv1, wvb1 = ws.next(w_v[1], hold=1)
        wkw, wkwb = ws.next(w_kw, hold=2)
        for s in range(W // 128 if DEBUG.get('tm', True) else 0):
            sl = slice(s * 128, (s + 1) * 128)
            ps, pb = k.psum()
            for kc in range(DC):
                wt, wb = (wv0, wvb0) if kc < 8 else (wv1, wvb1)
                k.op("pe", lambda e: e.matmul(ps[:, 0:512], big[:, kc, sl],
                                              wt[:, (kc % 8) * 512:(kc % 8 + 1) * 512],
                                              start=(kc == 0), stop=(kc == DC - 1)),
                     reads=[wb, big_b[kc]], writes=[pb], skip_self=True)
            so, sob, sds = getstage()
            k.op("pool", lambda e: e.memset(so[:, 0:516], 1.0), writes=[sob])
            sov = so[:, 0:516].rearrange("p (g c) -> p g c", c=129)
            k.op("act", lambda e: e.copy(out=sov[:, :, 0:128],
                                         in_=ps[:, 0:512].rearrange("p (g c) -> p g c", c=128)),
                 reads=[pb], writes=[sob])
            k.dma("sp", v_o[t0 + s * 128:t0 + (s + 1) * 128, :, :], sov, sds, reads=[sob])
            if not DEBUG.get("kw", True):
                continue
            ps2, pb2 = k.psum()
            for kc in range(DC):
                k.op("pe", lambda e: e.matmul(ps2[:, 0:144], big[:, kc, sl], wkw[:, kc * 144:(kc + 1) * 144],
                                              start=(kc == 0), stop=(kc == DC - 1)),
                     reads=[wkwb, big_b[kc]], writes=[pb2], skip_self=True)
            wi_t, wi_b, wi_ds = wist[s % 2], wist_b[s % 2], wist_ds[s % 2]
            k.op("dve", lambda e: e.tensor_scalar(out=wi_t[:], in0=ps2[:, 128:144],
                                                  scalar1=(16 ** -0.5) * (128 ** -0.5), scalar2=None,
                                                  op0=ALU.mult), reads=[pb2], writes=[wi_b])
            k.dma("sp", wi_o[t0 + s * 128:t0 + (s + 1) * 128, :], wi_t[:], wi_ds, reads=[wi_b])
            if not DEBUG.get("kiln", True):
                continue
            _n = [0]
            _lim = DEBUG.get("kiln_n", 99)
            def kop(*a_, **kw_):
                _n[0] += 1
                if _n[0] <= _lim:
                    k.op(*a_, **kw_)
            kop("dve", lambda e: e.memset(sm[:], 0.0), writes=[sm_b])
            kop("act", lambda e: e.copy(out=kif[:, 0:128], in_=ps2[:, 0:128]), reads=[pb2], writes=[kif_b])
            kop("dve", lambda e: e.reduce_sum(out=sm[:, 0:1], in_=kif[:, 0:128], axis=AX.X),
                 reads=[kif_b, sm_b], writes=[sm_b])
            t, tb = gettmp()
            kop("act", lambda e: e.activation(out=t[:, 0:128], in_=ps2[:, 0:128], func=AF.Square,
                                               accum_out=sm[:, 1:2]), reads=[pb2, sm_b], writes=[tb, sm_b])
            kop("dve", lambda e: e.tensor_scalar(out=sm[:, 2:4], in0=sm[:, 0:2], scalar1=1.0 / 128,
                                                  scalar2=None, op0=ALU.mult), reads=[sm_b], writes=[sm_b])
            kop("dve", lambda e: e.tensor_tensor(out=sm[:, 4:5], in0=sm[:, 2:3], in1=sm[:, 2:3],
                                                  op=ALU.mult), reads=[sm_b], writes=[sm_b])
            kop("dve", lambda e: e.tensor_tensor(out=sm[:, 5:6], in0=sm[:, 3:4], in1=sm[:, 4:5],
                                                  op=ALU.subtract), reads=[sm_b], writes=[sm_b])
            kop("dve", lambda e: e.tensor_scalar(out=sm[:, 5:6], in0=sm[:, 5:6], scalar1=0.0,
                                                  scalar2=LN_EPS, op0=ALU.max, op1=ALU.add),
                 reads=[sm_b], writes=[sm_b])
            kop("act", lambda e: e.activation(out=sm[:, 6:7], in_=sm[:, 5:6], func=AF.Sqrt),
                 reads=[sm_b], writes=[sm_b])
            kop("dve", lambda e: e.reciprocal(out=sm[:, 7:8], in_=sm[:, 6:7]), reads=[sm_b], writes=[sm_b])
            kop("dve", lambda e: e.tensor_scalar(out=kif[:, 0:128], in0=kif[:, 0:128],
                                                  scalar1=sm[:, 2:3], scalar2=sm[:, 7:8],
                                                  op0=ALU.subtract, op1=ALU.mult),
                 reads=[kif_b, sm_b], writes=[kif_b])
            kop("dve", lambda e: e.tensor_tensor(out=kif[:, 0:128], in0=kif[:, 0:128], in1=kln_t[:, 0:128],
                                                  op=ALU.mult), reads=[kif_b, const_b], writes=[kif_b])
            kop("dve", lambda e: e.tensor_tensor(out=kib[:], in0=kif[:, 0:128], in1=kln_t[:, 128:256],
                                                  op=ALU.add), reads=[kif_b, const_b], writes=[kib_b])
            if not DEBUG.get("kitr", True):
                continue
            ps3b, pb3 = tps, tps_b
            k.op("pe", lambda e: e.transpose(ps3b[:, 0:128], kib[:], identb[:]),
                 reads=[kib_b, ident_b], writes=[pb3])
            so, sob, sds = getstage()
            k.op("act", lambda e: e.copy(out=so[:, 0:128], in_=ps3b[:, 0:128]), reads=[pb3], writes=[sob])
            k.dma("sp", kiT_o[:, t0 + s * 128:t0 + (s + 1) * 128], so[:, 0:128], sds, reads=[sob])

    def program():
        consts()
        if DEBUG.get("halo", True):
            tile_pass(0, 2, True)
        for i in range(ntiles if DEBUG.get("main", True) else 0):
            tile_pass(2 + 512 * i, 512, False)

    k.dry = True
    program()
    k.dry = False
    ws.reset()
    tmp_i[0] = 0
    stage_i[0] = 0
    k.psum_i = 0
    program()
    for d in st_ds + xz_ds + (wist_ds if with_dsa_proj else []):
        if d.total:
            nc.sync.wait_ge(d.sem, d.total)
    return nc


def _lhsT_layout(w):
    K, M = w.shape
    return np.ascontiguousarray(w.reshape(K // 128, 128, M // 128, 128).transpose(2, 1, 0, 3)
                                .reshape(M // 128, 128, (K // 128) * 128))


def _rhs_layout(w):
    K, N = w.shape
    return np.ascontiguousarray(w.reshape(K // 128, 128, N).transpose(1, 0, 2).reshape(128, (K // 128) * N))


def _pvec(v):
    return np.ascontiguousarray(v.reshape(-1, 128).T)


class Rot:
    def __init__(self, k, name, shape, dt, n):
        self.t = [(k.nc.alloc_sbuf_tensor("%s%d" % (name, i), shape, dt), Buf("%s%d" % (name, i)))
                  for i in range(n)]
        self.i = 0

    def get(self):
        r = self.t[self.i % len(self.t)]
        self.i += 1
        return r


def build_gdn_stage(L):
    nc = bass.Bass("TRN2", target_bir_lowering=False)
    k = KB(nc)
    k.init_psum(7)
    tps = nc.alloc_psum_tensor("tps", [128, 1024], BF16)
    tps_b = Buf("tps", excl=True)
    NTILE = L // 512

    def din(name, shape, dt=F32):
        return nc.dram_tensor(name, shape, dt, kind="ExternalInput").ap()

    xT = din("xT", [D, L])
    w_fm = din("w_fm", [12, 128, DC * 128])
    w_ab = din("w_ab", [128, DC * 8])
    convw = din("convw", [128, 8 * 4])
    hvec = din("hvec", [128, 8])
    normg = din("normg", [128, 1])
    cmask = din("cmask", [128, 3 * 128])
    ogT = nc.dram_tensor("ogT", [512, L], BF16, kind="ExternalOutput").ap()

    sb = nc.alloc_sbuf_tensor
    wfm = sb("wfm", [128, 12, DC * 128], BF16)
    wfm_b = Buf("wfm")
    wab = sb("wab", [128, DC * 8], BF16)
    cw = sb("cw", [128, 32], F32)
    hv = sb("hv", [128, 8], F32)
    negA = sb("negA", [128, 4], F32)
    ng = sb("ng", [128, 1], F32)
    cm = sb("cm", [128, 384], F32)
    ones = sb("ones", [128, 128], F32)
    ident = sb("ident", [128, 128], F32)
    identb = sb("identb", [128, 128], BF16)
    const_b = Buf("const")
    U = cm[:, 0:128]
    PMtri = cm[:, 128:256]
    strict01 = cm[:, 256:384]

    xb = [sb("xb%d" % i, [128, DC, 512], BF16) for i in range(2)]
    xb_b = [Buf("xb%d" % i) for i in range(2)]
    xb_ds = [k.dsem() for _ in range(2)]
    halo = sb("halo", [128, 8, 3], F32)
    halo_b = [Buf("halo%d" % i) for i in range(8)]
    Gc = Rot(k, "Gc", [128, 516], F32, 2)
    t512 = Rot(k, "t512", [128, 512], F32, 4)
    qT = sb("qT", [128, 2, 512], BF16)
    qT_b = [Buf("qT%d" % i) for i in range(2)]
    kT = sb("kT", [128, 2, 512], BF16)
    kT_b = [Buf("kT%d" % i) for i in range(2)]
    vT = sb("vT", [128, 4, 512], BF16)
    vT_b = [Buf("vT%d" % i) for i in range(4)]
    zs = sb("zs", [128, 4, 512], F32)
    zs_b = [Buf("zs%d" % i) for i in range(4)]
    tok = sb("tok", [128, 4, 32], F32)
    tok_b = [Buf("tok%d" % i) for i in range(4)]
    f128 = Rot(k, "f128", [128, 128], F32, 48)
    b128 = Rot(k, "b128", [128, 128], BF16, 12)
    P_ktok = Rot(k, "pktok", [128, 128], BF16, 4)
    P_KK = Rot(k, "pkk", [128, 128], F32, 4)
    P_QK = Rot(k, "pqk", [128, 128], F32, 4)
    P_egc = Rot(k, "pegc", [128, 128], F32, 8)
    P_u = Rot(k, "pu", [128, 128], F32, 8)
    P_qkT = Rot(k, "pqkT", [128, 128], BF16, 8)
    P_wT = Rot(k, "pwT", [128, 128], BF16, 8)
    P_qd = Rot(k, "pqd", [128, 128], BF16, 8)
    P_kdec = Rot(k, "pkdec", [128, 128], BF16, 8)
    P_vn = Rot(k, "pvn", [128, 128], BF16, 8)
    S = sb("S", [128, 4, 128], F32)
    Sb = sb("Sb", [128, 4, 128], BF16)
    S_b = [Buf("S%d" % i) for i in range(4)]
    Sb_b = [Buf("Sb%d" % i) for i in range(4)]
    ostage = [sb("ost%d" % i, [128, 4, 512], BF16) for i in range(2)]
    ost_b = [Buf("ost%d" % i) for i in range(2)]
    ost_ds = [k.dsem() for _ in range(2)]

    def consts():
        dc = k.dsem()
        for i in range(12):
            k.dma("pool", wfm[:, i, :], w_fm[i], dc, writes=[wfm_b])
        k.dma("pool", wab[:], w_ab[:, :], dc, writes=[wfm_b])
        dc2 = k.dsem()
        k.dma("sp", cw[:], convw[:, :], dc2, writes=[const_b])
        k.dma("sp", hv[:], hvec[:, :], dc2, writes=[const_b])
        k.dma("sp", ng[:], normg[:, :], dc2, writes=[const_b])
        k.dma("sp", cm[:], cmask[:, :], dc2, writes=[const_b])
        k.op("dve", lambda e: e.memset(ones[:], 1.0), writes=[const_b])
        k.op("pool", lambda e: e.memset(ident[:], 0.0), writes=[const_b])
        k.op("pool", lambda e: e.affine_select(
            out=ident[:], in_=ones[:], pattern=[[-1, 128]], compare_op=ALU.is_equal,
            fill=0.0, base=0, channel_multiplier=1), reads=[const_b], writes=[const_b])
        k.op("dve", lambda e: e.tensor_copy(out=identb[:], in_=ident[:]), reads=[const_b], writes=[const_b])
        k.op("act", lambda e: e.activation(out=negA[:], in_=hv[:, 0:4], func=AF.Exp),
             reads=[const_b], writes=[const_b])
        k.op("dve", lambda e: e.tensor_scalar(out=negA[:], in0=negA[:], scalar1=-1.0, scalar2=None,
                                              op0=ALU.mult), reads=[const_b], writes=[const_b])
        k.op("dve", lambda e: e.memset(halo[:], 0.0), writes=halo_b)
        k.op("dve", lambda e: e.memset(S[:], 0.0), writes=S_b)
        k.op("dve", lambda e: e.memset(Sb[:], 0.0), writes=Sb_b)

    def load_x(i):
        k.dma("pool", xb[i % 2][:], xT[:, i * 512:(i + 1) * 512].rearrange("(m p) w -> p m w", p=128),
              xb_ds[i % 2], writes=[xb_b[i % 2]])

    def proj_tile(i):
        x, xbuf = xb[i % 2], xb_b[i % 2]
        for c in range(12):
            ps, pb = k.psum()
            for kc in range(DC):
                k.op("pe", lambda e: e.matmul(ps[:, 0:512], wfm[:, c, kc * 128:(kc + 1) * 128], x[:, kc, :],
                                              start=(kc == 0), stop=(kc == DC - 1)),
                     reads=[wfm_b, xbuf], writes=[pb], skip_self=True)
            if c >= 8:
                k.op("act", lambda e: e.activation(out=zs[:, c - 8, :], in_=ps[:, 0:512], func=AF.Silu),
                     reads=[pb], writes=[zs_b[c - 8]])
                continue
            g, gb = Gc.get()
            k.op("pool", lambda e: e.tensor_copy(out=g[:, 0:3], in_=halo[:, c, :]), reads=[halo_b[c]], writes=[gb])
            k.op("act", lambda e: e.copy(out=g[:, 3:515], in_=ps[:, 0:512]), reads=[pb], writes=[gb])
            k.op("pool", lambda e: e.tensor_copy(out=halo[:, c, :], in_=g[:, 512:515]), reads=[gb], writes=[halo_b[c]])
            t, tb = t512.get()
            k.op("dve", lambda e: e.tensor_scalar(out=t[:], in0=g[:, 0:512], scalar1=cw[:, 4 * c:4 * c + 1],
                                                  scalar2=None, op0=ALU.mult), reads=[gb, const_b], writes=[tb])
            for j in range(1, 4):
                k.op("dve", lambda e: e.scalar_tensor_tensor(out=t[:], in0=g[:, j:j + 512],
                                                             scalar=cw[:, 4 * c + j:4 * c + j + 1], in1=t[:],
                                                             op0=ALU.mult, op1=ALU.add),
                     reads=[gb, tb, const_b], writes=[tb])
            if c >= 4:
                k.op("act", lambda e: e.activation(out=vT[:, c - 4, :], in_=t[:], func=AF.Silu),
                     reads=[tb], writes=[vT_b[c - 4]])
                continue
            k.op("act", lambda e: e.activation(out=t[:], in_=t[:], func=AF.Silu), reads=[tb], writes=[tb])
            sq, sqb = t512.get()
            k.op("act", lambda e: e.activation(out=sq[:], in_=t[:], func=AF.Square), reads=[tb], writes=[sqb])
            ps2, pb2 = k.psum()
            k.op("pe", lambda e: e.matmul(ps2[:, 0:512], ones[:], sq[:], start=True, stop=True),
                 reads=[const_b, sqb], writes=[pb2])
            k.op("dve", lambda e: e.tensor_scalar(out=sq[:], in0=ps2[:, 0:512], scalar1=RMS_EPS, scalar2=None,
                                                  op0=ALU.add), reads=[pb2], writes=[sqb])
            k.op("act", lambda e: e.activation(out=sq[:], in_=sq[:], func=AF.Sqrt), reads=[sqb], writes=[sqb])
            k.op("dve", lambda e: e.reciprocal(out=sq[:], in_=sq[:]), reads=[sqb], writes=[sqb])
            if c < 2:
                k.op("dve", lambda e: e.scalar_tensor_tensor(out=qT[:, c, :], in0=t[:], scalar=128 ** -0.5,
                                                             in1=sq[:], op0=ALU.mult, op1=ALU.mult),
                     reads=[tb, sqb], writes=[qT_b[c]])
            else:
                k.op("dve", lambda e: e.tensor_tensor(out=kT[:, c - 2, :], in0=t[:], in1=sq[:], op=ALU.mult),
                     reads=[tb, sqb], writes=[kT_b[c - 2]])
        for s in range(4):
            ps, pb = k.psum()
            for kc in range(DC):
                k.op("pe", lambda e: e.matmul(ps[:, 0:8], x[:, kc, s * 128:(s + 1) * 128], wab[:, kc * 8:(kc + 1) * 8],
                                              start=(kc == 0), stop=(kc == DC - 1)),
                     reads=[wfm_b, xbuf], writes=[pb], skip_self=True)
            tk, tkb = tok[:, s, :], tok_b[s]
            k.op("act", lambda e: e.activation(out=tk[:, 4:8], in_=ps[:, 4:8], func=AF.Sigmoid),
                 reads=[pb], writes=[tkb])
            k.op("dve", lambda e: e.tensor_tensor(out=tk[:, 0:4], in0=ps[:, 0:4], in1=hv[:, 4:8], op=ALU.add),
                 reads=[pb, const_b, tkb], writes=[tkb])
            k.op("act", lambda e: e.activation(out=tk[:, 0:4], in_=tk[:, 0:4], func=AF.Exp), reads=[tkb], writes=[tkb])
            k.op("act", lambda e: e.activation(out=tk[:, 0:4], in_=tk[:, 0:4], func=AF.Ln, bias=1.0),
                 reads=[tkb], writes=[tkb])
            k.op("dve", lambda e: e.tensor_tensor(out=tk[:, 0:4], in0=tk[:, 0:4], in1=negA[:], op=ALU.mult),
                 reads=[tkb, const_b], writes=[tkb])
            k.op("dve", lambda e: e.tensor_scalar(out=tk[:, 8:12], in0=tk[:, 4:8], scalar1=-1.0, scalar2=None,
                                                  op0=ALU.mult), reads=[tkb], writes=[tkb])

    def chunk(i, s, ost, ostb):
        cs = slice(s * 128, (s + 1) * 128)
        tk, tkb = tok[:, s, :], tok_b[s]
        ps, pb = k.psum()
        k.op("pe", lambda e: e.matmul(ps[:, 0:4], U, tk[:, 0:4], start=True, stop=True),
             reads=[const_b, tkb], writes=[pb])
        k.op("dve", lambda e: e.tensor_copy(out=tk[:, 12:16], in_=ps[:, 0:4]), reads=[pb, tkb], writes=[tkb])
        ktok = []
        for qh in range(2):
            k.op("pe", lambda e: e.transpose(tps[:, qh * 128:(qh + 1) * 128], kT[:, qh, cs], identb[:]),
                 reads=[kT_b[qh], const_b], writes=[tps_b])
            kt, ktb = P_ktok.get()
            k.op("act", lambda e: e.copy(out=kt[:], in_=tps[:, qh * 128:(qh + 1) * 128]), reads=[tps_b], writes=[ktb])
            ktok.append((kt, ktb))
        KK = []
        QK = []
        for qh in range(2):
            ps1, pb1 = k.psum()
            k.op("pe", lambda e: e.matmul(ps1[:, 0:128], kT[:, qh, cs], kT[:, qh, cs], start=True, stop=True),
                 reads=[kT_b[qh]], writes=[pb1])
            ps2, pb2 = k.psum()
            k.op("pe", lambda e: e.matmul(ps2[:, 0:128], qT[:, qh, cs], kT[:, qh, cs], start=True, stop=True),
                 reads=[qT_b[qh], kT_b[qh]], writes=[pb2])
            kk, kkb = P_KK.get()
            k.op("act", lambda e: e.copy(out=kk[:], in_=ps1[:, 0:128]), reads=[pb1], writes=[kkb])
            qk, qkb = P_QK.get()
            k.op("act", lambda e: e.copy(out=qk[:], in_=ps2[:, 0:128]), reads=[pb2], writes=[qkb])
            KK.append((kk, kkb))
            QK.append((qk, qkb))
        pre = []
        hs = []
        for j in range(4):
            qh = j // 2
            gu, gub = f128.get()
            k.op("dve", lambda e: e.tensor_scalar(out=gu[:], in0=U, scalar1=tk[:, j:j + 1], scalar2=None,
                                                  op0=ALU.mult), reads=[const_b, tkb], writes=[gub])
            psr, pbr = k.psum()
            k.op("pe", lambda e: e.matmul(psr[:, 0:128], ones[:], gu[:], start=True, stop=True),
                 reads=[const_b, gub], writes=[pbr])
            gcr, gcrb = f128.get()
            k.op("act", lambda e: e.copy(out=gcr[:], in_=psr[:, 0:128]), reads=[pbr], writes=[gcrb])
            egc, egcb = P_egc.get()
            k.op("act", lambda e: e.activation(out=egc[:], in_=gcr[:], func=AF.Exp), reads=[gcrb], writes=[egcb])
            k.op("act", lambda e: e.activation(out=tk[:, 16 + j:17 + j], in_=tk[:, 12 + j:13 + j], func=AF.Exp),
                 reads=[tkb], writes=[tkb])
            k.op("dve", lambda e: e.tensor_tensor(out=tk[:, 16 + j:17 + j], in0=tk[:, 16 + j:17 + j],
                                                  in1=tk[:, 4 + j:5 + j], op=ALU.mult), reads=[tkb], writes=[tkb])
            k.op("act", lambda e: e.activation(out=tk[:, 20 + j:21 + j], in_=tk[:, 12 + j:13 + j], func=AF.Exp,
                                               bias=gcr[:, 127:128], scale=-1.0), reads=[tkb, gcrb], writes=[tkb])
            dec, decb = f128.get()
            k.op("dve", lambda e: e.scalar_tensor_tensor(out=dec[:], in0=gcr[:], scalar=tk[:, 12 + j:13 + j],
                                                         in1=PMtri, op0=ALU.subtract, op1=ALU.add),
                 reads=[gcrb, tkb, const_b], writes=[decb])
            k.op("act", lambda e: e.activation(out=dec[:], in_=dec[:], func=AF.Exp, scale=-1.0),
                 reads=[decb], writes=[decb])
            qkm, qkmb = b128.get()
            k.op("dve", lambda e: e.tensor_tensor(out=qkm[:], in0=QK[qh][0][:], in1=dec[:], op=ALU.mult),
                 reads=[QK[qh][1], decb], writes=[qkmb])
            k.op("pe", lambda e: e.transpose(tps[:, 256:384], qkm[:], identb[:]),
                 reads=[qkmb, const_b], writes=[tps_b])
            qkT, qkTb = P_qkT.get()
            k.op("act", lambda e: e.copy(out=qkT[:], in_=tps[:, 256:384]), reads=[tps_b], writes=[qkTb])
            M, Mb = f128.get()
            k.op("dve", lambda e: e.scalar_tensor_tensor(out=M[:], in0=KK[qh][0][:], scalar=tk[:, 8 + j:9 + j],
                                                         in1=dec[:], op0=ALU.mult, op1=ALU.mult),
                 reads=[KK[qh][1], tkb, decb], writes=[Mb])
            k.op("pool", lambda e: e.tensor_tensor(out=M[:], in0=M[:], in1=strict01, op=ALU.mult),
                 reads=[Mb, const_b], writes=[Mb])
            psx, pbx = k.psum()
            k.op("pe", lambda e: e.transpose(psx[:, 0:128], M[:], ident[:]), reads=[Mb, const_b], writes=[pbx])
            X, Xb = f128.get()
            k.op("act", lambda e: e.copy(out=X[:], in_=psx[:, 0:128]), reads=[pbx], writes=[Xb])
            TT, TTb = f128.get()
            k.op("pool", lambda e: e.tensor_tensor(out=TT[:], in0=X[:], in1=ident[:], op=ALU.add),
                 reads=[Xb, const_b], writes=[TTb])
            hs.append(dict(qh=qh, M=M, Mb=Mb, X=X, Xb=Xb, TT=TT, TTb=TTb, egc=egc, egcb=egcb,
                           qkT=qkT, qkTb=qkTb))
        for it in range(1, 7):
            for j in range(4):
                h_ = hs[j]
                M, Mb, X, Xb, TT, TTb = h_["M"], h_["Mb"], h_["X"], h_["Xb"], h_["TT"], h_["TTb"]
                psm, pbm = k.psum()
                k.op("pe", lambda e: e.matmul(psm[:, 0:128], X[:], M[:], start=True, stop=True),
                     reads=[Xb, Mb], writes=[pbm])
                if it < 6:
                    psx2, pbx2 = k.psum()
                    k.op("pe", lambda e: e.matmul(psx2[:, 0:128], M[:], X[:], start=True, stop=True),
                         reads=[Xb, Mb], writes=[pbx2])
                M2, M2b = f128.get()
                k.op("act", lambda e: e.copy(out=M2[:], in_=psm[:, 0:128]), reads=[pbm], writes=[M2b])
                if it < 6:
                    X2, X2b = f128.get()
                    k.op("dve", lambda e: e.tensor_copy(out=X2[:], in_=psx2[:, 0:128]), reads=[pbx2], writes=[X2b])
                pst, pbt = k.psum()
                k.op("pe", lambda e: e.matmul(pst[:, 0:128], M2[:], TT[:], start=True, stop=True),
                     reads=[M2b, TTb], writes=[pbt])
                TT2, TT2b = f128.get()
                k.op("dve", lambda e: e.tensor_tensor(out=TT2[:], in0=TT[:], in1=pst[:, 0:128], op=ALU.add),
                     reads=[TTb, pbt], writes=[TT2b])
                M, Mb = M2, M2b
                if it < 6:
                    X, Xb = X2, X2b
                TT, TTb = TT2, TT2b
                h_["M"], h_["Mb"], h_["X"], h_["Xb"], h_["TT"], h_["TTb"] = M, Mb, X, Xb, TT, TTb
        for j in range(4):
            h_ = hs[j]
            qh = h_["qh"]
            TT, TTb, egc, egcb, qkT, qkTb = h_["TT"], h_["TTb"], h_["egc"], h_["egcb"], h_["qkT"], h_["qkTb"]
            Tb, Tbb = b128.get()
            k.op("pool", lambda e: e.tensor_copy(out=Tb[:], in_=TT[:]), reads=[TTb], writes=[Tbb])
            kbg, kbgb = b128.get()
            k.op("pool", lambda e: e.tensor_scalar(out=kbg[:], in0=ktok[qh][0][:], scalar1=tk[:, 16 + j:17 + j],
                                                   scalar2=None, op0=ALU.mult), reads=[ktok[qh][1], tkb], writes=[kbgb])
            kdec, kdecb = P_kdec.get()
            k.op("pool", lambda e: e.tensor_scalar(out=kdec[:], in0=ktok[qh][0][:], scalar1=tk[:, 20 + j:21 + j],
                                                   scalar2=None, op0=ALU.mult), reads=[ktok[qh][1], tkb], writes=[kdecb])
            k.op("pe", lambda e: e.transpose(tps[:, 384:512], vT[:, j, cs], identb[:]),
                 reads=[vT_b[j], const_b], writes=[tps_b])
            vb, vbb = b128.get()
            k.op("act", lambda e: e.activation(out=vb[:], in_=tps[:, 384:512], func=AF.Identity,
                                               scale=tk[:, 4 + j:5 + j]), reads=[tps_b, tkb], writes=[vbb])
            psu, pbu = k.psum()
            k.op("pe", lambda e: e.matmul(psu[:, 0:128], Tb[:], vb[:], start=True, stop=True),
                 reads=[Tbb, vbb], writes=[pbu])
            u, ub = P_u.get()
            k.op("act", lambda e: e.copy(out=u[:], in_=psu[:, 0:128]), reads=[pbu], writes=[ub])
            psw, pbw = k.psum()
            k.op("pe", lambda e: e.matmul(psw[:, 0:128], kbg[:], Tb[:], start=True, stop=True),
                 reads=[kbgb, Tbb], writes=[pbw])
            wT, wTb = P_wT.get()
            k.op("act", lambda e: e.copy(out=wT[:], in_=psw[:, 0:128]), reads=[pbw], writes=[wTb])
            qd, qdb = P_qd.get()
            k.op("pool", lambda e: e.tensor_tensor(out=qd[:], in0=qT[:, qh, cs], in1=egc[:], op=ALU.mult),
                 reads=[qT_b[qh], egcb], writes=[qdb])
            pre.append(dict(u=(u, ub), wT=(wT, wTb), qd=(qd, qdb), qkT=(qkT, qkTb), kdec=(kdec, kdecb),
                            egc=(egc, egcb)))
        vn = []
        for j in range(4):
            p = pre[j]
            ps1, pb1 = k.psum()
            k.op("pe", lambda e: e.matmul(ps1[:, 0:128], p["wT"][0][:], Sb[:, j, :], start=True, stop=True),
                 reads=[p["wT"][1], Sb_b[j]], writes=[pb1])
            v, vb_ = P_vn.get()
            k.op("dve", lambda e: e.tensor_tensor(out=v[:], in0=p["u"][0][:], in1=ps1[:, 0:128], op=ALU.subtract),
                 reads=[p["u"][1], pb1], writes=[vb_])
            vn.append((v, vb_))
        for j in range(4):
            p = pre[j]
            v, vb_ = vn[j]
            pso, pbo = k.psum()
            k.op("pe", lambda e: e.matmul(pso[:, 0:128], Sb[:, j, :], p["qd"][0][:], start=True, stop=False),
                 reads=[Sb_b[j], p["qd"][1]], writes=[pbo])
            k.op("pe", lambda e: e.matmul(pso[:, 0:128], v[:], p["qkT"][0][:], start=False, stop=True),
                 reads=[vb_, p["qkT"][1]], writes=[pbo], skip_self=True)
            psk, pbk = k.psum()
            k.op("pe", lambda e: e.matmul(psk[:, 0:128], p["kdec"][0][:], v[:], start=True, stop=True),
                 reads=[p["kdec"][1], vb_], writes=[pbk])
            k.op("dve", lambda e: e.scalar_tensor_tensor(out=S[:, j, :], in0=S[:, j, :],
                                                         scalar=p["egc"][0][:, 127:128], in1=psk[:, 0:128],
                                                         op0=ALU.mult, op1=ALU.add),
                 reads=[S_b[j], p["egc"][1], pbk], writes=[S_b[j]])
            k.op("act", lambda e: e.copy(out=Sb[:, j, :], in_=S[:, j, :]), reads=[S_b[j]], writes=[Sb_b[j]])
            o, ob = f128.get()
            k.op("act", lambda e: e.copy(out=o[:], in_=pso[:, 0:128]), reads=[pbo], writes=[ob])
            sq, sqb = f128.get()
            k.op("act", lambda e: e.activation(out=sq[:], in_=o[:], func=AF.Square), reads=[ob], writes=[sqb])
            pss, pbs = k.psum()
            k.op("pe", lambda e: e.matmul(pss[:, 0:128], ones[:], sq[:], start=True, stop=True),
                 reads=[const_b, sqb], writes=[pbs])
            k.op("dve", lambda e: e.tensor_scalar(out=sq[:], in0=pss[:, 0:128], scalar1=1.0 / 128,
                                                  scalar2=RMS_EPS, op0=ALU.mult, op1=ALU.add),
                 reads=[pbs], writes=[sqb])
            k.op("act", lambda e: e.activation(out=sq[:], in_=sq[:], func=AF.Sqrt), reads=[sqb], writes=[sqb])
            k.op("dve", lambda e: e.reciprocal(out=sq[:], in_=sq[:]), reads=[sqb], writes=[sqb])
            k.op("dve", lambda e: e.tensor_tensor(out=o[:], in0=o[:], in1=sq[:], op=ALU.mult),
                 reads=[ob, sqb], writes=[ob])
            k.op("dve", lambda e: e.scalar_tensor_tensor(out=ost[:, j, cs], in0=o[:], scalar=ng[:, 0:1],
                                                         in1=zs[:, j, cs], op0=ALU.mult, op1=ALU.mult),
                 reads=[ob, const_b, zs_b[j]], writes=[ostb])

    def program():
        consts()
        load_x(0)
        for i in range(NTILE):
            if i + 1 < NTILE:
                load_x(i + 1)
            proj_tile(i)
            ost, ostb, ods = ostage[i % 2], ost_b[i % 2], ost_ds[i % 2]
            for s in range(4):
                chunk(i, s, ost, ostb)
            k.dma("sp", ogT[:, i * 512:(i + 1) * 512].rearrange("(j p) w -> p j w", p=128), ost[:], ods,
                  reads=[ostb])

    program()
    for d in ost_ds:
        if d.total:
            nc.sync.wait_ge(d.sem, d.total)
    return nc


def gdn_consts():
    r = np.arange(128)
    U = (r[:, None] <= r[None, :]).astype(np.float32)
    PM = np.where(r[None, :] <= r[:, None], 0.0, 1e5).astype(np.float32)
    strict = (r[None, :] < r[:, None]).astype(np.float32)
    return np.ascontiguousarray(np.concatenate([U, PM, strict], axis=1))


def gdn_inputs(c, xT, w_in, conv_w, a_log, dt_bias, norm_g):
    qc = [w_in[:, (2 * c + i) * 128:(2 * c + i + 1) * 128] for i in range(2)]
    kc = [w_in[:, 2048 + (2 * c + i) * 128:2048 + (2 * c + i + 1) * 128] for i in range(2)]
    vc = [w_in[:, 4096 + (4 * c + i) * 128:4096 + (4 * c + i + 1) * 128] for i in range(4)]
    zc = [w_in[:, 8192 + (4 * c + i) * 128:8192 + (4 * c + i + 1) * 128] for i in range(4)]
    wa = w_in[:, 12288 + 4 * c:12288 + 4 * c + 4]
    wb = w_in[:, 12320 + 4 * c:12320 + 4 * c + 4]
    chans = ([(2 * c + i) * 128 for i in range(2)] + [2048 + (2 * c + i) * 128 for i in range(2)]
             + [4096 + (4 * c + i) * 128 for i in range(4)])
    cw = np.stack([conv_w[:, ch:ch + 128].T for ch in chans], axis=1)
    hv = np.concatenate([a_log[4 * c:4 * c + 4], dt_bias[4 * c:4 * c + 4]])
    return {
        "xT": xT,
        "w_fm": _lhsT_layout(np.concatenate(qc + kc + vc + zc, axis=1)),
        "w_ab": _rhs_layout(np.concatenate([wa, wb], axis=1)),
        "convw": np.ascontiguousarray(cw.reshape(128, 32)),
        "hvec": np.ascontiguousarray(np.broadcast_to(hv[None, :], (128, 8))),
        "normg": np.ascontiguousarray(norm_g.reshape(128, 1)),
        "cmask": gdn_consts(),
    }


NEG = -1.0e30
NBIS = 24
TOPK = 256


def build_dsa_stage(NB):
    nc = bass.Bass("TRN2", target_bir_lowering=False)
    k = KB(nc)
    LK = 1024 * NB
    NQ = 128 * NB
    scale = 128 ** -0.5

    def din(name, shape, dt=F32):
        return nc.dram_tensor(name, shape, dt, kind="ExternalInput").ap()

    qT = din("qT", [D, NQ], BF16)
    qiT = din("qiT", [D, NQ], BF16)
    wi = din("wi", [128, NB * 16])
    kT = din("kT", [512, LK], BF16)
    vaug = din("vaug", [LK, 4 * 129], BF16)
    kiT = din("kiT", [128, LK], BF16)
    pmask = din("pmask", [128, 896])
    trimask = din("trimask", [128, 128])
    Dn = din("Dn", [128, 2 * 16 * 128])
    cfar = din("cfar", [128, 16])
    oT = nc.dram_tensor("oT", [D, NQ], BF16, kind="ExternalOutput").ap()

    acc_ps = [(nc.alloc_psum_tensor("acc%d" % i, [128, 512], F32), Buf("acc%d" % i, excl=True)) for i in range(4)]
    k.init_psum(3)
    tps = nc.alloc_psum_tensor("tps", [128, 1024], BF16)
    tps_b = Buf("tps", excl=True)

    sb = nc.alloc_sbuf_tensor
    sc = sb("sc", [128, LK], F32)
    sc_b = Buf("sc")
    selT = sb("selT", [128, LK], BF16)
    selT_b = Buf("selT")
    qb = [sb("qb%d" % i, [128, 16, 128], BF16) for i in range(2)]
    qb_b = [Buf("qb%d" % i) for i in range(2)]
    qib = [sb("qib%d" % i, [128, 16, 128], BF16) for i in range(2)]
    qib_b = [Buf("qib%d" % i) for i in range(2)]
    q_ds = [k.dsem() for _ in range(2)]
    wi_t = sb("wi_t", [128, NB * 16], F32)
    pm_t = sb("pm_t", [128, 896], F32)
    tri_t = sb("tri_t", [128, 128], F32)
    Dn_t = sb("Dn_t", [128, 2, 16, 128], F32)
    cf_t = sb("cf_t", [128, 16], F32)
    identb = sb("identb", [128, 128], BF16)
    ones = sb("ones", [128, 128], F32)
    const_b = Buf("const")
    kis = [(sb("kis%d" % i, [128, 512], BF16), Buf("kis%d" % i), k.dsem()) for i in range(3)]
    rl = Rot(k, "rl", [128, 512], F32, 4)
    eb = Rot(k, "eb", [128, 512], BF16, 6)
    pb_ = Rot(k, "pb", [128, 512], BF16, 6)
    grp_i = [0]
    nt = Rot(k, "nt", [128, 512], F32, 2)
    KCH = 1024
    ks = [(sb("ks%d" % i, [128, KCH], BF16), Buf("ks%d" % i), k.dsem()) for i in range(3)]
    vs = [(sb("vs%d" % i, [128, KCH // 128, 129], BF16), Buf("vs%d" % i), k.dsem()) for i in range(3)]
    side = sb("side", [128, 512], F32)
    side_b = Buf("side")
    bs = sb("bs", [128, 16], F32)
    bs_b = Buf("bs")
    bs2 = sb("bs2", [128, 4], F32)
    bs2_b = Buf("bs2")
    selT2_b = Buf("selT2")
    selc = Rot(k, "selc", [128, 512], BF16, 2)
    osb = Rot(k, "osb", [128, 128], BF16, 2)
    ost = [(sb("ost%d" % i, [128, 128], BF16), Buf("ost%d" % i), k.dsem()) for i in range(4)]
    cnt = {"ki": 0, "kv": 0, "ost": 0}

    def consts():
        dc = k.dsem()
        k.dma("sp", wi_t[:], wi[:, :], dc, writes=[const_b])
        k.dma("sp", pm_t[:], pmask[:, :], dc, writes=[const_b])
        k.dma("sp", tri_t[:], trimask[:, :], dc, writes=[const_b])
        k.dma("sp", Dn_t[:].rearrange("p a b c -> p (a b c)"), Dn[:, :], dc, writes=[const_b])
        k.dma("sp", cf_t[:], cfar[:, :], dc, writes=[const_b])
        k.op("dve", lambda e: e.memset(ones[:], 1.0), writes=[const_b])
        t, tb = rl.get()
        k.op("pool", lambda e: e.memset(t[:, 0:128], 0.0), writes=[tb])
        k.op("pool", lambda e: e.affine_select(
            out=t[:, 0:128], in_=ones[:], pattern=[[-1, 128]], compare_op=ALU.is_equal,
            fill=0.0, base=0, channel_multiplier=1), reads=[const_b], writes=[tb])
        k.op("dve", lambda e: e.tensor_copy(out=identb[:], in_=t[:, 0:128]), reads=[tb], writes=[const_b])
        for dl in range(2):
            for h in range(16):
                k.op("dve", lambda e: e.tensor_scalar(out=Dn_t[:, dl, h, :], in0=Dn_t[:, dl, h, :],
                                                      scalar1=cf_t[:, h:h + 1], scalar2=None, op0=ALU.subtract),
                     reads=[const_b], writes=[const_b])

    def load_q(j):
        i = j % 2
        k.dma("sp", qb[i][:], qT[:, j * 128:(j + 1) * 128].rearrange("(h p) t -> p h t", p=128), q_ds[i],
              writes=[qb_b[i]])
        k.dma("sp", qib[i][:], qiT[:, j * 128:(j + 1) * 128].rearrange("(h p) t -> p h t", p=128), q_ds[i],
              writes=[qib_b[i]])

    def block(j):
        L = 1024 * (j + 1)
        q, q_b, qi, qi_b = qb[j % 2], qb_b[j % 2], qib[j % 2], qib_b[j % 2]
        for ch in range(L // 512):
            kt, ktb, kds = kis[cnt["ki"] % 3]
            cnt["ki"] += 1
            k.dma("sp", kt[:], kiT[:, ch * 512:(ch + 1) * 512], kds, writes=[ktb])
            scc = sc[:, ch * 512:(ch + 1) * 512]
            for h in range(16):
                ps, pb = k.psum()
                k.op("pe", lambda e: e.matmul(ps[:, 0:512], qi[:, h, :], kt[:], start=True, stop=True),
                     reads=[qi_b, ktb], writes=[pb])
                r, rb = rl.get()
                k.op("act", lambda e: e.activation(out=r[:], in_=ps[:, 0:512], func=AF.Relu), reads=[pb], writes=[rb])
                wcol = wi_t[:, j * 16 + h:j * 16 + h + 1]
                if h == 0:
                    k.op("dve", lambda e: e.tensor_scalar(out=scc, in0=r[:], scalar1=wcol, scalar2=None,
                                                          op0=ALU.mult), reads=[rb, const_b], writes=[sc_b])
                else:
                    k.op("dve", lambda e: e.scalar_tensor_tensor(out=scc, in0=r[:], scalar=wcol, in1=scc,
                                                                 op0=ALU.mult, op1=ALU.add),
                         reads=[rb, const_b, sc_b], writes=[sc_b])
        k.op("dve", lambda e: e.tensor_reduce(out=bs[:, 0:1], in_=sc[:, 0:L], axis=AX.X, op=ALU.min),
             reads=[sc_b], writes=[bs_b])
        k.op("dve", lambda e: e.tensor_scalar(out=bs[:, 0:1], in0=bs[:, 0:1], scalar1=-1.0, scalar2=None,
                                              op0=ALU.add), reads=[bs_b], writes=[bs_b])
        k.op("dve", lambda e: e.tensor_tensor(out=sc[:, 0:896], in0=sc[:, 0:896], in1=pm_t[:], op=ALU.add),
             reads=[sc_b, const_b], writes=[sc_b])
        k.op("dve", lambda e: e.tensor_tensor(out=sc[:, L - 128:L], in0=sc[:, L - 128:L], in1=tri_t[:], op=ALU.add),
             reads=[sc_b, const_b], writes=[sc_b])
        k.op("dve", lambda e: e.reduce_max(out=bs[:, 1:2], in_=sc[:, 0:L], axis=AX.X), reads=[sc_b, bs_b], writes=[bs_b])
        lo, hi, mid, cn, ge, df, nmid, sg2 = (bs[:, i:i + 1] for i in (0, 1, 2, 3, 4, 5, 6, 7))
        sg = bs2[:, 0:1]
        for it in range(NBIS):
            k.op("dve", lambda e: e.tensor_tensor(out=mid, in0=lo, in1=hi, op=ALU.add), reads=[bs_b], writes=[bs_b])
            k.op("dve", lambda e: e.tensor_scalar(out=mid, in0=mid, scalar1=0.5, scalar2=None, op0=ALU.mult),
                 reads=[bs_b], writes=[bs_b])
            LD = (L * 7 // 16) // 128 * 128
            k.op("dve", lambda e: e.tensor_scalar(out=nmid, in0=mid, scalar1=-1.0, scalar2=None, op0=ALU.mult),
                 reads=[bs_b], writes=[bs_b])
            k.op("act", lambda e: e.activation(out=selT[:, LD:L], in_=sc[:, LD:L], func=AF.Sign, bias=nmid,
                                               scale=1.0, accum_out=sg), reads=[sc_b, bs_b, selT_b], writes=[selT2_b, bs2_b])
            k.op("dve", lambda e: e.tensor_scalar(out=selT[:, 0:LD], in0=sc[:, 0:LD], scalar1=mid, scalar2=0.0,
                                                  op0=ALU.is_gt, op1=ALU.add, accum_out=cn),
                 reads=[sc_b, bs_b, selT_b], writes=[selT_b, bs_b])
            k.op("dve", lambda e: e.tensor_scalar(out=sg2, in0=sg, scalar1=float(L - LD), scalar2=0.5,
                                                  op0=ALU.add, op1=ALU.mult), reads=[bs2_b, bs_b], writes=[bs_b])
            k.op("dve", lambda e: e.tensor_tensor(out=cn, in0=cn, in1=sg2, op=ALU.add), reads=[bs_b], writes=[bs_b])
            k.op("dve", lambda e: e.tensor_scalar(out=ge, in0=cn, scalar1=float(TOPK) - 0.5, scalar2=None,
                                                  op0=ALU.is_gt), reads=[bs_b], writes=[bs_b])
            k.op("dve", lambda e: e.tensor_tensor(out=df, in0=mid, in1=lo, op=ALU.subtract), reads=[bs_b], writes=[bs_b])
            k.op("dve", lambda e: e.scalar_tensor_tensor(out=lo, in0=df, scalar=ge, in1=lo, op0=ALU.mult, op1=ALU.add),
                 reads=[bs_b], writes=[bs_b])
            k.op("dve", lambda e: e.tensor_tensor(out=df, in0=hi, in1=mid, op=ALU.subtract), reads=[bs_b], writes=[bs_b])
            k.op("dve", lambda e: e.scalar_tensor_tensor(out=hi, in0=df, scalar=ge, in1=mid, op0=ALU.mult, op1=ALU.add),
                 reads=[bs_b], writes=[bs_b])
        for ch in range(L // 512):
            s1, s1b = selc.get()
            k.op("dve", lambda e: e.tensor_scalar(out=s1[:], in0=sc[:, ch * 512:(ch + 1) * 512], scalar1=lo,
                                                  scalar2=None, op0=ALU.is_gt), reads=[sc_b, bs_b], writes=[s1b])
            for i in range(4):
                k.op("pe", lambda e: e.transpose(tps[:, i * 128:(i + 1) * 128], s1[:, i * 128:(i + 1) * 128], identb[:]),
                     reads=[s1b, const_b], writes=[tps_b], skip_self=True)
            k.op("act", lambda e: e.copy(out=selT[:, ch * 512:(ch + 1) * 512], in_=tps[:, 0:512]),
                 reads=[tps_b], writes=[selT_b, selT2_b])
        nS = L // 128
        for g in range(4):
            q4 = q[:, 4 * g:4 * g + 4, :]
            slabs = []
            for kc in range(L // KCH):
                kt, ktb, kds = ks[cnt["kv"] % 3]
                vt, vtb, vds = vs[cnt["kv"] % 3]
                cnt["kv"] += 1
                slabs.append((kc, kt, ktb, vt, vtb, kds, vds))

            def load_slab(sl):
                kc, kt, ktb, vt, vtb, kds, vds = sl
                k.dma("sp", kt[:, 0:KCH], kT[g * 128:(g + 1) * 128, kc * KCH:(kc + 1) * KCH], kds, writes=[ktb])
                k.dma("sp", vt[:, 0:KCH // 128, :],
                      vaug[kc * KCH:(kc + 1) * KCH, g * 129:(g + 1) * 129].rearrange("(s p) c -> p s c", p=128),
                      vds, writes=[vtb])

            def logits(S):
                kc, kt, ktb, vt, vtb, _, _ = slabs[S // (KCH // 128)]
                si = S % (KCH // 128)
                ps, pb = k.psum()
                k.op("pe", lambda e: e.matmul(ps[:, 0:512], kt[:, si * 128:(si + 1) * 128], q4,
                                              start=True, stop=True), reads=[ktb, q_b], writes=[pb])
                e_, e_b = eb.get()
                dlt = nS - 1 - S
                if dlt >= 2:
                    k.op("act", lambda e: e.activation(out=e_[:], in_=ps[:, 0:512], func=AF.Exp, scale=scale),
                         reads=[pb], writes=[e_b])
                else:
                    n_, n_b = nt.get()
                    k.op("dve", lambda e: e.scalar_tensor_tensor(
                        out=n_[:].rearrange("p (h t) -> p h t", h=4),
                        in0=ps[:, 0:512].rearrange("p (h t) -> p h t", h=4), scalar=scale,
                        in1=Dn_t[:, dlt, 4 * g:4 * g + 4, :], op0=ALU.mult, op1=ALU.add),
                        reads=[pb, const_b], writes=[n_b])
                    k.op("act", lambda e: e.activation(out=e_[:], in_=n_[:], func=AF.Exp),
                         reads=[n_b], writes=[e_b])
                p_, p_b = pb_.get()
                grp_i[0] += 1
                k.op("pool" if grp_i[0] % 4 == 0 else "dve", lambda e: e.tensor_tensor(
                    out=p_[:].rearrange("p (h t) -> p h t", h=4), in0=e_[:].rearrange("p (h t) -> p h t", h=4),
                    in1=selT[:, S * 128:(S + 1) * 128].unsqueeze(1).to_broadcast([128, 4, 128]), op=ALU.mult),
                    reads=[e_b, selT_b], writes=[p_b])
                return p_, p_b

            def pv(S, p_, p_b):
                kc, kt, ktb, vt, vtb, _, _ = slabs[S // (KCH // 128)]
                si = S % (KCH // 128)
                for hh in range(4):
                    ops_, opb = acc_ps[hh]
                    k.op("pe", lambda e: e.matmul(ops_[:, 0:129], p_[:, hh * 128:(hh + 1) * 128], vt[:, si, :],
                                                  start=(S == 0), stop=(S == nS - 1)),
                         reads=[p_b, vtb], writes=[opb], skip_self=True)

            LOOK = 2
            pend = []
            nslab = 0
            for S in range(nS):
                need = S // (KCH // 128)
                while nslab <= min(need + 1, len(slabs) - 1):
                    if nslab - 2 >= 0 and nslab - 2 > (S - LOOK - 1) // (KCH // 128):
                        break
                    load_slab(slabs[nslab])
                    nslab += 1
                pend.append((S,) + logits(S))
                if len(pend) > LOOK:
                    pv(*pend.pop(0))
            while pend:
                pv(*pend.pop(0))
            for hh in range(4):
                h = 4 * g + hh
                ops_, opb = acc_ps[hh]
                k.op("dve", lambda e: e.reciprocal(out=bs[:, 8 + hh:9 + hh], in_=ops_[:, 128:129]),
                     reads=[opb, bs_b], writes=[bs_b])
                o_, o_b = osb.get()
                k.op("act", lambda e: e.activation(out=o_[:], in_=ops_[:, 0:128], func=AF.Identity,
                                                   scale=bs[:, 8 + hh:9 + hh]), reads=[opb, bs_b], writes=[o_b])
                k.op("pe", lambda e: e.transpose(tps[:, 512 + hh * 128:512 + (hh + 1) * 128], o_[:], identb[:]),
                     reads=[o_b, const_b], writes=[tps_b])
                st_, st_b, st_ds = ost[cnt["ost"] % 4]
                cnt["ost"] += 1
                k.op("act", lambda e: e.copy(out=st_[:], in_=tps[:, 512 + hh * 128:512 + (hh + 1) * 128]),
                     reads=[tps_b], writes=[st_b])
                k.dma("sp", oT[h * 128:(h + 1) * 128, j * 128:(j + 1) * 128], st_[:], st_ds, reads=[st_b])

    consts()
    load_q(0)
    for j in range(NB):
        if j + 1 < NB:
            load_q(j + 1)
        block(j)
    for _, _, d in ost:
        if d.total:
            nc.sync.wait_ge(d.sem, d.total)
    return nc


def rel_bucket_np(d):
    d = np.maximum(d, 0)
    df = np.maximum(d, 1).astype(np.float32)
    large = 16 + (np.log(df / np.float32(16)) / np.float32(math.log(128 / 16)) * np.float32(16)).astype(np.int32)
    large = np.minimum(large, 31)
    return np.where(d < 16, d, large)


def dsa_inputs(c, NB, qT, qiT, wi, kT, vaug, kiT, rel_bias):
    SEQL = 1024 * NB
    blocks = [8 * j + c for j in range(NB)]
    cols = np.concatenate([np.arange(b * 128, (b + 1) * 128) for b in blocks])
    sh = 128 * (7 - c)

    def shift_cols(a):
        out = np.zeros_like(a)
        out[:, sh:] = a[:, :SEQL - sh]
        return out

    def shift_rows(a):
        out = np.zeros_like(a)
        out[sh:] = a[:SEQL - sh]
        return out
    pm = np.zeros((128, 896), np.float32)
    pm[:, :sh] = NEG
    r = np.arange(128)
    tri = np.where(r[None, :] <= r[:, None], 0.0, NEG).astype(np.float32)
    dn = np.zeros((128, 2, 16, 128), np.float32)
    for dl in range(2):
        dist = 128 * dl + r[None, :] - r[:, None]
        dn[:, dl, :, :] = rel_bias[rel_bucket_np(dist)].transpose(0, 2, 1)
    wic = wi[cols].reshape(NB, 128, 16).transpose(1, 0, 2).reshape(128, NB * 16)
    return {
        "qT": np.ascontiguousarray(qT[:, cols]), "qiT": np.ascontiguousarray(qiT[:, cols]),
        "wi": np.ascontiguousarray(wic),
        "kT": shift_cols(kT), "vaug": shift_rows(vaug), "kiT": shift_cols(kiT),
        "pmask": pm, "trimask": tri, "Dn": np.ascontiguousarray(dn.reshape(128, -1)),
        "cfar": np.ascontiguousarray(np.broadcast_to(rel_bias[31][None, :], (128, 16))),
    }


_NC_CACHE = {}


def _get_nc(key, fn):
    if key not in _NC_CACHE:
        _NC_CACHE[key] = fn()
    return _NC_CACHE[key]


def _run(nc, in_maps):
    res = run_bass_kernel_spmd(nc, in_maps, core_ids=list(range(NCORE)))
    return res.results


def _token_inputs(li, w_o, ln1_g, ln1_b, ffn_w_gate, ffn_w_up, ffn_conv_w, ffn_w_down, ln2_g, ln2_b,
                  ple_w_proj, ple_w_gate):
    return {
        "w_o": _lhsT_layout(w_o), "w_gate": _lhsT_layout(ffn_w_gate[li]), "w_up": _lhsT_layout(ffn_w_up[li]),
        "w_down": _lhsT_layout(ffn_w_down[li]), "w_pg": _lhsT_layout(ple_w_gate[li]),
        "w_pp": _lhsT_layout(ple_w_proj[li]),
        "vecs": np.ascontiguousarray(np.concatenate([_pvec(ln1_g[li]), _pvec(ln1_b[li]), _pvec(ln2_g[li]),
                                                     _pvec(ln2_b[li])], axis=1)),
        "convw": np.ascontiguousarray(ffn_conv_w[li].reshape(3, FC, 128).transpose(2, 1, 0).reshape(128, FC * 3)),
    }


def _seg(aT, c, nt=2048):
    out = np.zeros((aT.shape[0], nt + 2), aT.dtype)
    if c == 0:
        out[:, 2:] = aT[:, 0:nt]
    else:
        out[:] = aT[:, nt * c - 2:nt * (c + 1)]
    return out


def kernel(x, p, gdn_w_in, gdn_conv_w, gdn_a_log, gdn_dt_bias, gdn_norm_g, gdn_w_o,
           dsa_w_in, dsa_kidx_ln_g, dsa_kidx_ln_b, dsa_w_o, rel_bias,
           ln1_g, ln1_b, ffn_w_gate, ffn_w_up, ffn_conv_w, ffn_w_down, ln2_g, ln2_b,
           ple_w_proj, ple_w_gate):
    import sys
    import time
    t00 = time.time()

    def log(msg):
        print("[kernel %.1fs] %s" % (time.time() - t00, msg), file=sys.stderr, flush=True)
    f32 = lambda a: np.ascontiguousarray(np.asarray(a, dtype=np.float32))
    x = f32(x)
    p = f32(p)
    xT = np.ascontiguousarray(x[0].T)
    pT = [np.ascontiguousarray(p[i, 0].T) for i in range(2)]
    NT = SEQ // NCORE

    nc_a = _get_nc("gdn", lambda: build_gdn_stage(SEQ))
    log("gdn built")
    w_in = f32(gdn_w_in[0])
    ins = [gdn_inputs(c, xT, w_in, f32(gdn_conv_w[0]), f32(gdn_a_log[0]), f32(gdn_dt_bias[0]), f32(gdn_norm_g[0]))
           for c in range(NCORE)]
    ra = _run(nc_a, ins)
    log("gdn done")
    ogT = np.concatenate([np.asarray(r["ogT"]) for r in ra], axis=0)
    del ins, ra

    nc_b = _get_nc("tok0", lambda: build_token_stage(32, NT // 512, True))
    log("tok0 built")
    wd = f32(dsa_w_in[0])
    shared = _token_inputs(0, f32(gdn_w_o[0]), f32(ln1_g), f32(ln1_b), f32(ffn_w_gate), f32(ffn_w_up),
                           f32(ffn_conv_w), f32(ffn_w_down), f32(ln2_g), f32(ln2_b), f32(ple_w_proj), f32(ple_w_gate))
    shared["w_fm"] = _lhsT_layout(np.concatenate([wd[:, 0:2048], wd[:, 2048:2560], wd[:, 3072:5120]], axis=1))
    shared["w_v"] = np.ascontiguousarray(_rhs_layout(wd[:, 2560:3072]).reshape(128, 2, 8 * 512).transpose(1, 0, 2))
    shared["w_kw"] = _rhs_layout(wd[:, 5120:5264])
    shared["kln"] = np.ascontiguousarray(np.broadcast_to(
        np.concatenate([f32(dsa_kidx_ln_g[0]), f32(dsa_kidx_ln_b[0])])[None, :], (128, 256)))
    ins = []
    for c in range(NCORE):
        d = dict(shared)
        d["xT"] = _seg(xT, c)
        d["mixT"] = _seg(ogT, c)
        d["pT"] = _seg(pT[0], c)
        d["hscale"] = np.full((128, 1), 0.0 if c == 0 else 1.0, np.float32)
        ins.append(d)
    rb = _run(nc_b, ins)
    log("tok0 done")
    cat = lambda name, ax: np.concatenate([np.asarray(r[name]) for r in rb], axis=ax)
    x1T = cat("outT", 1)
    qT = cat("qT_o", 1)
    kT = cat("kT_o", 1)
    qiT = cat("qiT_o", 1)
    vaug = cat("v_o", 0).reshape(SEQ, 516)
    kiT = cat("kiT_o", 1)
    wi = cat("wi_o", 0)
    del ins, rb, ogT, shared

    NB = SEQ // 1024
    nc_c = _get_nc("dsa", lambda: build_dsa_stage(NB))
    log("dsa built")
    rb_ = f32(rel_bias)
    ins = [dsa_inputs(c, NB, qT, qiT, wi, kT, vaug, kiT, rb_) for c in range(NCORE)]
    rc = _run(nc_c, ins)
    log("dsa done")
    oT = np.zeros((D, SEQ), NPBF)
    for c in range(NCORE):
        oc = np.asarray(rc[c]["oT"])
        for j in range(NB):
            b = 8 * j + c
            oT[:, b * 128:(b + 1) * 128] = oc[:, j * 128:(j + 1) * 128]
    del ins, rc, qT, qiT, kT, vaug, kiT, wi

    nc_d = _get_nc("tok1", lambda: build_token_stage(16, NT // 512, False))
    log("tok1 built")
    shared = _token_inputs(1, f32(dsa_w_o[0]), f32(ln1_g), f32(ln1_b), f32(ffn_w_gate), f32(ffn_w_up),
                           f32(ffn_conv_w), f32(ffn_w_down), f32(ln2_g), f32(ln2_b), f32(ple_w_proj), f32(ple_w_gate))
    ins = []
    for c in range(NCORE):
        d = dict(shared)
        d["xT"] = _seg(x1T, c)
        d["mixT"] = _seg(oT, c)
        d["pT"] = _seg(pT[1], c)
        d["hscale"] = np.full((128, 1), 0.0 if c == 0 else 1.0, np.float32)
        ins.append(d)
    rd = _run(nc_d, ins)
    log("tok1 done")
    outT = np.concatenate([np.asarray(r["outT"]) for r in rd], axis=1)
    return np.ascontiguousarray(outT.T).reshape(1, SEQ, D).astype(np.float32)
```
